# Optimizing a Trainium2 kernel written in Bass

```python
import functools
import jax, jax.numpy as jnp
from jax import lax
import numpy as np

D_MODEL = 2048
BATCH = 4
SEQ = 2048
DEPTH = 2
DEC_BATCH = 32
DEC_SEQ = 1
PAST_LEN = 16384
PAGE_SIZE = 128

MIX_WIDTH = D_MODEL
HA = 4
DK_A = 128
DV_A = 128
A_WIDTH = HA * DV_A
HB = 8
KV_HEADS = 2
HEAD_DIM = 128
B_WIDTH = HB * HEAD_DIM
WINDOW = 128
HC = 4
DK_C = 128
DV_C = 128
C_WIDTH = HC * DV_C
CHUNK = 64
D_FF = 5632
CONV_W = 3
NORM_EPS = 1e-6

SPLIT_SIZES = (HA * DK_A, HA * DK_A, HA * DV_A, A_WIDTH,
               HB * HEAD_DIM, KV_HEADS * HEAD_DIM, KV_HEADS * HEAD_DIM,
               HC * DK_C, HC * DK_C, HC * DV_C, C_WIDTH)
IN_COLS = sum(SPLIT_SIZES)
SPLIT_POINTS = tuple(int(s) for s in np.cumsum(SPLIT_SIZES)[:-1])

kernel_name = 'hybrid_hgrn2_swa_retention_step'


def rms_norm(x, g):
    x32 = x.astype(jnp.float32)
    y = x32 * lax.rsqrt(jnp.mean(x32 * x32, axis=-1, keepdims=True) + NORM_EPS)
    return (y * g.astype(jnp.float32)).astype(x.dtype)


def head_rms_norm(o, g):
    H, Dv = o.shape[-2:]
    y = o * lax.rsqrt(jnp.mean(o * o, axis=-1, keepdims=True) + NORM_EPS)
    return y * g.astype(jnp.float32).reshape(H, Dv)


def head_group_norm(o, g):
    H, Dv = o.shape[-2:]
    c = o - jnp.mean(o, axis=-1, keepdims=True)
    y = c * lax.rsqrt(jnp.mean(c * c, axis=-1, keepdims=True) + NORM_EPS)
    return y * g.astype(jnp.float32).reshape(H, Dv)


def alibi_slopes():
    return jnp.exp2(-(8.0 / HB) * (jnp.arange(HB, dtype=jnp.float32) + 1.0))


def retention_log_gamma():
    return jnp.log1p(-jnp.exp2(-5.0 - jnp.arange(HC, dtype=jnp.float32)))


def scan_chunks(step, state0, xs, chunk):
    B, T = xs[0].shape[:2]
    nc = T // chunk
    xs_c = tuple(jnp.moveaxis(a.reshape((B, nc, chunk) + a.shape[2:]), 1, 0) for a in xs)
    state, out = lax.scan(step, state0, xs_c)
    out = jnp.moveaxis(out, 0, 1)
    return out.reshape((B, T) + out.shape[3:]), state


def hgrn2_chunk_step(S, inp):
    q, k, v, log_f = inp
    L = q.shape[1]
    b = jnp.cumsum(log_f, axis=1)
    causal = jnp.tril(jnp.ones((L, L), dtype=bool))
    diff = b[:, :, None] - b[:, None, :]
    decay = jnp.exp(jnp.where(causal[None, :, :, None, None], diff, -jnp.inf))
    att = jnp.einsum('bthd,bshd,btshd->bhts', q, k, decay)
    o = (jnp.einsum('bhts,bshv->bthv', att, v)
         + jnp.einsum('bthd,bhdv->bthv', q * jnp.exp(b), S))
    b_last = b[:, -1]
    S_new = (jnp.exp(b_last)[..., None] * S
             + jnp.einsum('bshd,bshv->bhdv', k * jnp.exp(b_last[:, None] - b), v))
    return S_new, o


def retention_chunk_step(S, inp, log_gamma):
    q, k, v = inp
    L = q.shape[1]
    pos = jnp.arange(L, dtype=jnp.float32)
    diff = pos[:, None] - pos[None, :]
    causal = diff >= 0
    dmat = jnp.where(causal[None], jnp.exp(jnp.where(causal, diff, 0.0)[None] * log_gamma[:, None, None]), 0.0)
    att = jnp.einsum('bthd,bshd->bhts', q, k) * dmat[None]
    inner = jnp.exp((pos[:, None] + 1.0) * log_gamma[None, :])
    o = (jnp.einsum('bhts,bshv->bthv', att, v)
         + jnp.einsum('bthd,bhdv->bthv', q, S) * inner[None, :, :, None])
    tail = jnp.exp((L - 1.0 - pos)[:, None] * log_gamma[None, :])
    S_new = (jnp.exp(L * log_gamma)[None, :, None, None] * S
             + jnp.einsum('bshd,bshv->bhdv', k * tail[None, :, :, None], v))
    return S_new, o


def sink_attention(q, k, v, dist, valid, sinks, slopes):
    lead = q.shape[:-3]
    Q, H, Dh = q.shape[-3:]
    kvh = k.shape[-2]
    G = H // kvh
    qg = q.reshape(lead + (Q, kvh, G, Dh))
    s = jnp.einsum('...qkgd,...skd->...kgqs', qg, k) * (Dh ** -0.5)
    s = s - slopes.reshape(kvh, G, 1, 1) * dist
    s = jnp.where(valid, s, -jnp.inf)
    sink = sinks.reshape(kvh, G, 1, 1)
    m = jnp.maximum(jnp.max(s, axis=-1, keepdims=True), sink)
    p = jnp.exp(s - m)
    denom = jnp.sum(p, axis=-1, keepdims=True) + jnp.exp(sink - m)
    o = jnp.einsum('...kgqs,...skd->...qkgd', p / denom, v)
    return o.reshape(lead + (Q, H, Dh))


def swa_prompt(q, k, v, sinks, slopes):
    B, T = q.shape[:2]
    nb = T // WINDOW
    qb = q.reshape(B, nb, WINDOW, HB, HEAD_DIM)
    kb = k.reshape(B, nb, WINDOW, KV_HEADS, HEAD_DIM)
    vb = v.reshape(B, nb, WINDOW, KV_HEADS, HEAD_DIM)
    pad = ((0, 0), (1, 0), (0, 0), (0, 0), (0, 0))
    kk = jnp.concatenate([jnp.pad(kb, pad)[:, :-1], kb], axis=2)
    vv = jnp.concatenate([jnp.pad(vb, pad)[:, :-1], vb], axis=2)
    qi = jnp.arange(WINDOW)[:, None]
    sj = jnp.arange(2 * WINDOW)[None, :]
    dist = WINDOW + qi - sj
    blk = jnp.arange(nb)[:, None, None]
    valid = (dist >= 0) & (dist < WINDOW) & ((sj >= WINDOW) | (blk > 0))
    o = sink_attention(qb, kk, vv, dist.astype(jnp.float32), valid[:, None, None], sinks, slopes)
    return o.reshape(B, T, HB, HEAD_DIM)


def swa_sample(q, k, v, k_cache, v_cache, sinks, slopes):
    T = q.shape[1]
    kk = jnp.concatenate([k_cache, k], axis=1)
    vv = jnp.concatenate([v_cache, v], axis=1)
    t = jnp.arange(T)[:, None]
    j = jnp.arange(WINDOW + T)[None, :]
    dist = WINDOW + t - j
    valid = (dist >= 0) & (dist < WINDOW)
    o = sink_attention(q, kk, vv, dist.astype(jnp.float32), valid, sinks, slopes)
    return o, kk[:, -WINDOW:], vv[:, -WINDOW:]


def conv_ffn(h, w_up, conv_w, conv_b, w_down, prefix):
    u = jnp.matmul(h, w_up)
    T = u.shape[1]
    up = jnp.concatenate([prefix.astype(u.dtype), u], axis=1)
    c = conv_b
    for j in range(CONV_W):
        c = c + conv_w[j] * up[:, j:j + T]
    gate, val = jnp.split(c, 2, axis=-1)
    y = jnp.matmul(jax.nn.gelu(gate) * val, w_down)
    return y, up[:, -(CONV_W - 1):]


def trunk_layer(x, s_hgrn, s_ret, conv_buf, k_cache, v_cache,
                norm1, w_in, lb, a_gain, sinks, c_gain, w_out,
                norm2, w_up, conv_w, conv_b, w_down):
    B, T, _ = x.shape
    f32 = jnp.float32
    chunk = CHUNK if T % CHUNK == 0 else T
    h = rms_norm(x, norm1)
    z = jnp.matmul(h, w_in).astype(f32)
    qa, fa, ia, ga, qb, kb, vb, qc, kc, vc, gc = jnp.split(z, SPLIT_POINTS, axis=-1)

    lb = lb.reshape(HA, DK_A)
    fa = fa.reshape(B, T, HA, DK_A)
    log_f = jnp.logaddexp(jnp.log(lb), jnp.log1p(-lb) + jax.nn.log_sigmoid(fa))
    k_a = (1.0 - lb) * jax.nn.sigmoid(-fa)
    o_a, s_hgrn_new = scan_chunks(
        hgrn2_chunk_step, s_hgrn,
        (qa.reshape(B, T, HA, DK_A), k_a, ia.reshape(B, T, HA, DV_A), log_f), chunk)
    o_a = (head_rms_norm(o_a, a_gain) * jax.nn.silu(ga.reshape(B, T, HA, DV_A))).reshape(B, T, A_WIDTH)

    qb = qb.reshape(B, T, HB, HEAD_DIM)
    kb = kb.reshape(B, T, KV_HEADS, HEAD_DIM)
    vb = vb.reshape(B, T, KV_HEADS, HEAD_DIM)
    slopes = alibi_slopes()
    sinks = sinks.astype(f32)
    if k_cache is None:
        o_b = swa_prompt(qb, kb, vb, sinks, slopes)
        k_win, v_win = kb[:, -WINDOW:], vb[:, -WINDOW:]
    else:
        o_b, k_win, v_win = swa_sample(qb, kb, vb, k_cache.astype(f32), v_cache.astype(f32), sinks, slopes)
    o_b = o_b.reshape(B, T, B_WIDTH)

    step_c = functools.partial(retention_chunk_step, log_gamma=retention_log_gamma())
    o_c, s_ret_new = scan_chunks(
        step_c, s_ret,
        (qc.reshape(B, T, HC, DK_C), kc.reshape(B, T, HC, DK_C) * (DK_C ** -0.5), vc.reshape(B, T, HC, DV_C)), chunk)
    o_c = (head_group_norm(o_c, c_gain) * jax.nn.silu(gc.reshape(B, T, HC, DV_C))).reshape(B, T, C_WIDTH)

    mix = jnp.concatenate([o_a, o_b, o_c], axis=-1).astype(x.dtype)
    x = x + jnp.matmul(mix, w_out)
    y, conv_new = conv_ffn(rms_norm(x, norm2), w_up, conv_w, conv_b, w_down, conv_buf)
    x = x + y
    dt = x.dtype
    return x, (s_hgrn_new.astype(dt), k_win.astype(dt), v_win.astype(dt), s_ret_new.astype(dt), conv_new.astype(dt))


def setup_inputs(seed: int = 0) -> dict:
    key = jax.random.key(seed)
    ks = jax.random.split(key, 24)
    f32 = jnp.float32

    def nrm(k, shape, scale):
        return jax.random.normal(k, shape, f32) * scale

    return {
        'x_prompt': nrm(ks[0], (BATCH, SEQ, D_MODEL), 1.0),
        'x_sample': nrm(ks[1], (DEC_BATCH, DEC_SEQ, D_MODEL), 1.0),
        'state_hgrn': nrm(ks[2], (DEPTH, DEC_BATCH, HA, DK_A, DV_A), 1.0),
        'cache_swa_k': nrm(ks[3], (DEPTH, DEC_BATCH, WINDOW, KV_HEADS, HEAD_DIM), 1.0),
        'cache_swa_v': nrm(ks[4], (DEPTH, DEC_BATCH, WINDOW, KV_HEADS, HEAD_DIM), 1.0),
        'state_ret': nrm(ks[5], (DEPTH, DEC_BATCH, HC, DK_C, DV_C), 1.0),
        'state_conv': nrm(ks[6], (DEPTH, DEC_BATCH, CONV_W - 1, 2 * D_FF), 1.0),
        'norm1_g': 1.0 + nrm(ks[7], (DEPTH, D_MODEL), 0.01),
        'w_in': nrm(ks[8], (DEPTH, D_MODEL, IN_COLS), D_MODEL ** -0.5),
        'hgrn_lb_raw': nrm(ks[9], (DEPTH, HA * DK_A), 1.0),
        'hgrn_norm_g': 1.0 + nrm(ks[10], (DEPTH, A_WIDTH), 0.01),
        'swa_sinks': nrm(ks[11], (DEPTH, HB), 1.0),
        'ret_norm_g': 1.0 + nrm(ks[12], (DEPTH, C_WIDTH), 0.01),
        'w_out': nrm(ks[13], (DEPTH, MIX_WIDTH, D_MODEL), MIX_WIDTH ** -0.5),
        'norm2_g': 1.0 + nrm(ks[14], (DEPTH, D_MODEL), 0.01),
        'w_up': nrm(ks[15], (DEPTH, D_MODEL, 2 * D_FF), D_MODEL ** -0.5),
        'conv_w': nrm(ks[16], (DEPTH, CONV_W, 2 * D_FF), CONV_W ** -0.5),
        'conv_b': nrm(ks[17], (DEPTH, 2 * D_FF), 0.01),
        'w_down': nrm(ks[18], (DEPTH, D_FF, D_MODEL), D_FF ** -0.5),
        'final_norm_g': 1.0 + nrm(ks[19], (D_MODEL,), 0.01),
    }


def reference(x_prompt, x_sample, state_hgrn, cache_swa_k, cache_swa_v, state_ret, state_conv,
              norm1_g, w_in, hgrn_lb_raw, hgrn_norm_g, swa_sinks, ret_norm_g, w_out,
              norm2_g, w_up, conv_w, conv_b, w_down, final_norm_g):
    f32 = jnp.float32
    lb_cum = jnp.cumsum(jax.nn.softmax(hgrn_lb_raw.astype(f32), axis=0), axis=0)
    lb_all = lb_cum - lb_cum[0]
    bp = x_prompt.shape[0]
    xp, xs = x_prompt, x_sample
    p_new, s_new = [], []
    for l in range(DEPTH):
        w = (norm1_g[l], w_in[l], lb_all[l], hgrn_norm_g[l], swa_sinks[l], ret_norm_g[l], w_out[l],
             norm2_g[l], w_up[l], conv_w[l], conv_b[l], w_down[l])
        xp, st_p = trunk_layer(xp,
                               jnp.zeros((bp, HA, DK_A, DV_A), f32),
                               jnp.zeros((bp, HC, DK_C, DV_C), f32),
                               jnp.zeros((bp, CONV_W - 1, 2 * D_FF), xp.dtype),
                               None, None, *w)
        p_new.append(st_p)
        xs, st_s = trunk_layer(xs,
                               state_hgrn[l].astype(f32),
                               state_ret[l].astype(f32),
                               state_conv[l],
                               cache_swa_k[l], cache_swa_v[l], *w)
        s_new.append(st_s)
    y_prompt = rms_norm(xp, final_norm_g)
    y_sample = rms_norm(xs, final_norm_g)
    p_hgrn, p_k, p_v, p_ret, p_conv = (jnp.stack(a) for a in zip(*p_new))
    s_hgrn, s_k, s_v, s_ret, s_conv = (jnp.stack(a) for a in zip(*s_new))
    return (y_prompt, y_sample, p_hgrn, p_k, p_v, p_ret, p_conv, s_hgrn, s_k, s_v, s_ret, s_conv)
```

```python
import numpy as np
from contextlib import ExitStack
import concourse.bass as bass
import concourse.mybir as mybir
from concourse.bass_utils import run_bass_kernel_spmd

F32 = mybir.dt.float32
F32R = mybir.dt.float32r
BF16 = mybir.dt.bfloat16
AF = mybir.ActivationFunctionType
ALU = mybir.AluOpType
AX = mybir.AxisListType

P = 128
D = 2048
NKC = 16
T = 2048
TT = 256
NT = T // TT
NS = 4
DFF = 5632
NJ = DFF // P
INC = 5632
EPS = 1e-6
SCALE = 128.0 ** -0.5
GAMMAS = [float(1.0 - 2.0 ** (-5.0 - h)) for h in range(4)]
SLOPES = [float(2.0 ** (-(h + 1.0))) for h in range(8)]
VW = 1024
V_G1, V_G2, V_GF = 0, 32, 64
V_LB, V_OML, V_LBM1 = 80, 88, 96
V_GA, V_GC = 104, 112
V_CW = 120
V_CB = V_CW + 2 * 3 * 88
V_EPS = V_CB + 2 * 88
V_RAW = V_EPS + 1
V_MLO = V_RAW + 8
V_MHI = V_MLO + 1
C_QA, C_FA, C_IA, C_GA, C_QB, C_KB, C_VB, C_QC, C_KC, C_VC, C_GC = 0, 512, 1024, 1536, 2048, 3072, 3328, 3584, 4096, 4608, 5120


class Sched:
    def __init__(self, nc, sems, dma_sems):
        self.nc = nc
        self.eng = {'pe': nc.tensor, 'act': nc.scalar, 'dve': nc.vector, 'pool': nc.gpsimd, 'sp': nc.sync}
        self.sem = sems
        self.cnt = {e: 0 for e in self.eng}
        self.dma_sems = list(dma_sems)
        self.dma_cnt = {}
        self.dma_tot = {}
        self.key_sem = {}
        self.waited = {e: {} for e in self.eng}
        self.lastw = {}
        self.readers = {}
        self.nops = 0

    def _need(self, e, rec, waits):
        semname, sem, val, src = rec
        if src == e and e == 'pe':
            return
        if src == 'dma':
            val = self.dma_cnt[semname[4:]] if False else self.dma_tot[semname]
        if self.waited[e].get(semname, 0) >= val:
            return
        prev = waits.get(semname)
        if prev is None or prev[1] < val:
            waits[semname] = (sem, val)

    def _deps(self, e, reads, writes):
        waits = {}
        for k in reads:
            w = self.lastw.get(k)
            if w is not None:
                self._need(e, w, waits)
        for k in writes:
            w = self.lastw.get(k)
            if w is not None and not (w[3] == e):
                self._need(e, w, waits)
            for r in self.readers.get(k, ()):
                if r[3] == e:
                    continue
                self._need(e, r, waits)
        for semname, (sem, val) in waits.items():
            self.eng[e].wait_ge(sem, val)
            self.waited[e][semname] = val

    def _record(self, rec, reads, writes):
        for k in reads:
            self.readers.setdefault(k, []).append(rec)
        for k in writes:
            self.lastw[k] = rec
            self.readers[k] = []
        self.nops += 1

    def op(self, e, fn, reads=(), writes=()):
        self._deps(e, reads, writes)
        ins = fn(self.eng[e])
        self.cnt[e] += 1
        ins.then_inc(self.sem[e], 1)
        self._record((e, self.sem[e], self.cnt[e], e), reads, writes)
        return ins

    def dma(self, e, out, in_, reads=(), writes=(), semkey=None):
        self._deps(e, reads, writes)
        if semkey is None:
            semkey = 'misc'
        if semkey not in self.key_sem:
            self.key_sem[semkey] = self.dma_sems.pop()
            self.dma_cnt[semkey] = 0
        sem = self.key_sem[semkey]
        ins = self.eng[e].dma_start(out=out, in_=in_)
        self.dma_cnt[semkey] += 16
        self.dma_tot['dma:%s' % (semkey,)] = self.dma_cnt[semkey]
        ins.then_inc(sem, 16)
        self._record(('dma:%s' % (semkey,), sem, self.dma_cnt[semkey], 'dma'), reads, writes)
        return ins

    def settle(self, semkey, keys):
        sem = self.key_sem[semkey]
        for k in keys:
            self.lastw[k] = ('dma:%s' % (semkey,), sem, self.dma_cnt[semkey], 'dma')

    def finish(self, e='sp'):
        for k, sem in self.key_sem.items():
            self.eng[e].wait_ge(sem, self.dma_cnt[k])
        for n, sem in self.sem.items():
            if self.cnt[n] > 0:
                self.eng[e].wait_ge(sem, self.cnt[n])


def build_program(nt=NT):
    nc = bass.Bass("TRN2", target_bir_lowering=False)
    nc.dge_precook = False

    def din(name, shape, dt=F32):
        return nc.dram_tensor(name, list(shape), dt, kind="ExternalInput").ap()

    def dout(name, shape, dt=F32):
        return nc.dram_tensor(name, list(shape), dt, kind="ExternalOutput").ap()

    xT = din("xT", [NKC, P, T + NS])
    st_h = din("st_h", [2, NS, 4, P, P])
    st_r = din("st_r", [2, NS, 4, P, P])
    ck = din("ck", [2, NS, P, 256])
    cv = din("cv", [2, NS, P, 256])
    scT = din("scT", [2, P, 88, NS, 2])
    w_in = din("w_in", [2, D, INC], F32R)
    w_out = din("w_out", [2, D, D], F32R)
    w_up = din("w_up", [2, D, 2 * DFF], F32R)
    w_down = din("w_down", [2, DFF, D], F32R)
    vecs = din("vecs", [P, VW])
    sinkb = din("sinkb", [P, 16])
    sinks4 = din("sinks4", [4, 2, 2])
    cmask = din("cmask", [P, 256], BF16)
    dmask = din("dmask", [P, 4, 256], BF16)
    rowc = din("rowc", [P, 9, 256])
    biasd = din("biasd", [P, 16, 256], BF16)
    biass = din("biass", [4, 8, 128])
    seld = din("seld", [P, 4, P])
    identb = din("identb", [P, P], BF16)
    identf = din("identf", [P, P])
    onesd = din("onesd", [P, 2, P], F32R)

    yT = dout("yT", [NKC, P, T + NS])
    o_hs = dout("o_hs", [2, 4, P, P])
    o_rs = dout("o_rs", [2, 4, P, P])
    o_pk = dout("o_pk", [2, P, 256])
    o_pv = dout("o_pv", [2, P, 256])
    o_pc = dout("o_pc", [2, P, 88, 2])
    o_shs = dout("o_shs", [2, NS, 4, P, P])
    o_srs = dout("o_srs", [2, NS, 4, P, P])
    o_sk = dout("o_sk", [2, NS, P, 256])
    o_sv = dout("o_sv", [2, NS, P, 256])
    o_sc = dout("o_sc", [2, P, 88, NS, 2])


    with ExitStack() as es:
        def sbuf(name, shape, dt=F32):
            return es.enter_context(nc.sbuf_tensor(name, list(shape), dt))

        def psum(name, shape, dt=F32):
            return es.enter_context(nc.psum_tensor(name, list(shape), dt))

        WT = TT + NS
        x_sb = sbuf("x_sb", [P, NKC, WT])
        hT = sbuf("hT", [P, NKC, WT], F32R)
        mix = sbuf("mix", [P, NKC, WT], F32R)
        m_sb = sbuf("m_sb", [P, NJ // 2, WT], F32R)
        NWS = 2
        wsl = [sbuf("wsl%d" % i, [P, NKC * 256], F32R) for i in range(NWS)]
        vec = sbuf("vec", [P, VW])
        sinkt = sbuf("sinkt", [P, 16])
        sink4 = sbuf("sink4", [4, 2, 2])
        cmk = sbuf("cmk", [P, 256], BF16)
        dmk = sbuf("dmk", [P, 4, 256], BF16)
        rwc = sbuf("rwc", [P, 9, 256])
        bia = sbuf("bia", [P, 16, 256], BF16)
        bis = sbuf("bis", [4, 8, 128])
        sel = sbuf("sel", [P, 4, P])
        idb = sbuf("idb", [P, P], BF16)
        idf = sbuf("idf", [P, P])
        ones = sbuf("ones", [P, 2, P], F32R)
        Sf = {(k, l, h): sbuf("Sf%s%d%d" % (k, l, h), [P, P]) for k in 'ar' for l in range(2) for h in range(4)}
        kwin = [sbuf("kwin%d" % l, [P, 2, 384], BF16) for l in range(2)]
        vwin = [sbuf("vwin%d" % l, [P, 3, 256], BF16) for l in range(2)]
        upre = [sbuf("upre%d" % l, [P, 88, 2]) for l in range(2)]
        sig = sbuf("sig", [P, WT]); lf = sbuf("lf", [P, WT]); ka = sbuf("ka", [P, WT])
        bb = sbuf("bb", [P, TT]); Eb = sbuf("Eb", [P, TT]); Ei = sbuf("Ei", [P, TT]); kh = sbuf("kh", [P, TT])
        qt = sbuf("qt", [P, TT], BF16); qin = sbuf("qin", [P, TT], BF16); khb = sbuf("khb", [P, TT], BF16)
        kbar = sbuf("kbar", [P, TT], BF16); kbT = sbuf("kbT", [P, 2, 2, P], BF16); vt = sbuf("vt", [P, 2, P], BF16)
        PTs = sbuf("PTs", [P, TT], BF16); Sb = sbuf("Sb", [P, 4, P], BF16)
        o_sb = sbuf("o_sb", [P, WT], F32R); cen = sbuf("cen", [P, WT], F32R); sqo = sbuf("sqo", [P, WT], F32R)
        rstd = sbuf("rstd", [P, WT]); gate = sbuf("gate", [P, WT])
        sqx = [sbuf("sqx%d" % i, [P, WT], F32R) for i in range(2)]
        qb = sbuf("qb", [P, 8, TT], BF16)
        kv32 = sbuf("kv32", [P, 512])
        s_sc = [Eb, Ei]
        p_sc = [kh, bb]
        pn_sc = [sbuf("pn_sc%d" % i, [P, 256], BF16) for i in range(2)]
        pT_sc = [sbuf("pT_sc%d" % i, [P, 256], BF16) for i in range(2)]
        sm = [sbuf("sm%d" % i, [P, 8]) for i in range(2)]
        ug = [sbuf("ug%d" % i, [P, TT + 2]) for i in range(2)]
        cg, cvv, tg, sg = sig, lf, ka, gate
        Ss = sbuf("Ss", [P, NS, P]); tmpS = sbuf("tmpS", [P, NS, P])
        vs = sbuf("vs", [P, P]); qs = sbuf("qs", [P, NS]); ks = sbuf("ks", [P, NS]); fs = sbuf("fs", [P, NS])
        Kw = sbuf("Kw", [P, 256]); Vw = sbuf("Vw", [P, 256]); KwT = sbuf("KwT", [P, 2, P])
        kvs = sbuf("kvs", [4, 512]); qsw = sbuf("qsw", [P, NS, 8])
        ssm = sbuf("ssm", [4, 2, P]); psm = sbuf("psm", [P, 2, P]); st4 = sbuf("st4", [4, 6, 8])
        pnT = sbuf("pnT", [P, 32])
        scp = [sbuf("scp%d" % l, [P, 88, NS, 2]) for l in range(2)]
        sco = [sbuf("sco%d" % l, [P, 88, NS, 2]) for l in range(2)]
        pb = [psum("pb%d" % i, [P, 512]) for i in range(8)]
        pbb = [b.bitcast(BF16) for b in pb]

        sems = {n: es.enter_context(nc.semaphore("s_" + n)) for n in ['pe', 'act', 'dve', 'pool']}
        dsems = [es.enter_context(nc.semaphore("d%d" % i)) for i in range(16)]
        es.enter_context(nc.Block())
        S = Sched(nc, sems, dsems)

        def vcol(c):
            return vec[:, c:c + 1]

        bank_rr = [0]

        def pbank(group):
            i = bank_rr[0]
            bank_rr[0] += 1
            return (i % 4) if group == 'proj' else 4 + (i % 4)

        def mm(out, lhsT, rhs, start, stop, reads, writes):
            S.op('pe', lambda e: e.matmul(out, lhsT=lhsT, rhs=rhs, start=start, stop=stop), reads=reads, writes=writes)

        def act(out, in_, func, reads, writes, **kw):
            S.op('act', lambda e: e.activation(out=out, in_=in_, func=func, **kw), reads=reads, writes=writes)

        def tt(eng, out, in0, in1, op, reads, writes):
            S.op(eng, lambda e: e.tensor_tensor(out=out, in0=in0, in1=in1, op=op), reads=reads, writes=writes)

        def ts(out, in0, s1, s2, op0, op1, reads, writes):
            if op1 is None:
                S.op('dve', lambda e: e.tensor_scalar(out=out, in0=in0, scalar1=s1, scalar2=0.0, op0=op0, op1=ALU.add), reads=reads, writes=writes)
            else:
                S.op('dve', lambda e: e.tensor_scalar(out=out, in0=in0, scalar1=s1, scalar2=s2, op0=op0, op1=op1), reads=reads, writes=writes)

        def stt(out, in0, scalar, in1, op0, op1, reads, writes):
            S.op('dve', lambda e: e.scalar_tensor_tensor(out=out, in0=in0, scalar=scalar, in1=in1, op0=op0, op1=op1), reads=reads, writes=writes)

        wq = []
        wstate = {'issued': 0, 'used': 0}

        def plan_weights():
            for t in range(nt):
                for l in range(2):
                    for h in range(4):
                        wq.append((w_in[l], 0, 16, [C_QA + h * 128, C_FA + h * 128]))
                        wq.append((w_in[l], 0, 16, [C_GA + h * 128, C_IA + h * 128]))
                    for hp in range(4):
                        wq.append((w_in[l], 0, 16, [C_QB + hp * 256, C_QB + hp * 256 + 128]))
                    wq.append((w_in[l], 0, 16, [C_KB, C_KB + 128]))
                    wq.append((w_in[l], 0, 16, [C_VB, C_VB + 128]))
                    for h in range(4):
                        wq.append((w_in[l], 0, 16, [C_QC + h * 128, C_KC + h * 128]))
                        wq.append((w_in[l], 0, 16, [C_GC + h * 128, C_VC + h * 128]))
                    for oc in range(8):
                        wq.append((w_out[l], 0, 16, [oc * 256, oc * 256 + 128]))
                    for half in range(2):
                        for j in range(22 * half, 22 * half + 22):
                            wq.append((w_up[l], 0, 16, [j * 128, DFF + j * 128]))
                        for oc in range(16):
                            wq.append((w_down[l], 22 * half, 22, [oc * 128]))

        def issue_w(i):
            mat, k0, nk, cols = wq[i]
            slot = i % NWS
            nc_ = len(cols)
            view = wsl[slot][:, 0:nk * nc_ * 128].rearrange("p (k c) -> p k c", k=nk)
            for ci, c0 in enumerate(cols):
                src = mat[k0 * 128:(k0 + nk) * 128, c0:c0 + 128].rearrange("(k p) c -> p k c", p=128)
                S.dma('sp', view[:, :, ci * 128:(ci + 1) * 128], src, writes=[('w', slot)], semkey=('w', slot))

        def next_w():
            i = wstate['used']
            wstate['used'] += 1
            while wstate['issued'] < min(len(wq), i + NWS):
                issue_w(wstate['issued'])
                wstate['issued'] += 1
            mat, k0, nk, cols = wq[i]
            slot = i % NWS
            view = wsl[slot][:, 0:nk * len(cols) * 128].rearrange("p (k c) -> p k c", k=nk)
            return view, ('w', slot)

        def proj_fm(wv, wkey, ci, nk, k0, actT, akey, bank, W, first=True, last=True):
            bk = ('ps', bank)
            for k in range(nk):
                st = first and k == 0
                sp_ = last and k == nk - 1 and W == TT
                ak = akey(k0 + k) if callable(akey) else akey
                mm(pb[bank][:, 0:TT], wv[:, k, ci * 128:(ci + 1) * 128], actT[:, k0 + k, 0:TT], st, sp_, [wkey, ak], [bk])
                if W > TT:
                    mm(pb[bank][:, TT:W], wv[:, k, ci * 128:(ci + 1) * 128], actT[:, k0 + k, TT:W], False, last and k == nk - 1, [wkey, ak], [bk])

        def proj_tm(wv, wkey, ci, bank, W):
            bk = ('ps', bank)
            for j in range(2):
                for k in range(NKC):
                    mm(pb[bank][:, j * 128:(j + 1) * 128], hT[:, k, j * 128:(j + 1) * 128], wv[:, k, ci * 128:(ci + 1) * 128], k == 0, k == NKC - 1, [wkey, 'hT'], [bk])
            if W > TT:
                for k in range(NKC):
                    mm(pb[bank][0:4, 256:384], hT[:, k, TT:W], wv[:, k, ci * 128:(ci + 1) * 128], k == 0, k == NKC - 1, [wkey, 'hT'], [bk])

        def rmsnorm(gcol0, dst, dkey, W):
            bank = pbank('mx')
            bk = ('ps', bank)
            for c in range(NKC):
                sq = sqx[c % 2]
                act(sq[:, 0:W], x_sb[:, c, 0:W], AF.Square, ['x'], [('sqx', c % 2)])
                mm(pb[bank][:, 0:W], ones[:, 0, :], sq[:, 0:W], c == 0, c == NKC - 1, [('sqx', c % 2), 'ones'], [bk])
            act(rstd[:, 0:W], pb[bank][:, 0:W], AF.Ln, [bk, 'vec'], ['rstd'], bias=vcol(V_EPS), scale=1.0)
            act(rstd[:, 0:W], rstd[:, 0:W], AF.Exp, ['rstd'], ['rstd'], scale=-0.5)
            for c in range(NKC):
                stt(dst[:, c, 0:W], x_sb[:, c, 0:W], vcol(gcol0 + c), rstd[:, 0:W], ALU.mult, ALU.mult, ['x', 'rstd', 'vec'], [dkey(c) if callable(dkey) else dkey])

        import os
        KSTOP = int(os.environ.get('KSTOP', '99'))
        KSUB = int(os.environ.get('KSUB', '99'))

        def lin_head(kind, t, l, h, W):
            first_tile = (t == 0)
            w1, k1 = next_w()
            bq = pbank('proj'); bf = pbank('proj')
            proj_fm(w1, k1, 0, 16, 0, hT, 'hT', bq, W)
            proj_fm(w1, k1, 1, 16, 0, hT, 'hT', bf, W)
            w2, k2 = next_w()
            bg = pbank('proj'); bi = pbank('proj')
            proj_fm(w2, k2, 0, 16, 0, hT, 'hT', bg, W)
            proj_tm(w2, k2, 1, bi, W)
            kq, kf, kg, ki = ('ps', bq), ('ps', bf), ('ps', bg), ('ps', bi)
            lh = l * 4 + h
            act(vt[:, :, :], pb[bi][:, 0:256].rearrange("p (j d) -> p j d", j=2), AF.Copy, [ki], ['vt'])
            if W > TT:
                act(vs[0:4, :], pb[bi][0:4, 256:384], AF.Copy, [ki], ['vs'])
                act(qs[:, :], pb[bq][:, TT:W], AF.Copy, [kq], ['qs'])
            if KSUB < 1:
                return
            act(gate[:, 0:W], pb[bg][:, 0:W], AF.Silu, [kg], ['gate'])
            if KSUB < 2:
                return
            if kind == 'a':
                act(sig[:, 0:W], pb[bf][:, 0:W], AF.Sigmoid, [kf], ['sig'])
                act(lf[:, 0:W], sig[:, 0:W], AF.Ln, ['sig', 'vec'], ['lf'], scale=vcol(V_OML + lh), bias=vcol(V_LB + lh))
                ts(ka[:, 0:W], sig[:, 0:W], -1.0, vcol(V_LBM1 + lh), ALU.add, ALU.mult, ['sig', 'vec'], ['ka'])
                S.op('dve', lambda e: e.tensor_tensor_scan(out=bb[:, :], data0=rwc[:, 0, :], data1=lf[:, 0:TT], initial=0.0, op0=ALU.mult, op1=ALU.add), reads=['lf', 'rwc'], writes=['bb'])
                act(Eb[:, :], bb[:, :], AF.Exp, ['bb'], ['Eb'])
                act(Ei[:, :], bb[:, :], AF.Exp, ['bb'], ['Ei'], scale=-1.0)
                tt('dve', qt[:, :], pb[bq][:, 0:TT], Eb[:, :], ALU.mult, [kq, 'Eb'], ['qt'])
                tt('dve', kh[:, :], ka[:, 0:TT], Ei[:, :], ALU.mult, ['ka', 'Ei'], ['kh'])
                act(khb[:, :], kh[:, :], AF.Copy, ['kh'], ['khb'])
                E3 = Eb[:, :].rearrange("p (c t) -> p c t", t=64)
                tt('dve', kbar[:, :].rearrange("p (c t) -> p c t", t=64), kh[:, :].rearrange("p (c t) -> p c t", t=64),
                   E3[:, :, 63:64].to_broadcast([P, 4, 64]), ALU.mult, ['kh', 'Eb'], ['kbar'])
                qinter = qt
                maskT = cmk[:, :]
                nch = 4
            else:
                act(qt[:, :], pb[bq][:, 0:TT], AF.Copy, [kq], ['qt'])
                tt('dve', qin[:, :], pb[bq][:, 0:TT], rwc[:, 1 + h, :], ALU.mult, [kq, 'rwc'], ['qin'])
                act(khb[:, :], pb[bf][:, 0:TT], AF.Copy, [kf], ['khb'])
                tt('dve', kbar[:, :], pb[bf][:, 0:TT], rwc[:, 5 + h, :], ALU.mult, [kf, 'rwc'], ['kbar'])
                qinter = qin
                maskT = dmk[:, h, :]
                nch = 2
            if KSUB < 3:
                return
            btr = pbank('mx')
            for j in range(2):
                S.op('pe', lambda e, j=j: e.transpose(out=pbb[btr][:, j * 128:(j + 1) * 128], in_=kbar[:, j * 128:(j + 1) * 128], identity=idb[:, :]), reads=['kbar', 'idb'], writes=[('ps', btr)])
            if kind == 'a':
                act(kbT[:, 0, :, :], pbb[btr][:, 0:256].rearrange("p (j d) -> p j d", j=2), AF.Identity, [('ps', btr), 'vec'], ['kbT'], scale=vcol(V_MLO))
                act(kbT[:, 1, :, :], pbb[btr][:, 0:256].rearrange("p (j d) -> p j d", j=2), AF.Identity, [('ps', btr), 'vec'], ['kbT'], scale=vcol(V_MHI))
            else:
                act(kbT[:, 0, :, :], pbb[btr][:, 0:256].rearrange("p (j d) -> p j d", j=2), AF.Copy, [('ps', btr)], ['kbT'])
            if KSUB < 4:
                return
            ba = pbank('mx')
            for j in range(2):
                mm(pb[ba][:, j * 128:(j + 1) * 128], khb[:, j * 128:(j + 1) * 128], qt[:, j * 128:(j + 1) * 128], True, True, ['khb', 'qt'], [('ps', ba)])
            tt('dve', PTs[:, :], pb[ba][:, 0:TT], maskT, ALU.mult, [('ps', ba), 'masks'], ['PTs'])
            if KSUB < 5:
                return
            bd = pbank('mx')
            csz = TT // nch
            for c in range(nch):
                j = (c * csz) // 128
                half = ((c * csz) % 128) // 64 if kind == 'a' else 0
                mm(pb[bd][:, c * 128:(c + 1) * 128], kbT[:, half, j, :], vt[:, j, :], True, True, ['kbT', 'vt'], [('ps', bd)])
            Sfk = ('Sf', kind, l, h)
            Sft = Sf[(kind, l, h)]
            for c in range(nch):
                act(Sb[:, c, :], Sft[:, :], AF.Copy, [Sfk], [('Sb', c)])
                if kind == 'a':
                    stt(Sft[:, :], Sft[:, :], Eb[:, c * 64 + 63:c * 64 + 64], pb[bd][:, c * 128:(c + 1) * 128], ALU.mult, ALU.add, [Sfk, 'Eb', ('ps', bd)], [Sfk])
                else:
                    stt(Sft[:, :], Sft[:, :], float(GAMMAS[h] ** 128), pb[bd][:, c * 128:(c + 1) * 128], ALU.mult, ALU.add, [Sfk, ('ps', bd)], [Sfk])
            if KSUB < 6:
                return
            bo = pbank('mx')
            ko = ('ps', bo)
            for j in range(2):
                mm(pb[bo][:, j * 128:(j + 1) * 128], vt[:, j, :], PTs[:, j * 128:(j + 1) * 128], True, False, ['vt', 'PTs'], [ko])
                cpb = nch // 2
                for cc in range(cpb):
                    c = j * cpb + cc
                    mm(pb[bo][:, c * csz:(c + 1) * csz], Sb[:, c, :], qinter[:, c * csz:(c + 1) * csz], False, cc == cpb - 1, [('Sb', c), 'qt', 'qin'], [ko])
            act(o_sb[:, 0:TT], pb[bo][:, 0:TT], AF.Copy, [ko], ['o_sb'])
            if KSUB < 7:
                return
            if W > TT:
                sample_lin(kind, l, h, bq, bf)
            if KSUB < 8:
                return
            gcol = (V_GA if kind == 'a' else V_GC) + lh
            bn = pbank('mx')
            kn = ('ps', bn)
            if kind == 'a':
                act(sqo[:, 0:W], o_sb[:, 0:W].bitcast(F32), AF.Square, ['o_sb'], ['sqo'])
                src = o_sb
            else:
                mm(pb[bn][:, 0:TT], ones[:, 1, :], o_sb[:, 0:TT], True, W == TT, ['o_sb', 'ones'], [kn])
                if W > TT:
                    mm(pb[bn][:, TT:W], ones[:, 1, :], o_sb[:, TT:W], False, True, ['o_sb', 'ones'], [kn])
                tt('dve', cen[:, 0:W], o_sb[:, 0:W].bitcast(F32), pb[bn][:, 0:W], ALU.subtract, ['o_sb', kn], ['cen'])
                act(sqo[:, 0:W], cen[:, 0:W].bitcast(F32), AF.Square, ['cen'], ['sqo'])
                src = cen
                bn = pbank('mx')
                kn = ('ps', bn)
            mm(pb[bn][:, 0:TT], ones[:, 1, :], sqo[:, 0:TT], True, W == TT, ['sqo', 'ones'], [kn])
            if W > TT:
                mm(pb[bn][:, TT:W], ones[:, 1, :], sqo[:, TT:W], False, True, ['sqo', 'ones'], [kn])
            act(rstd[:, 0:W], pb[bn][:, 0:W], AF.Ln, [kn, 'vec'], ['rstd'], bias=vcol(V_EPS), scale=1.0)
            act(rstd[:, 0:W], rstd[:, 0:W], AF.Exp, ['rstd'], ['rstd'], scale=-0.5)
            mrow = h if kind == 'a' else 12 + h
            stt(cg[:, 0:W], src[:, 0:W].bitcast(F32), vcol(gcol), rstd[:, 0:W], ALU.mult, ALU.mult, ['o_sb', 'cen', 'rstd', 'vec'], ['sig'])
            tt('dve', mix[:, mrow, 0:W], cg[:, 0:W], gate[:, 0:W], ALU.mult, ['sig', 'gate'], [('mix', mrow)])

        def sample_lin(kind, l, h, bq, bf):
            src_st = st_h if kind == 'a' else st_r
            dst_st = o_shs if kind == 'a' else o_srs
            S.dma('pool', Ss[:, :, :], src_st[l, :, h, :, :].rearrange("b k v -> k b v"), writes=['Ss'], semkey='ss')
            if kind == 'a':
                act(fs[:, :], lf[:, TT:TT + NS], AF.Exp, ['lf'], ['fs'])
                act(ks[:, :], ka[:, TT:TT + NS], AF.Copy, ['ka'], ['ks'])
            else:
                act(ks[:, :], pb[bf][:, TT:TT + NS], AF.Identity, [('ps', bf)], ['ks'], scale=SCALE)
            bv = pbank('mx')
            for b in range(NS):
                mm(pb[bv][:, b * 128:(b + 1) * 128], sel[:, b, :], vs[:, :], True, True, ['sel', 'vs'], [('ps', bv)])
            tt('dve', tmpS[:, :, :], pb[bv][:, :].rearrange("p (b v) -> p b v", b=NS), ks[:, :].unsqueeze(2).to_broadcast([P, NS, P]), ALU.mult, [('ps', bv), 'ks'], ['tmpS'])
            if kind == 'a':
                tt('dve', Ss[:, :, :], Ss[:, :, :], fs[:, :].unsqueeze(2).to_broadcast([P, NS, P]), ALU.mult, ['Ss', 'fs'], ['Ss'])
                tt('dve', Ss[:, :, :], Ss[:, :, :], tmpS[:, :, :], ALU.add, ['Ss', 'tmpS'], ['Ss'])
            else:
                stt(Ss[:, :, :].rearrange("p b v -> p (b v)"), Ss[:, :, :].rearrange("p b v -> p (b v)"), float(GAMMAS[h]), tmpS[:, :, :].rearrange("p b v -> p (b v)"), ALU.mult, ALU.add, ['Ss', 'tmpS'], ['Ss'])
            S.dma('pool', dst_st[l, :, h, :, :].rearrange("b k v -> k b v"), Ss[:, :, :], reads=['Ss'], semkey='ssout')
            bo2 = pbank('mx')
            for b in range(NS):
                mm(pb[bo2][:, b * 4:b * 4 + 4], Ss[:, b, :], qs[:, 0:4], True, True, ['Ss', 'qs'], [('ps', bo2)])
            act(o_sb[:, TT:TT + NS], pb[bo2][:, 0:16:5], AF.Copy, [('ps', bo2)], ['o_sb'])

        def swa(t, l, W):
            for hp in range(4):
                w, wk = next_w()
                for ci in range(2):
                    hh = hp * 2 + ci
                    bq = pbank('proj')
                    proj_fm(w, wk, ci, 16, 0, hT, 'hT', bq, W)
                    act(qb[:, hh, :], pb[bq][:, 0:TT], AF.Copy, [('ps', bq)], [('qb', hh)])
                    if W > TT:
                        act(qsw[:, :, hh], pb[bq][:, TT:W], AF.Copy, [('ps', bq)], ['qsw'])
            w, wk = next_w()
            for kv in range(2):
                bk_ = pbank('proj')
                proj_fm(w, wk, kv, 16, 0, hT, 'hT', bk_, TT)
                act(kwin[l][:, kv, 128:384], pb[bk_][:, 0:TT], AF.Copy, [('ps', bk_)], [('kwin', l)])
            bkt = [pbank('proj'), pbank('proj')]
            for kv in range(2):
                proj_tm(w, wk, kv, bkt[kv], W)
            w2, wk2 = next_w()
            bvt = [pbank('proj'), pbank('proj')]
            for kv in range(2):
                proj_tm(w2, wk2, kv, bvt[kv], W)
            for kv in range(2):
                act(vwin[l][:, 1:3, kv * 128:(kv + 1) * 128], pb[bvt[kv]][:, 0:256].rearrange("p (j d) -> p j d", j=2), AF.Copy, [('ps', bvt[kv])], [('vwin', l)])
                act(kv32[:, kv * 128:(kv + 1) * 128], pb[bkt[kv]][:, 128:256], AF.Copy, [('ps', bkt[kv])], ['kv32'])
                act(kv32[:, 256 + kv * 128:256 + (kv + 1) * 128], pb[bvt[kv]][:, 128:256], AF.Copy, [('ps', bvt[kv])], ['kv32'])
                if W > TT:
                    act(kvs[:, kv * 128:(kv + 1) * 128], pb[bkt[kv]][0:4, 256:384], AF.Copy, [('ps', bkt[kv])], ['kvs'])
                    act(kvs[:, 256 + kv * 128:256 + (kv + 1) * 128], pb[bvt[kv]][0:4, 256:384], AF.Copy, [('ps', bvt[kv])], ['kvs'])
            if t == nt - 1:
                S.dma('pool', o_pk[l], kv32[:, 0:256], reads=['kv32'], semkey='kvout')
                S.dma('pool', o_pv[l], kv32[:, 256:512], reads=['kv32'], semkey='kvout')
            it = 0
            for j in range(2):
                for hh in range(8):
                    kv = hh // 4
                    r = it % 2
                    it += 1
                    bs = pbank('mx')
                    mm(pb[bs][:, 0:256], qb[:, hh, j * 128:(j + 1) * 128], kwin[l][:, kv, j * 128:j * 128 + 256], True, True, [('qb', hh), ('kwin', l)], [('ps', bs)])
                    brow = hh + (8 if (t == 0 and j == 0) else 0)
                    stt(s_sc[r][:, :], pb[bs][:, 0:256], SCALE, bia[:, brow, :], ALU.mult, ALU.add, [('ps', bs), 'bia'], [['Eb', 'Ei'][r]])
                    S.op('dve', lambda e, r=r: e.tensor_reduce(out=sm[r][:, 0:1], in_=s_sc[r][:, :], axis=AX.X, op=ALU.max), reads=[['Eb', 'Ei'][r]], writes=[('sm', r)])
                    ts(sm[r][:, 1:2], sm[r][:, 0:1], sinkt[:, l * 8 + hh:l * 8 + hh + 1], -1.0, ALU.max, ALU.mult, [('sm', r), 'sinkt'], [('sm', r)])
                    act(p_sc[r][:, :], s_sc[r][:, :], AF.Exp, [['Eb', 'Ei'][r], ('sm', r)], [['kh', 'bb'][r]], bias=sm[r][:, 1:2], scale=1.0)
                    S.op('dve', lambda e, r=r: e.tensor_reduce(out=sm[r][:, 2:3], in_=p_sc[r][:, :], axis=AX.X, op=ALU.add), reads=[['kh', 'bb'][r]], writes=[('sm', r)])
                    act(sm[r][:, 3:4], sinkt[:, l * 8 + hh:l * 8 + hh + 1], AF.Exp, ['sinkt', ('sm', r)], [('sm', r)], bias=sm[r][:, 1:2], scale=1.0)
                    tt('dve', sm[r][:, 4:5], sm[r][:, 2:3], sm[r][:, 3:4], ALU.add, [('sm', r)], [('sm', r)])
                    S.op('dve', lambda e, r=r: e.reciprocal(out=sm[r][:, 5:6], in_=sm[r][:, 4:5]), reads=[('sm', r)], writes=[('sm', r)])
                    ts(pn_sc[r][:, :], p_sc[r][:, :], sm[r][:, 5:6], None, ALU.mult, None, [['kh', 'bb'][r], ('sm', r)], [('pn_sc', r)])
                    bt_ = pbank('mx')
                    for half in range(2):
                        S.op('pe', lambda e, half=half, r=r, bt_=bt_: e.transpose(out=pbb[bt_][:, half * 128:(half + 1) * 128], in_=pn_sc[r][:, half * 128:(half + 1) * 128], identity=idb[:, :]), reads=[('pn_sc', r), 'idb'], writes=[('ps', bt_)])
                    act(pT_sc[r][:, :], pbb[bt_][:, 0:256], AF.Copy, [('ps', bt_)], [('pT_sc', r)])
                    bo = pbank('mx')
                    mm(pb[bo][:, 0:128], vwin[l][:, j, kv * 128:(kv + 1) * 128], pT_sc[r][:, 0:128], True, False, [('vwin', l), ('pT_sc', r)], [('ps', bo)])
                    mm(pb[bo][:, 0:128], vwin[l][:, j + 1, kv * 128:(kv + 1) * 128], pT_sc[r][:, 128:256], False, True, [('vwin', l), ('pT_sc', r)], [('ps', bo)])
                    act(mix[:, 4 + hh, j * 128:(j + 1) * 128], pb[bo][:, 0:128], AF.Copy, [('ps', bo)], [('mix', 4 + hh)])
            if W > TT:
                sample_swa(l)
            act(kwin[l][:, :, 0:128], kwin[l][:, :, 256:384], AF.Copy, [('kwin', l)], [('kwin', l)])
            act(vwin[l][:, 0, :], vwin[l][:, 2, :], AF.Copy, [('vwin', l)], [('vwin', l)])

        def sample_swa(l):
            for b in range(NS):
                S.dma('pool', Kw[0:127, :], ck[l, b, 1:128, :], writes=['Kw'], semkey='kvw')
                S.dma('pool', Vw[0:127, :], cv[l, b, 1:128, :], writes=['Vw'], semkey='kvw')
                S.dma('pool', Kw[127:128, :], kvs[b:b + 1, 0:256], reads=['kvs'], writes=['Kw'], semkey='kvw')
                S.dma('pool', Vw[127:128, :], kvs[b:b + 1, 256:512], reads=['kvs'], writes=['Vw'], semkey='kvw')
                S.settle('kvw', ['Kw', 'Vw'])
                S.dma('pool', o_sk[l, b], Kw[:, :], reads=['Kw'], semkey='kvwout')
                S.dma('pool', o_sv[l, b], Vw[:, :], reads=['Vw'], semkey='kvwout')
                bt_ = pbank('mx')
                for kv in range(2):
                    S.op('pe', lambda e, kv=kv, bt_=bt_: e.transpose(out=pb[bt_][:, kv * 128:(kv + 1) * 128], in_=Kw[:, kv * 128:(kv + 1) * 128], identity=idf[:, :]), reads=['Kw', 'idf'], writes=[('ps', bt_)])
                act(KwT[:, :, :], pb[bt_][:, 0:256].rearrange("p (i j) -> p i j", i=2), AF.Copy, [('ps', bt_)], ['KwT'])
                bs2 = pbank('mx')
                for kv in range(2):
                    mm(pb[bs2][0:4, kv * 128:(kv + 1) * 128], qsw[:, b, kv * 4:kv * 4 + 4], KwT[:, kv, :], True, True, ['qsw', 'KwT'], [('ps', bs2)])
                stt(ssm[:, :, :].rearrange("g i j -> g (i j)"), pb[bs2][0:4, 0:256], SCALE, bis[:, b * 2:b * 2 + 2, :].rearrange("g i j -> g (i j)"), ALU.mult, ALU.add, [('ps', bs2), 'bis'], ['ssm'])
                sk4 = sink4[:, l, :]
                S.op('dve', lambda e: e.tensor_reduce(out=st4[:, 0, 0:2], in_=ssm[:, :, :], axis=AX.X, op=ALU.max), reads=['ssm'], writes=['st4'])
                tt('dve', st4[:, 1, 0:2], st4[:, 0, 0:2], sk4, ALU.max, ['st4', 'sink4'], ['st4'])
                tt('dve', ssm[:, :, :], ssm[:, :, :], st4[:, 1, 0:2].unsqueeze(2).to_broadcast([4, 2, P]), ALU.subtract, ['ssm', 'st4'], ['ssm'])
                act(psm[0:4, :, :], ssm[:, :, :], AF.Exp, ['ssm'], ['psm'])
                S.op('dve', lambda e: e.tensor_reduce(out=st4[:, 2, 0:2], in_=psm[0:4, :, :], axis=AX.X, op=ALU.add), reads=['psm'], writes=['st4'])
                tt('dve', st4[:, 3, 0:2], sk4, st4[:, 1, 0:2], ALU.subtract, ['st4', 'sink4'], ['st4'])
                act(st4[:, 3, 0:2], st4[:, 3, 0:2], AF.Exp, ['st4'], ['st4'])
                tt('dve', st4[:, 4, 0:2], st4[:, 2, 0:2], st4[:, 3, 0:2], ALU.add, ['st4'], ['st4'])
                S.op('dve', lambda e: e.reciprocal(out=st4[:, 5, 0:2], in_=st4[:, 4, 0:2]), reads=['st4'], writes=['st4'])
                tt('dve', psm[0:4, :, :], psm[0:4, :, :], st4[:, 5, 0:2].unsqueeze(2).to_broadcast([4, 2, P]), ALU.mult, ['psm', 'st4'], ['psm'])
                bt2 = pbank('mx')
                for kv in range(2):
                    S.op('pe', lambda e, kv=kv, bt2=bt2: e.transpose(out=pb[bt2][:, kv * 128:(kv + 1) * 128], in_=psm[:, kv, :], identity=idf[:, :]), reads=['psm', 'idf'], writes=[('ps', bt2)])
                act(pnT[:, 0:8].rearrange("p (k g) -> p k g", k=2), pb[bt2][:, 0:256].rearrange("p (k j) -> p k j", k=2)[:, :, 0:4], AF.Copy, [('ps', bt2)], ['pnT'])
                bo = pbank('mx')
                for kv in range(2):
                    mm(pb[bo][:, kv * 4:kv * 4 + 4], Vw[:, kv * 128:(kv + 1) * 128], pnT[:, kv * 4:kv * 4 + 4], True, True, ['Vw', 'pnT'], [('ps', bo)])
                act(mix[:, 4:12, TT + b], pb[bo][:, 0:8], AF.Copy, [('ps', bo)], [('mix', 4 + i) for i in range(8)])

        def conv_chunk(l, uc, bank, W, dst, r, t, dk):
            bk = ('ps', bank)
            u = ug[r]
            uk = ('ug', r)
            act(u[:, 2:2 + TT], pb[bank][:, 0:TT], AF.Copy, [bk], [uk])
            S.op('pool', lambda e: e.tensor_copy(out=u[:, 0:2], in_=upre[l][:, uc, :]), reads=[('upre', l, uc)], writes=[uk])
            cwc = V_CW + (l * 3) * 88 + uc
            act(dst[:, 0:W], pb[bank][:, 0:W], AF.Identity, [bk, 'vec'], [dk], scale=vcol(cwc + 2 * 88), bias=vcol(V_CB + l * 88 + uc))
            stt(dst[:, 0:TT], u[:, 1:1 + TT], vcol(cwc + 88), dst[:, 0:TT], ALU.mult, ALU.add, [uk, dk, 'vec'], [dk])
            stt(dst[:, 0:TT], u[:, 0:TT], vcol(cwc), dst[:, 0:TT], ALU.mult, ALU.add, [uk, dk, 'vec'], [dk])
            S.op('pool', lambda e: e.tensor_copy(out=upre[l][:, uc, :], in_=u[:, TT:TT + 2]), reads=[uk], writes=[('upre', l, uc)])
            if W > TT:
                stt(dst[:, TT:W], scp[l][:, uc, :, 1], vcol(cwc + 88), dst[:, TT:W], ALU.mult, ALU.add, ['scp', dk, 'vec'], [dk])
                stt(dst[:, TT:W], scp[l][:, uc, :, 0], vcol(cwc), dst[:, TT:W], ALU.mult, ALU.add, ['scp', dk, 'vec'], [dk])
                S.op('pool', lambda e: e.tensor_copy(out=sco[l][:, uc, :, 0], in_=scp[l][:, uc, :, 1]), reads=['scp'], writes=[('sco', l)])
                act(sco[l][:, uc, :, 1], pb[bank][:, TT:W], AF.Copy, [bk], [('sco', l)])

        def down_half(half, l, W):
            for oc in range(16):
                by = pbank('proj')
                wa, wka = next_w()
                proj_fm(wa, wka, 0, 22, 0, m_sb, (lambda k: ('m', k)), by, W)
                tt('dve', x_sb[:, oc, 0:W], x_sb[:, oc, 0:W], pb[by][:, 0:W], ALU.add, ['x', ('ps', by)], ['x'])

        def ffn(t, l, W):
            rmsnorm(V_G2 + l * 16, hT, 'hT', W)
            for j in range(NJ):
                w, wk = next_w()
                bg = pbank('proj'); bv = pbank('proj')
                proj_fm(w, wk, 0, 16, 0, hT, 'hT', bg, W)
                proj_fm(w, wk, 1, 16, 0, hT, 'hT', bv, W)
                conv_chunk(l, j, bg, W, cg, 0, t, 'sig')
                conv_chunk(l, NJ + j, bv, W, cvv, 1, t, 'lf')
                act(tg[:, 0:W], cg[:, 0:W], AF.Square, ['sig'], ['ka'])
                ts(tg[:, 0:W], tg[:, 0:W], 0.044715, 1.0, ALU.mult, ALU.add, ['ka'], ['ka'])
                tt('dve', tg[:, 0:W], tg[:, 0:W], cg[:, 0:W], ALU.mult, ['ka', 'sig'], ['ka'])
                act(sg[:, 0:W], tg[:, 0:W], AF.Sigmoid, ['ka'], ['gate'], scale=1.5957691216057308)
                tt('dve', sg[:, 0:W], sg[:, 0:W], cg[:, 0:W], ALU.mult, ['gate', 'sig'], ['gate'])
                tt('dve', m_sb[:, j % 22, 0:W], sg[:, 0:W], cvv[:, 0:W], ALU.mult, ['gate', 'lf'], [('m', j % 22)])
                if j % 22 == 21:
                    down_half(j // 22, l, W)
            if t == nt - 1:
                S.dma('pool', o_pc[l], upre[l][:, :, :], reads=[('upre', l, uc) for uc in range(88)], semkey='out')
            if W > TT:
                S.dma('pool', o_sc[l], sco[l][:, :, :, :], reads=[('sco', l)], semkey='out')

        def layer(t, l, W):
            if KSTOP < 1:
                return
            rmsnorm(V_G1 + l * 16, hT, 'hT', W)
            if KSTOP < 2:
                return
            for h in range(4):
                lin_head('a', t, l, h, W)
                if KSTOP < 3:
                    return
            if KSTOP < 4:
                return
            swa(t, l, W)
            if KSTOP < 5:
                return
            for h in range(4):
                lin_head('r', t, l, h, W)
            if KSTOP < 6:
                return
            mixkeys = [('mix', i) for i in range(16)]
            for ocp in range(8):
                w, wk = next_w()
                for ci in range(2):
                    oc = ocp * 2 + ci
                    by = pbank('proj')
                    proj_fm(w, wk, ci, 16, 0, mix, (lambda k: ('mix', k)), by, W)
                    tt('dve', x_sb[:, oc, 0:W], x_sb[:, oc, 0:W], pb[by][:, 0:W], ALU.add, ['x', ('ps', by)], ['x'])
            if KSTOP < 7:
                return
            ffn(t, l, W)

        plan_weights()
        for dst, src, key in [(vec, vecs, 'vec'), (sinkt, sinkb, 'sinkt'), (sink4, sinks4, 'sink4'), (cmk, cmask, 'masks'), (dmk, dmask, 'masks'),
                              (rwc, rowc, 'rwc'), (bia, biasd, 'bia'), (bis, biass, 'bis'), (sel, seld, 'sel'), (idb, identb, 'idb'),
                              (idf, identf, 'idf'), (ones, onesd, 'ones')]:
            S.dma('pool', dst[:], src, writes=[key], semkey='const')
        for l in range(2):
            S.dma('pool', scp[l][:, :, :, :], scT[l], writes=['scp'], semkey='const')
        S.settle('const', ['vec', 'sinkt', 'sink4', 'masks', 'rwc', 'bia', 'bis', 'sel', 'idb', 'idf', 'ones', 'scp'])
        tt('dve', vec[:, V_LB + 4:V_LB + 8], vec[:, V_RAW + 4:V_RAW + 8], vec[:, V_RAW:V_RAW + 4], ALU.subtract, ['vec'], ['vec'])
        act(vec[:, V_LB + 4:V_LB + 8], vec[:, V_LB + 4:V_LB + 8], AF.Sigmoid, ['vec'], ['vec'])
        ts(vec[:, V_OML:V_OML + 8], vec[:, V_LB:V_LB + 8], -1.0, 1.0, ALU.mult, ALU.add, ['vec'], ['vec'])
        ts(vec[:, V_LBM1:V_LBM1 + 8], vec[:, V_LB:V_LB + 8], -1.0, 0.0, ALU.add, ALU.add, ['vec'], ['vec'])
        for l in range(2):
            if l == 0:
                S.op('pool', lambda e: e.memset(vs[:, :], 0.0), writes=['vs'])
            S.op('pool', lambda e, l=l: e.memset(upre[l][:, :, :], 0.0), writes=[('upre', l, uc) for uc in range(88)])
            S.op('pool', lambda e, l=l: e.memset(kwin[l][:, :, :], 0.0), writes=[('kwin', l)])
            S.op('pool', lambda e, l=l: e.memset(vwin[l][:, :, :], 0.0), writes=[('vwin', l)])
            for k in 'ar':
                for h in range(4):
                    S.op('pool', lambda e, k=k, l=l, h=h: e.memset(Sf[(k, l, h)][:, :], 0.0), writes=[('Sf', k, l, h)])

        for t in range(nt):
            W = TT + NS if t == 0 else TT
            S.dma('sp', x_sb[:, :, 0:TT], xT[:, :, t * TT:(t + 1) * TT].rearrange("c p n -> p c n"), writes=['x'], reads=[], semkey='xload')
            if t == 0:
                S.dma('sp', x_sb[:, :, TT:W], xT[:, :, T:T + NS].rearrange("c p n -> p c n"), writes=['x'], semkey='xload')
            for l in range(2):
                layer(t, l, W)
            rmsnorm(V_GF, x_sb, 'x', W)
            S.dma('pool', yT[:, :, t * TT:(t + 1) * TT].rearrange("c p n -> p c n"), x_sb[:, :, 0:TT], reads=['x'], semkey='yout')
            if t == 0:
                S.dma('pool', yT[:, :, T:T + NS].rearrange("c p n -> p c n"), x_sb[:, :, TT:W], reads=['x'], semkey='yout')
        for l in range(2):
            for h in range(4):
                S.dma('pool', o_hs[l, h], Sf[('a', l, h)][:, :], reads=[('Sf', 'a', l, h)], semkey='out')
                S.dma('pool', o_rs[l, h], Sf[('r', l, h)][:, :], reads=[('Sf', 'r', l, h)], semkey='out')
        assert KSTOP < 99 or wstate['used'] == len(wq), (wstate, len(wq))
        S.finish('sp')
        print("ops", S.nops, {k: v for k, v in S.cnt.items()})
    return nc


_CACHE = {}


def _consts():
    import ml_dtypes
    bf = ml_dtypes.bfloat16
    c = {}
    s = np.arange(128)[:, None]
    tcol = np.arange(256)[None, :]
    tl = tcol % 128
    c['cmask'] = (((s // 64) == (tl // 64)) & (s <= tl)).astype(np.float32).astype(bf)
    dm = np.zeros((128, 4, 256), np.float32)
    rowc = np.zeros((128, 9, 256), np.float32)
    rowc[:, 0, :] = 1.0
    rowc[:, 0, ::64] = 0.0
    for h in range(4):
        g = np.float64(GAMMAS[h])
        dm[:, h, :] = np.where(tl >= s, g ** np.maximum(tl - s, 0) * SCALE, 0.0)
        rowc[:, 1 + h, :] = g ** (tl + 1.0)
        rowc[:, 5 + h, :] = g ** (127.0 - tl) * SCALE
    c['dmask'] = dm.astype(bf)
    c['rowc'] = rowc
    qi = np.arange(128)[:, None]
    sj = np.arange(256)[None, :]
    dist = 128 + qi - sj
    valid = (dist >= 0) & (dist < 128)
    bd = np.zeros((128, 16, 256), np.float32)
    for h in range(8):
        bd[:, h, :] = np.where(valid, -SLOPES[h] * dist, -1e30)
        bd[:, 8 + h, :] = np.where(valid & (sj >= 128), -SLOPES[h] * dist, -1e30)
    c['biasd'] = bd.astype(bf)
    bs = np.zeros((4, 8, 128), np.float32)
    for g in range(4):
        for pair in range(8):
            kv = pair % 2
            bs[g, pair, :] = -SLOPES[kv * 4 + g] * (127.0 - np.arange(128))
    c['biass'] = bs
    sd = np.zeros((128, 4, 128), np.float32)
    for b in range(4):
        sd[b, b, :] = 1.0
    c['seld'] = sd
    c['identb'] = np.eye(128, dtype=np.float32).astype(bf)
    c['identf'] = np.eye(128, dtype=np.float32)
    on = np.zeros((128, 2, 128), np.float32)
    on[:, 0, :] = 1.0 / 2048.0
    on[:, 1, :] = 1.0 / 128.0
    c['onesd'] = on
    return c


def kernel(x_prompt, x_sample, state_hgrn, cache_swa_k, cache_swa_v, state_ret, state_conv,
           norm1_g, w_in, hgrn_lb_raw, hgrn_norm_g, swa_sinks, ret_norm_g, w_out,
           norm2_g, w_up, conv_w, conv_b, w_down, final_norm_g):
    f = lambda a: np.ascontiguousarray(np.asarray(a, dtype=np.float32))
    x_prompt, x_sample = f(x_prompt), f(x_sample)
    state_hgrn, state_ret = f(state_hgrn), f(state_ret)
    cache_swa_k, cache_swa_v, state_conv = f(cache_swa_k), f(cache_swa_v), f(state_conv)
    w_in, w_out, w_up, w_down = f(w_in), f(w_out), f(w_up), f(w_down)
    if 'nc' not in _CACHE:
        _CACHE['nc'] = build_program()
        _CACHE['c'] = _consts()
    nc = _CACHE['nc']
    cst = _CACHE['c']

    raise_if = None
    in_maps = []
    n_cores = 8
    vecs = np.zeros((128, VW), np.float32)
    n1 = f(norm1_g).reshape(2, 16, 128); n2 = f(norm2_g).reshape(2, 16, 128); nf = f(final_norm_g).reshape(16, 128)
    for l in range(2):
        vecs[:, 0 + l * 16:0 + (l + 1) * 16] = n1[l].T
        vecs[:, 32 + l * 16:32 + (l + 1) * 16] = n2[l].T
    vecs[:, 64:80] = nf.T
    ga = f(hgrn_norm_g).reshape(2, 4, 128); gc = f(ret_norm_g).reshape(2, 4, 128)
    for l in range(2):
        vecs[:, 104 + l * 4:104 + (l + 1) * 4] = ga[l].T
        vecs[:, 112 + l * 4:112 + (l + 1) * 4] = gc[l].T
    cw = f(conv_w).reshape(2, 3, 88, 128); cb = f(conv_b).reshape(2, 88, 128)
    for l in range(2):
        for tap in range(3):
            vecs[:, V_CW + (l * 3 + tap) * 88:V_CW + (l * 3 + tap + 1) * 88] = cw[l, tap].T
        vecs[:, V_CB + l * 88:V_CB + (l + 1) * 88] = cb[l].T
    vecs[:, V_EPS] = EPS
    vecs[0:64, V_MLO] = 1.0
    vecs[64:128, V_MHI] = 1.0
    lbraw = f(hgrn_lb_raw).reshape(2, 4, 128)
    vecs[:, V_RAW:V_RAW + 4] = lbraw[0].T
    vecs[:, V_RAW + 4:V_RAW + 8] = lbraw[1].T
    sk = f(swa_sinks)
    sinkb = np.broadcast_to(sk.reshape(1, 16), (128, 16)).copy()
    sinks4 = np.ascontiguousarray(sk.reshape(2, 2, 4).transpose(2, 0, 1))
    for c in range(n_cores):
        b = c % 4
        xT = np.empty((16, 128, T + NS), np.float32)
        xT[:, :, :T] = x_prompt[b].T.reshape(16, 128, T)
        xT[:, :, T:] = x_sample[c * 4:(c + 1) * 4, 0, :].T.reshape(16, 128, NS)
        m = {
            'xT': xT,
            'st_h': np.ascontiguousarray(state_hgrn[:, c * 4:(c + 1) * 4]),
            'st_r': np.ascontiguousarray(state_ret[:, c * 4:(c + 1) * 4]),
            'ck': np.ascontiguousarray(cache_swa_k[:, c * 4:(c + 1) * 4].reshape(2, 4, 128, 256)),
            'cv': np.ascontiguousarray(cache_swa_v[:, c * 4:(c + 1) * 4].reshape(2, 4, 128, 256)),
            'scT': np.ascontiguousarray(state_conv[:, c * 4:(c + 1) * 4].reshape(2, 4, 2, 88, 128).transpose(0, 4, 3, 1, 2)),
            'w_in': w_in, 'w_out': w_out, 'w_up': w_up, 'w_down': w_down,
            'vecs': vecs, 'sinkb': sinkb, 'sinks4': sinks4,
        }
        m.update(cst)
        in_maps.append(m)
    if _CACHE.get('maps_only'):
        return in_maps
    res = run_bass_kernel_spmd(nc, in_maps, core_ids=list(range(n_cores)))
    R = res.results
    y_prompt = np.stack([R[b]['yT'][:, :, :T].reshape(D, T).T for b in range(4)])
    y_sample = np.concatenate([R[c]['yT'][:, :, T:].reshape(D, NS).T for c in range(8)])[:, None, :]
    p_hgrn = np.stack([R[b]['o_hs'] for b in range(4)], axis=1)
    p_ret = np.stack([R[b]['o_rs'] for b in range(4)], axis=1)
    p_k = np.stack([R[b]['o_pk'].reshape(2, 128, 2, 128) for b in range(4)], axis=1)
    p_v = np.stack([R[b]['o_pv'].reshape(2, 128, 2, 128) for b in range(4)], axis=1)
    p_conv = np.stack([R[b]['o_pc'].transpose(0, 3, 2, 1).reshape(2, 2, 11264) for b in range(4)], axis=1)
    s_hgrn = np.concatenate([R[c]['o_shs'] for c in range(8)], axis=1)
    s_ret = np.concatenate([R[c]['o_srs'] for c in range(8)], axis=1)
    s_k = np.concatenate([R[c]['o_sk'].reshape(2, 4, 128, 2, 128) for c in range(8)], axis=1)
    s_v = np.concatenate([R[c]['o_sv'].reshape(2, 4, 128, 2, 128) for c in range(8)], axis=1)
    s_conv = np.concatenate([R[c]['o_sc'].transpose(0, 3, 4, 2, 1).reshape(2, 4, 2, 11264) for c in range(8)], axis=1)
    out = (y_prompt, y_sample, p_hgrn, p_k, p_v, p_ret, p_conv, s_hgrn, s_k, s_v, s_ret, s_conv)
    return tuple(np.ascontiguousarray(o, dtype=np.float32) for o in out)
```

```python
import numpy as np
from contextlib import ExitStack
import concourse.bass as bass
import concourse.mybir as mybir
from concourse.bass_utils import run_bass_kernel_spmd

F32 = mybir.dt.float32
F32R = mybir.dt.float32r
BF16 = mybir.dt.bfloat16
AF = mybir.ActivationFunctionType
ALU = mybir.AluOpType
AX = mybir.AxisListType

P = 128
D = 2048
NKC = 16
T = 2048
TT = 256
NT = T // TT
NS = 4
DFF = 5632
NJ = DFF // P
MGRP = [(0, 6), (6, 12), (12, 18), (18, 22)]
INC = 5632
EPS = 1e-6
SCALE = 128.0 ** -0.5
GAMMAS = [float(1.0 - 2.0 ** (-5.0 - h)) for h in range(4)]
SLOPES = [float(2.0 ** (-(h + 1.0))) for h in range(8)]
VW = 1024
V_G1, V_G2, V_GF = 0, 32, 64
V_LB, V_OML, V_LBM1 = 80, 88, 96
V_GA, V_GC = 104, 112
V_CW = 120
V_CB = V_CW + 2 * 3 * 88
V_EPS = V_CB + 2 * 88
V_RAW = V_EPS + 1
V_MLO = V_RAW + 8
V_MHI = V_MLO + 1
C_QA, C_FA, C_IA, C_GA, C_QB, C_KB, C_VB, C_QC, C_KC, C_VC, C_GC = 0, 512, 1024, 1536, 2048, 3072, 3328, 3584, 4096, 4608, 5120


class Sched:
    def __init__(self, nc, sems, dma_sems):
        self.nc = nc
        self.eng = {'pe': nc.tensor, 'act': nc.scalar, 'dve': nc.vector, 'pool': nc.gpsimd, 'sp': nc.sync}
        self.sem = sems
        self.cnt = {e: 0 for e in self.eng}
        self.dma_sems = list(dma_sems)
        self.dma_cnt = {}
        self.dma_tot = {}
        self.key_sem = {}
        self.waited = {e: {} for e in self.eng}
        self.lastw = {}
        self.readers = {}
        self.nops = 0

    def _need(self, e, rec, waits):
        semname, sem, val, src = rec
        if src == e and e == 'pe':
            return
        if src == 'dma':
            val = self.dma_cnt[semname[4:]] if False else self.dma_tot[semname]
        if self.waited[e].get(semname, 0) >= val:
            return
        prev = waits.get(semname)
        if prev is None or prev[1] < val:
            waits[semname] = (sem, val)

    def _deps(self, e, reads, writes):
        waits = {}
        for k in reads:
            w = self.lastw.get(k)
            if w is not None:
                self._need(e, w, waits)
        for k in writes:
            w = self.lastw.get(k)
            if w is not None and not (w[3] == e):
                self._need(e, w, waits)
            for r in self.readers.get(k, ()):
                if r[3] == e:
                    continue
                self._need(e, r, waits)
        for semname, (sem, val) in waits.items():
            self.eng[e].wait_ge(sem, val)
            self.waited[e][semname] = val

    def _record(self, rec, reads, writes):
        for k in reads:
            self.readers.setdefault(k, []).append(rec)
        for k in writes:
            self.lastw[k] = rec
            self.readers[k] = []
        self.nops += 1

    def op(self, e, fn, reads=(), writes=()):
        self._deps(e, reads, writes)
        ins = fn(self.eng[e])
        self.cnt[e] += 1
        ins.then_inc(self.sem[e], 1)
        self._record((e, self.sem[e], self.cnt[e], e), reads, writes)
        return ins

    def dma(self, e, out, in_, reads=(), writes=(), semkey=None):
        self._deps(e, reads, writes)
        if semkey is None:
            semkey = 'misc'
        if semkey not in self.key_sem:
            self.key_sem[semkey] = self.dma_sems.pop()
            self.dma_cnt[semkey] = 0
        sem = self.key_sem[semkey]
        ins = self.eng[e].dma_start(out=out, in_=in_)
        self.dma_cnt[semkey] += 16
        self.dma_tot['dma:%s' % (semkey,)] = self.dma_cnt[semkey]
        ins.then_inc(sem, 16)
        self._record(('dma:%s' % (semkey,), sem, self.dma_cnt[semkey], 'dma'), reads, writes)
        return ins

    def settle(self, semkey, keys):
        sem = self.key_sem[semkey]
        for k in keys:
            self.lastw[k] = ('dma:%s' % (semkey,), sem, self.dma_cnt[semkey], 'dma')

    def finish(self, e='sp'):
        for k, sem in self.key_sem.items():
            self.eng[e].wait_ge(sem, self.dma_cnt[k])
        for n, sem in self.sem.items():
            if self.cnt[n] > 0:
                self.eng[e].wait_ge(sem, self.cnt[n])


def build_program(nt=NT):
    nc = bass.Bass("TRN2", target_bir_lowering=False)
    nc.dge_precook = False

    def din(name, shape, dt=F32):
        return nc.dram_tensor(name, list(shape), dt, kind="ExternalInput").ap()

    def dout(name, shape, dt=F32):
        return nc.dram_tensor(name, list(shape), dt, kind="ExternalOutput").ap()

    xT = din("xT", [NKC, P, T + NS])
    st_h = din("st_h", [2, NS, 4, P, P])
    st_r = din("st_r", [2, NS, 4, P, P])
    ck = din("ck", [2, NS, P, 256])
    cv = din("cv", [2, NS, P, 256])
    scT = din("scT", [2, P, 88, NS, 2])
    w_in = din("w_in", [2, D, INC], F32R)
    w_out = din("w_out", [2, D, D], F32R)
    w_up = din("w_up", [2, D, 2 * DFF], F32R)
    w_down = din("w_down", [2, DFF, D], F32R)
    vecs = din("vecs", [P, VW])
    sinkb = din("sinkb", [P, 16])
    sinks4 = din("sinks4", [4, 2, 2])
    cmask = din("cmask", [P, 256], BF16)
    dmask = din("dmask", [P, 4, 256], BF16)
    rowc = din("rowc", [P, 9, 256])
    biasd = din("biasd", [P, 16, 256], BF16)
    biass = din("biass", [4, 2, 128])
    seld = din("seld", [P, 4, P])
    identb = din("identb", [P, P], BF16)
    identf = din("identf", [P, P])
    onesd = din("onesd", [P, 2, P], F32R)

    yT = dout("yT", [NKC, P, T + NS])
    o_hs = dout("o_hs", [2, 4, P, P])
    o_rs = dout("o_rs", [2, 4, P, P])
    o_pk = dout("o_pk", [2, P, 256])
    o_pv = dout("o_pv", [2, P, 256])
    o_pc = dout("o_pc", [2, P, 88, 2])
    o_shs = dout("o_shs", [2, NS, 4, P, P])
    o_srs = dout("o_srs", [2, NS, 4, P, P])
    o_sk = dout("o_sk", [2, NS, P, 256])
    o_sv = dout("o_sv", [2, NS, P, 256])
    o_sc = dout("o_sc", [2, P, 88, NS, 2])


    with ExitStack() as es:
        def sbuf(name, shape, dt=F32):
            return es.enter_context(nc.sbuf_tensor(name, list(shape), dt))

        def psum(name, shape, dt=F32):
            return es.enter_context(nc.psum_tensor(name, list(shape), dt))

        WT = TT + NS
        x_sb = sbuf("x_sb", [P, NKC, WT])
        hT = sbuf("hT", [P, NKC, WT], F32R)
        mix = sbuf("mix", [P, NKC, WT], F32R)
        m_sb = sbuf("m_sb", [P, 12, WT], F32R)
        NWS = 3
        wsl = [sbuf("wsl%d" % i, [P, NKC * 256], F32R) for i in range(NWS)]
        vec = sbuf("vec", [P, VW])
        sinkt = sbuf("sinkt", [P, 16])
        sink4 = sbuf("sink4", [4, 2, 2])
        cmk = sbuf("cmk", [P, 256], BF16)
        dmk = sbuf("dmk", [P, 4, 256], BF16)
        rwc = sbuf("rwc", [P, 9, 256])
        bia = sbuf("bia", [P, 16, 256], BF16)
        bis = sbuf("bis", [4, 2, 128])
        sel = sbuf("sel", [P, 4, P])
        idb = sbuf("idb", [P, P], BF16)
        idf = sbuf("idf", [P, P])
        ones = sbuf("ones", [P, 2, P], F32R)
        Sf = {(k, l, h): sbuf("Sf%s%d%d" % (k, l, h), [P, P]) for k in 'ar' for l in range(2) for h in range(4)}
        kwin = [sbuf("kwin%d" % l, [P, 2, 384], BF16) for l in range(2)]
        vwin = [sbuf("vwin%d" % l, [P, 3, 256], BF16) for l in range(2)]
        upre = [sbuf("upre%d" % l, [P, 88, 2]) for l in range(2)]
        sig = sbuf("sig", [P, WT]); lf = sbuf("lf", [P, WT]); ka = sbuf("ka", [P, WT])
        bb = sbuf("bb", [P, TT]); Eb = sbuf("Eb", [P, TT]); Ei = sbuf("Ei", [P, TT]); kh = sbuf("kh", [P, TT])
        qt = sbuf("qt", [P, TT], BF16); qin = sbuf("qin", [P, TT], BF16); khb = sbuf("khb", [P, TT], BF16)
        kbar = sbuf("kbar", [P, TT], BF16); kbT = sbuf("kbT", [P, 2, 2, P], BF16); vt = sbuf("vt", [P, 2, P], BF16)
        PTs = sbuf("PTs", [P, TT], BF16); Sb = sbuf("Sb", [P, 4, P], BF16)
        o_sb = sbuf("o_sb", [P, WT], F32R); cen = sbuf("cen", [P, WT], F32R); sqo = sbuf("sqo", [P, WT], F32R)
        rstd = sbuf("rstd", [P, WT]); gate = sbuf("gate", [P, WT])
        sqx = [sbuf("sqx%d" % i, [P, WT], F32R) for i in range(2)]
        qb = sbuf("qb", [P, 8, TT], BF16)
        kv32 = sbuf("kv32", [P, 512])
        s_sc = [Eb, Ei]
        p_sc = [kh, bb]
        pn_sc = [sbuf("pn_sc%d" % i, [P, 256], BF16) for i in range(2)]
        pT_sc = [sbuf("pT_sc%d" % i, [P, 256], BF16) for i in range(2)]
        sm = [sbuf("sm%d" % i, [P, 8]) for i in range(2)]
        ug = [sbuf("ug%d" % i, [P, TT + 2]) for i in range(2)]
        cg, cvv, tg, sg = sig, lf, ka, gate
        Ss = sbuf("Ss", [P, NS, P]); tmpS = sbuf("tmpS", [P, NS, P])
        vs = sbuf("vs", [P, P]); qs = sbuf("qs", [P, NS]); ks = sbuf("ks", [P, NS]); fs = sbuf("fs", [P, NS])
        Kw = sbuf("Kw", [P, 256]); Vw = sbuf("Vw", [P, 256]); KwT = sbuf("KwT", [P, 2, P])
        kvs = sbuf("kvs", [4, 512]); qsw = sbuf("qsw", [P, NS, 8])
        ssm = sbuf("ssm", [4, 2, P]); psm = sbuf("psm", [P, 2, P]); st4 = sbuf("st4", [4, 6, 8])
        pnT = sbuf("pnT", [P, 32])
        scp = [sbuf("scp%d" % l, [P, 88, NS, 2]) for l in range(2)]
        sco = [sbuf("sco%d" % l, [P, 88, NS, 2]) for l in range(2)]
        pb = [psum("pb%d" % i, [P, 512]) for i in range(8)]
        pbb = [b.bitcast(BF16) for b in pb]

        sems = {n: es.enter_context(nc.semaphore("s_" + n)) for n in ['pe', 'act', 'dve', 'pool']}
        dsems = [es.enter_context(nc.semaphore("d%d" % i)) for i in range(16)]
        es.enter_context(nc.Block())
        S = Sched(nc, sems, dsems)

        def vcol(c):
            return vec[:, c:c + 1]

        bank_rr = [0]

        def pbank(group):
            i = bank_rr[0]
            bank_rr[0] += 1
            return (i % 4) if group == 'proj' else 4 + (i % 4)

        def mm(out, lhsT, rhs, start, stop, reads, writes):
            S.op('pe', lambda e: e.matmul(out, lhsT=lhsT, rhs=rhs, start=start, stop=stop), reads=reads, writes=writes)

        def act(out, in_, func, reads, writes, **kw):
            S.op('act', lambda e: e.activation(out=out, in_=in_, func=func, **kw), reads=reads, writes=writes)

        def tt(eng, out, in0, in1, op, reads, writes):
            S.op(eng, lambda e: e.tensor_tensor(out=out, in0=in0, in1=in1, op=op), reads=reads, writes=writes)

        def ts(out, in0, s1, s2, op0, op1, reads, writes):
            if op1 is None:
                S.op('dve', lambda e: e.tensor_scalar(out=out, in0=in0, scalar1=s1, scalar2=0.0, op0=op0, op1=ALU.add), reads=reads, writes=writes)
            else:
                S.op('dve', lambda e: e.tensor_scalar(out=out, in0=in0, scalar1=s1, scalar2=s2, op0=op0, op1=op1), reads=reads, writes=writes)

        def stt(out, in0, scalar, in1, op0, op1, reads, writes):
            S.op('dve', lambda e: e.scalar_tensor_tensor(out=out, in0=in0, scalar=scalar, in1=in1, op0=op0, op1=op1), reads=reads, writes=writes)

        wq = []
        wstate = {'issued': 0, 'used': 0}

        def plan_weights():
            for t in range(nt):
                for l in range(2):
                    for h in range(4):
                        wq.append((w_in[l], 0, 16, [(C_QA + h * 128, 128), (C_FA + h * 128, 128)]))
                        wq.append((w_in[l], 0, 16, [(C_GA + h * 128, 128), (C_IA + h * 128, 128)]))
                    for hp in range(4):
                        wq.append((w_in[l], 0, 16, [(C_QB + hp * 256, 256)]))
                    wq.append((w_in[l], 0, 16, [(C_KB, 256)]))
                    wq.append((w_in[l], 0, 16, [(C_VB, 256)]))
                    for h in range(4):
                        wq.append((w_in[l], 0, 16, [(C_QC + h * 128, 128), (C_KC + h * 128, 128)]))
                        wq.append((w_in[l], 0, 16, [(C_GC + h * 128, 128), (C_VC + h * 128, 128)]))
                    for oc in range(8):
                        wq.append((w_out[l], 0, 16, [(oc * 256, 256)]))
                    for (p0, p1) in MGRP:
                        for jp in range(p0, p1):
                            wq.append((w_up[l], 0, 16, [(jp * 256, 256)]))
                            wq.append((w_up[l], 0, 16, [(DFF + jp * 256, 256)]))
                        for ocp in range(8):
                            wq.append((w_down[l], 2 * p0, 2 * (p1 - p0), [(ocp * 256, 256)]))

        def issue_w(i):
            mat, k0, nk, cols = wq[i]
            slot = i % NWS
            tw = sum(w for _, w in cols)
            view = wsl[slot][:, 0:nk * tw].rearrange("p (k c) -> p k c", k=nk)
            off = 0
            for c0, w in cols:
                src = mat[k0 * 128:(k0 + nk) * 128, c0:c0 + w].rearrange("(k p) c -> p k c", p=128)
                S.dma('sp', view[:, :, off:off + w], src, writes=[('w', slot)], semkey=('w', slot))
                off += w

        def next_w():
            i = wstate['used']
            wstate['used'] += 1
            while wstate['issued'] < min(len(wq), i + NWS):
                issue_w(wstate['issued'])
                wstate['issued'] += 1
            mat, k0, nk, cols = wq[i]
            slot = i % NWS
            tw = sum(w for _, w in cols)
            view = wsl[slot][:, 0:nk * tw].rearrange("p (k c) -> p k c", k=nk)
            return view, ('w', slot)

        def proj_fm(wv, wkey, ci, nk, k0, actT, akey, bank, W, first=True, last=True):
            bk = ('ps', bank)
            for k in range(nk):
                st = first and k == 0
                sp_ = last and k == nk - 1 and W == TT
                ak = akey(k0 + k) if callable(akey) else akey
                mm(pb[bank][:, 0:TT], wv[:, k, ci * 128:(ci + 1) * 128], actT[:, k0 + k, 0:TT], st, sp_, [wkey, ak], [bk])
                if W > TT:
                    mm(pb[bank][:, TT:W], wv[:, k, ci * 128:(ci + 1) * 128], actT[:, k0 + k, TT:W], False, last and k == nk - 1, [wkey, ak], [bk])

        def proj_tm(wv, wkey, ci, bank, W):
            bk = ('ps', bank)
            for j in range(2):
                for k in range(NKC):
                    mm(pb[bank][:, j * 128:(j + 1) * 128], hT[:, k, j * 128:(j + 1) * 128], wv[:, k, ci * 128:(ci + 1) * 128], k == 0, k == NKC - 1, [wkey, 'hT'], [bk])
            if W > TT:
                for k in range(NKC):
                    mm(pb[bank][0:4, 256:384], hT[:, k, TT:W], wv[:, k, ci * 128:(ci + 1) * 128], k == 0, k == NKC - 1, [wkey, 'hT'], [bk])

        def rmsnorm(gcol0, dst, dkey, W):
            bank = pbank('mx')
            bk = ('ps', bank)
            for c in range(NKC):
                sq = sqx[c % 2]
                act(sq[:, 0:W], x_sb[:, c, 0:W], AF.Square, ['x'], [('sqx', c % 2)])
                mm(pb[bank][:, 0:W], ones[:, 0, :], sq[:, 0:W], c == 0, c == NKC - 1, [('sqx', c % 2), 'ones'], [bk])
            act(rstd[:, 0:W], pb[bank][:, 0:W], AF.Ln, [bk, 'vec'], ['rstd'], bias=vcol(V_EPS), scale=1.0)
            act(rstd[:, 0:W], rstd[:, 0:W], AF.Exp, ['rstd'], ['rstd'], scale=-0.5)
            for c in range(NKC):
                stt(dst[:, c, 0:W], x_sb[:, c, 0:W], vcol(gcol0 + c), rstd[:, 0:W], ALU.mult, ALU.mult, ['x', 'rstd', 'vec'], [dkey(c) if callable(dkey) else dkey])

        import os
        KSTOP = int(os.environ.get('KSTOP', '99'))
        KSUB = int(os.environ.get('KSUB', '99'))

        def lin_head(kind, t, l, h, W):
            first_tile = (t == 0)
            w1, k1 = next_w()
            bq = pbank('proj'); bf = pbank('proj')
            proj_fm(w1, k1, 0, 16, 0, hT, 'hT', bq, W)
            proj_fm(w1, k1, 1, 16, 0, hT, 'hT', bf, W)
            w2, k2 = next_w()
            bg = pbank('proj'); bi = pbank('proj')
            proj_fm(w2, k2, 0, 16, 0, hT, 'hT', bg, W)
            proj_tm(w2, k2, 1, bi, W)
            kq, kf, kg, ki = ('ps', bq), ('ps', bf), ('ps', bg), ('ps', bi)
            lh = l * 4 + h
            act(vt[:, :, :], pb[bi][:, 0:256].rearrange("p (j d) -> p j d", j=2), AF.Copy, [ki], ['vt'])
            if W > TT:
                act(vs[0:4, :], pb[bi][0:4, 256:384], AF.Copy, [ki], ['vs'])
                act(qs[:, :], pb[bq][:, TT:W], AF.Copy, [kq], ['qs'])
            if KSUB < 1:
                return
            act(gate[:, 0:W], pb[bg][:, 0:W], AF.Silu, [kg], ['gate'])
            if KSUB < 2:
                return
            if kind == 'a':
                act(sig[:, 0:W], pb[bf][:, 0:W], AF.Sigmoid, [kf], ['sig'])
                act(lf[:, 0:W], sig[:, 0:W], AF.Ln, ['sig', 'vec'], ['lf'], scale=vcol(V_OML + lh), bias=vcol(V_LB + lh))
                ts(ka[:, 0:W], sig[:, 0:W], -1.0, vcol(V_LBM1 + lh), ALU.add, ALU.mult, ['sig', 'vec'], ['ka'])
                S.op('dve', lambda e: e.tensor_tensor_scan(out=bb[:, :], data0=rwc[:, 0, :], data1=lf[:, 0:TT], initial=0.0, op0=ALU.mult, op1=ALU.add), reads=['lf', 'rwc'], writes=['bb'])
                act(Eb[:, :], bb[:, :], AF.Exp, ['bb'], ['Eb'])
                act(Ei[:, :], bb[:, :], AF.Exp, ['bb'], ['Ei'], scale=-1.0)
                tt('dve', qt[:, :], pb[bq][:, 0:TT], Eb[:, :], ALU.mult, [kq, 'Eb'], ['qt'])
                tt('dve', kh[:, :], ka[:, 0:TT], Ei[:, :], ALU.mult, ['ka', 'Ei'], ['kh'])
                act(khb[:, :], kh[:, :], AF.Copy, ['kh'], ['khb'])
                E3 = Eb[:, :].rearrange("p (c t) -> p c t", t=64)
                tt('dve', kbar[:, :].rearrange("p (c t) -> p c t", t=64), kh[:, :].rearrange("p (c t) -> p c t", t=64),
                   E3[:, :, 63:64].to_broadcast([P, 4, 64]), ALU.mult, ['kh', 'Eb'], ['kbar'])
                qinter = qt
                maskT = cmk[:, :]
                nch = 4
            else:
                act(qt[:, :], pb[bq][:, 0:TT], AF.Copy, [kq], ['qt'])
                tt('dve', qin[:, :], pb[bq][:, 0:TT], rwc[:, 1 + h, :], ALU.mult, [kq, 'rwc'], ['qin'])
                act(khb[:, :], pb[bf][:, 0:TT], AF.Copy, [kf], ['khb'])
                tt('dve', kbar[:, :], pb[bf][:, 0:TT], rwc[:, 5 + h, :], ALU.mult, [kf, 'rwc'], ['kbar'])
                qinter = qin
                maskT = dmk[:, h, :]
                nch = 2
            if KSUB < 3:
                return
            btr = pbank('mx')
            for j in range(2):
                S.op('pe', lambda e, j=j: e.transpose(out=pbb[btr][:, j * 128:(j + 1) * 128], in_=kbar[:, j * 128:(j + 1) * 128], identity=idb[:, :]), reads=['kbar', 'idb'], writes=[('ps', btr)])
            if kind == 'a':
                act(kbT[:, 0, :, :], pbb[btr][:, 0:256].rearrange("p (j d) -> p j d", j=2), AF.Identity, [('ps', btr), 'vec'], ['kbT'], scale=vcol(V_MLO))
                act(kbT[:, 1, :, :], pbb[btr][:, 0:256].rearrange("p (j d) -> p j d", j=2), AF.Identity, [('ps', btr), 'vec'], ['kbT'], scale=vcol(V_MHI))
            else:
                act(kbT[:, 0, :, :], pbb[btr][:, 0:256].rearrange("p (j d) -> p j d", j=2), AF.Copy, [('ps', btr)], ['kbT'])
            if KSUB < 4:
                return
            ba = pbank('mx')
            for j in range(2):
                mm(pb[ba][:, j * 128:(j + 1) * 128], khb[:, j * 128:(j + 1) * 128], qt[:, j * 128:(j + 1) * 128], True, True, ['khb', 'qt'], [('ps', ba)])
            tt('dve', PTs[:, :], pb[ba][:, 0:TT], maskT, ALU.mult, [('ps', ba), 'masks'], ['PTs'])
            if KSUB < 5:
                return
            bd = pbank('mx')
            csz = TT // nch
            for c in range(nch):
                j = (c * csz) // 128
                half = ((c * csz) % 128) // 64 if kind == 'a' else 0
                mm(pb[bd][:, c * 128:(c + 1) * 128], kbT[:, half, j, :], vt[:, j, :], True, True, ['kbT', 'vt'], [('ps', bd)])
            Sfk = ('Sf', kind, l, h)
            Sft = Sf[(kind, l, h)]
            for c in range(nch):
                act(Sb[:, c, :], Sft[:, :], AF.Copy, [Sfk], [('Sb', c)])
                if kind == 'a':
                    stt(Sft[:, :], Sft[:, :], Eb[:, c * 64 + 63:c * 64 + 64], pb[bd][:, c * 128:(c + 1) * 128], ALU.mult, ALU.add, [Sfk, 'Eb', ('ps', bd)], [Sfk])
                else:
                    stt(Sft[:, :], Sft[:, :], float(GAMMAS[h] ** 128), pb[bd][:, c * 128:(c + 1) * 128], ALU.mult, ALU.add, [Sfk, ('ps', bd)], [Sfk])
            if KSUB < 6:
                return
            bo = pbank('mx')
            ko = ('ps', bo)
            for j in range(2):
                mm(pb[bo][:, j * 128:(j + 1) * 128], vt[:, j, :], PTs[:, j * 128:(j + 1) * 128], True, False, ['vt', 'PTs'], [ko])
                cpb = nch // 2
                for cc in range(cpb):
                    c = j * cpb + cc
                    mm(pb[bo][:, c * csz:(c + 1) * csz], Sb[:, c, :], qinter[:, c * csz:(c + 1) * csz], False, cc == cpb - 1, [('Sb', c), 'qt', 'qin'], [ko])
            act(o_sb[:, 0:TT], pb[bo][:, 0:TT], AF.Copy, [ko], ['o_sb'])
            if KSUB < 7:
                return
            if W > TT:
                sample_lin(kind, l, h, bq, bf)
            if KSUB < 8:
                return
            gcol = (V_GA if kind == 'a' else V_GC) + lh
            bn = pbank('mx')
            kn = ('ps', bn)
            if kind == 'a':
                act(sqo[:, 0:W], o_sb[:, 0:W].bitcast(F32), AF.Square, ['o_sb'], ['sqo'])
                src = o_sb
            else:
                mm(pb[bn][:, 0:TT], ones[:, 1, :], o_sb[:, 0:TT], True, W == TT, ['o_sb', 'ones'], [kn])
                if W > TT:
                    mm(pb[bn][:, TT:W], ones[:, 1, :], o_sb[:, TT:W], False, True, ['o_sb', 'ones'], [kn])
                tt('dve', cen[:, 0:W], o_sb[:, 0:W].bitcast(F32), pb[bn][:, 0:W], ALU.subtract, ['o_sb', kn], ['cen'])
                act(sqo[:, 0:W], cen[:, 0:W].bitcast(F32), AF.Square, ['cen'], ['sqo'])
                src = cen
                bn = pbank('mx')
                kn = ('ps', bn)
            mm(pb[bn][:, 0:TT], ones[:, 1, :], sqo[:, 0:TT], True, W == TT, ['sqo', 'ones'], [kn])
            if W > TT:
                mm(pb[bn][:, TT:W], ones[:, 1, :], sqo[:, TT:W], False, True, ['sqo', 'ones'], [kn])
            act(rstd[:, 0:W], pb[bn][:, 0:W], AF.Ln, [kn, 'vec'], ['rstd'], bias=vcol(V_EPS), scale=1.0)
            act(rstd[:, 0:W], rstd[:, 0:W], AF.Exp, ['rstd'], ['rstd'], scale=-0.5)
            mrow = h if kind == 'a' else 12 + h
            stt(cg[:, 0:W], src[:, 0:W].bitcast(F32), vcol(gcol), rstd[:, 0:W], ALU.mult, ALU.mult, ['o_sb', 'cen', 'rstd', 'vec'], ['sig'])
            tt('dve', mix[:, mrow, 0:W], cg[:, 0:W], gate[:, 0:W], ALU.mult, ['sig', 'gate'], [('mix', mrow)])

        def sample_lin(kind, l, h, bq, bf):
            src_st = st_h if kind == 'a' else st_r
            dst_st = o_shs if kind == 'a' else o_srs
            S.dma('pool', Ss[:, :, :], src_st[l, :, h, :, :].rearrange("b k v -> k b v"), writes=['Ss'], semkey='ss')
            if kind == 'a':
                act(fs[:, :], lf[:, TT:TT + NS], AF.Exp, ['lf'], ['fs'])
                act(ks[:, :], ka[:, TT:TT + NS], AF.Copy, ['ka'], ['ks'])
            else:
                act(ks[:, :], pb[bf][:, TT:TT + NS], AF.Identity, [('ps', bf)], ['ks'], scale=SCALE)
            bv = pbank('mx')
            for b in range(NS):
                mm(pb[bv][:, b * 128:(b + 1) * 128], sel[:, b, :], vs[:, :], True, True, ['sel', 'vs'], [('ps', bv)])
            tt('dve', tmpS[:, :, :], pb[bv][:, :].rearrange("p (b v) -> p b v", b=NS), ks[:, :].unsqueeze(2).to_broadcast([P, NS, P]), ALU.mult, [('ps', bv), 'ks'], ['tmpS'])
            if kind == 'a':
                tt('dve', Ss[:, :, :], Ss[:, :, :], fs[:, :].unsqueeze(2).to_broadcast([P, NS, P]), ALU.mult, ['Ss', 'fs'], ['Ss'])
                tt('dve', Ss[:, :, :], Ss[:, :, :], tmpS[:, :, :], ALU.add, ['Ss', 'tmpS'], ['Ss'])
            else:
                stt(Ss[:, :, :].rearrange("p b v -> p (b v)"), Ss[:, :, :].rearrange("p b v -> p (b v)"), float(GAMMAS[h]), tmpS[:, :, :].rearrange("p b v -> p (b v)"), ALU.mult, ALU.add, ['Ss', 'tmpS'], ['Ss'])
            S.dma('pool', dst_st[l, :, h, :, :].rearrange("b k v -> k b v"), Ss[:, :, :], reads=['Ss'], semkey='ssout')
            bo2 = pbank('mx')
            for b in range(NS):
                mm(pb[bo2][:, b * 4:b * 4 + 4], Ss[:, b, :], qs[:, 0:4], True, True, ['Ss', 'qs'], [('ps', bo2)])
            act(o_sb[:, TT:TT + NS], pb[bo2][:, 0:16:5], AF.Copy, [('ps', bo2)], ['o_sb'])

        def swa(t, l, W):
            for hp in range(4):
                w, wk = next_w()
                for ci in range(2):
                    hh = hp * 2 + ci
                    bq = pbank('proj')
                    proj_fm(w, wk, ci, 16, 0, hT, 'hT', bq, W)
                    act(qb[:, hh, :], pb[bq][:, 0:TT], AF.Copy, [('ps', bq)], [('qb', hh)])
                    if W > TT:
                        act(qsw[:, :, hh], pb[bq][:, TT:W], AF.Copy, [('ps', bq)], ['qsw'])
            w, wk = next_w()
            for kv in range(2):
                bk_ = pbank('proj')
                proj_fm(w, wk, kv, 16, 0, hT, 'hT', bk_, TT)
                act(kwin[l][:, kv, 128:384], pb[bk_][:, 0:TT], AF.Copy, [('ps', bk_)], [('kwin', l)])
            bkt = [pbank('proj'), pbank('proj')]
            for kv in range(2):
                proj_tm(w, wk, kv, bkt[kv], W)
            w2, wk2 = next_w()
            bvt = [pbank('proj'), pbank('proj')]
            for kv in range(2):
                proj_tm(w2, wk2, kv, bvt[kv], W)
            for kv in range(2):
                act(vwin[l][:, 1:3, kv * 128:(kv + 1) * 128], pb[bvt[kv]][:, 0:256].rearrange("p (j d) -> p j d", j=2), AF.Copy, [('ps', bvt[kv])], [('vwin', l)])
                act(kv32[:, kv * 128:(kv + 1) * 128], pb[bkt[kv]][:, 128:256], AF.Copy, [('ps', bkt[kv])], ['kv32'])
                act(kv32[:, 256 + kv * 128:256 + (kv + 1) * 128], pb[bvt[kv]][:, 128:256], AF.Copy, [('ps', bvt[kv])], ['kv32'])
                if W > TT:
                    act(kvs[:, kv * 128:(kv + 1) * 128], pb[bkt[kv]][0:4, 256:384], AF.Copy, [('ps', bkt[kv])], ['kvs'])
                    act(kvs[:, 256 + kv * 128:256 + (kv + 1) * 128], pb[bvt[kv]][0:4, 256:384], AF.Copy, [('ps', bvt[kv])], ['kvs'])
            if t == nt - 1:
                S.dma('pool', o_pk[l], kv32[:, 0:256], reads=['kv32'], semkey='kvout')
                S.dma('pool', o_pv[l], kv32[:, 256:512], reads=['kv32'], semkey='kvout')
            it = 0
            for j in range(2):
                for hh in range(8):
                    kv = hh // 4
                    r = it % 2
                    it += 1
                    bs = pbank('mx')
                    mm(pb[bs][:, 0:256], qb[:, hh, j * 128:(j + 1) * 128], kwin[l][:, kv, j * 128:j * 128 + 256], True, True, [('qb', hh), ('kwin', l)], [('ps', bs)])
                    brow = hh + (8 if (t == 0 and j == 0) else 0)
                    stt(s_sc[r][:, :], pb[bs][:, 0:256], SCALE, bia[:, brow, :], ALU.mult, ALU.add, [('ps', bs), 'bia'], [['Eb', 'Ei'][r]])
                    S.op('dve', lambda e, r=r: e.tensor_reduce(out=sm[r][:, 0:1], in_=s_sc[r][:, :], axis=AX.X, op=ALU.max), reads=[['Eb', 'Ei'][r]], writes=[('sm', r)])
                    ts(sm[r][:, 1:2], sm[r][:, 0:1], sinkt[:, l * 8 + hh:l * 8 + hh + 1], -1.0, ALU.max, ALU.mult, [('sm', r), 'sinkt'], [('sm', r)])
                    act(p_sc[r][:, :], s_sc[r][:, :], AF.Exp, [['Eb', 'Ei'][r], ('sm', r)], [['kh', 'bb'][r]], bias=sm[r][:, 1:2], scale=1.0)
                    S.op('dve', lambda e, r=r: e.tensor_reduce(out=sm[r][:, 2:3], in_=p_sc[r][:, :], axis=AX.X, op=ALU.add), reads=[['kh', 'bb'][r]], writes=[('sm', r)])
                    act(sm[r][:, 3:4], sinkt[:, l * 8 + hh:l * 8 + hh + 1], AF.Exp, ['sinkt', ('sm', r)], [('sm', r)], bias=sm[r][:, 1:2], scale=1.0)
                    tt('dve', sm[r][:, 4:5], sm[r][:, 2:3], sm[r][:, 3:4], ALU.add, [('sm', r)], [('sm', r)])
                    S.op('dve', lambda e, r=r: e.reciprocal(out=sm[r][:, 5:6], in_=sm[r][:, 4:5]), reads=[('sm', r)], writes=[('sm', r)])
                    ts(pn_sc[r][:, :], p_sc[r][:, :], sm[r][:, 5:6], None, ALU.mult, None, [['kh', 'bb'][r], ('sm', r)], [('pn_sc', r)])
                    bt_ = pbank('mx')
                    for half in range(2):
                        S.op('pe', lambda e, half=half, r=r, bt_=bt_: e.transpose(out=pbb[bt_][:, half * 128:(half + 1) * 128], in_=pn_sc[r][:, half * 128:(half + 1) * 128], identity=idb[:, :]), reads=[('pn_sc', r), 'idb'], writes=[('ps', bt_)])
                    act(pT_sc[r][:, :], pbb[bt_][:, 0:256], AF.Copy, [('ps', bt_)], [('pT_sc', r)])
                    bo = pbank('mx')
                    mm(pb[bo][:, 0:128], vwin[l][:, j, kv * 128:(kv + 1) * 128], pT_sc[r][:, 0:128], True, False, [('vwin', l), ('pT_sc', r)], [('ps', bo)])
                    mm(pb[bo][:, 0:128], vwin[l][:, j + 1, kv * 128:(kv + 1) * 128], pT_sc[r][:, 128:256], False, True, [('vwin', l), ('pT_sc', r)], [('ps', bo)])
                    act(mix[:, 4 + hh, j * 128:(j + 1) * 128], pb[bo][:, 0:128], AF.Copy, [('ps', bo)], [('mix', 4 + hh)])
            if W > TT:
                sample_swa(l)
            act(kwin[l][:, :, 0:128], kwin[l][:, :, 256:384], AF.Copy, [('kwin', l)], [('kwin', l)])
            act(vwin[l][:, 0, :], vwin[l][:, 2, :], AF.Copy, [('vwin', l)], [('vwin', l)])

        def sample_swa(l):
            for b in range(NS):
                S.dma('pool', Kw[0:127, :], ck[l, b, 1:128, :], writes=['Kw'], semkey='kvw')
                S.dma('pool', Vw[0:127, :], cv[l, b, 1:128, :], writes=['Vw'], semkey='kvw')
                S.dma('pool', Kw[127:128, :], kvs[b:b + 1, 0:256], reads=['kvs'], writes=['Kw'], semkey='kvw')
                S.dma('pool', Vw[127:128, :], kvs[b:b + 1, 256:512], reads=['kvs'], writes=['Vw'], semkey='kvw')
                S.settle('kvw', ['Kw', 'Vw'])
                S.dma('pool', o_sk[l, b], Kw[:, :], reads=['Kw'], semkey='kvwout')
                S.dma('pool', o_sv[l, b], Vw[:, :], reads=['Vw'], semkey='kvwout')
                bt_ = pbank('mx')
                for kv in range(2):
                    S.op('pe', lambda e, kv=kv, bt_=bt_: e.transpose(out=pb[bt_][:, kv * 128:(kv + 1) * 128], in_=Kw[:, kv * 128:(kv + 1) * 128], identity=idf[:, :]), reads=['Kw', 'idf'], writes=[('ps', bt_)])
                act(KwT[:, :, :], pb[bt_][:, 0:256].rearrange("p (i j) -> p i j", i=2), AF.Copy, [('ps', bt_)], ['KwT'])
                bs2 = pbank('mx')
                for kv in range(2):
                    mm(pb[bs2][0:4, kv * 128:(kv + 1) * 128], qsw[:, b, kv * 4:kv * 4 + 4], KwT[:, kv, :], True, True, ['qsw', 'KwT'], [('ps', bs2)])
                stt(ssm[:, :, :].rearrange("g i j -> g (i j)"), pb[bs2][0:4, 0:256], SCALE, bis[:, 0:2, :].rearrange("g i j -> g (i j)"), ALU.mult, ALU.add, [('ps', bs2), 'bis'], ['ssm'])
                sk4 = sink4[:, l, :]
                S.op('dve', lambda e: e.tensor_reduce(out=st4[:, 0, 0:2], in_=ssm[:, :, :], axis=AX.X, op=ALU.max), reads=['ssm'], writes=['st4'])
                tt('dve', st4[:, 1, 0:2], st4[:, 0, 0:2], sk4, ALU.max, ['st4', 'sink4'], ['st4'])
                tt('dve', ssm[:, :, :], ssm[:, :, :], st4[:, 1, 0:2].unsqueeze(2).to_broadcast([4, 2, P]), ALU.subtract, ['ssm', 'st4'], ['ssm'])
                act(psm[0:4, :, :], ssm[:, :, :], AF.Exp, ['ssm'], ['psm'])
                S.op('dve', lambda e: e.tensor_reduce(out=st4[:, 2, 0:2], in_=psm[0:4, :, :], axis=AX.X, op=ALU.add), reads=['psm'], writes=['st4'])
                tt('dve', st4[:, 3, 0:2], sk4, st4[:, 1, 0:2], ALU.subtract, ['st4', 'sink4'], ['st4'])
                act(st4[:, 3, 0:2], st4[:, 3, 0:2], AF.Exp, ['st4'], ['st4'])
                tt('dve', st4[:, 4, 0:2], st4[:, 2, 0:2], st4[:, 3, 0:2], ALU.add, ['st4'], ['st4'])
                S.op('dve', lambda e: e.reciprocal(out=st4[:, 5, 0:2], in_=st4[:, 4, 0:2]), reads=['st4'], writes=['st4'])
                tt('dve', psm[0:4, :, :], psm[0:4, :, :], st4[:, 5, 0:2].unsqueeze(2).to_broadcast([4, 2, P]), ALU.mult, ['psm', 'st4'], ['psm'])
                bt2 = pbank('mx')
                for kv in range(2):
                    S.op('pe', lambda e, kv=kv, bt2=bt2: e.transpose(out=pb[bt2][:, kv * 128:(kv + 1) * 128], in_=psm[:, kv, :], identity=idf[:, :]), reads=['psm', 'idf'], writes=[('ps', bt2)])
                act(pnT[:, 0:8].rearrange("p (k g) -> p k g", k=2), pb[bt2][:, 0:256].rearrange("p (k j) -> p k j", k=2)[:, :, 0:4], AF.Copy, [('ps', bt2)], ['pnT'])
                bo = pbank('mx')
                for kv in range(2):
                    mm(pb[bo][:, kv * 4:kv * 4 + 4], Vw[:, kv * 128:(kv + 1) * 128], pnT[:, kv * 4:kv * 4 + 4], True, True, ['Vw', 'pnT'], [('ps', bo)])
                act(mix[:, 4:12, TT + b], pb[bo][:, 0:8], AF.Copy, [('ps', bo)], [('mix', 4 + i) for i in range(8)])

        def conv_chunk(l, uc, bank, W, dst, r, t, dk):
            bk = ('ps', bank)
            u = ug[r]
            uk = ('ug', r)
            act(u[:, 2:2 + TT], pb[bank][:, 0:TT], AF.Copy, [bk], [uk])
            S.op('pool', lambda e: e.tensor_copy(out=u[:, 0:2], in_=upre[l][:, uc, :]), reads=[('upre', l, uc)], writes=[uk])
            cwc = V_CW + (l * 3) * 88 + uc
            act(dst[:, 0:W], pb[bank][:, 0:W], AF.Identity, [bk, 'vec'], [dk], scale=vcol(cwc + 2 * 88), bias=vcol(V_CB + l * 88 + uc))
            stt(dst[:, 0:TT], u[:, 1:1 + TT], vcol(cwc + 88), dst[:, 0:TT], ALU.mult, ALU.add, [uk, dk, 'vec'], [dk])
            stt(dst[:, 0:TT], u[:, 0:TT], vcol(cwc), dst[:, 0:TT], ALU.mult, ALU.add, [uk, dk, 'vec'], [dk])
            S.op('pool', lambda e: e.tensor_copy(out=upre[l][:, uc, :], in_=u[:, TT:TT + 2]), reads=[uk], writes=[('upre', l, uc)])
            if W > TT:
                stt(dst[:, TT:W], scp[l][:, uc, :, 1], vcol(cwc + 88), dst[:, TT:W], ALU.mult, ALU.add, ['scp', dk, 'vec'], [dk])
                stt(dst[:, TT:W], scp[l][:, uc, :, 0], vcol(cwc), dst[:, TT:W], ALU.mult, ALU.add, ['scp', dk, 'vec'], [dk])
                S.op('pool', lambda e: e.tensor_copy(out=sco[l][:, uc, :, 0], in_=scp[l][:, uc, :, 1]), reads=['scp'], writes=[('sco', l)])
                act(sco[l][:, uc, :, 1], pb[bank][:, TT:W], AF.Copy, [bk], [('sco', l)])

        def down_half(nk, l, W):
            for ocp in range(8):
                by = [pbank('proj'), pbank('proj')]
                wa, wka = next_w()
                for ci in range(2):
                    proj_fm(wa, wka, ci, nk, 0, m_sb, (lambda k: ('m', k)), by[ci], W)
                for ci in range(2):
                    oc = ocp * 2 + ci
                    tt('dve', x_sb[:, oc, 0:W], x_sb[:, oc, 0:W], pb[by[ci]][:, 0:W], ALU.add, ['x', ('ps', by[ci])], ['x'])

        def ffn(t, l, W):
            rmsnorm(V_G2 + l * 16, hT, 'hT', W)
            for jp in range(NJ // 2):
                grp = 'proj' if jp % 2 == 0 else 'mx'
                wg, wgk = next_w()
                bg = [pbank(grp), pbank(grp)]
                for ci in range(2):
                    proj_fm(wg, wgk, ci, 16, 0, hT, 'hT', bg[ci], W)
                wv, wvk = next_w()
                bv = [pbank(grp), pbank(grp)]
                for ci in range(2):
                    proj_fm(wv, wvk, ci, 16, 0, hT, 'hT', bv[ci], W)
                for ci in range(2):
                    j = jp * 2 + ci
                    conv_chunk(l, j, bg[ci], W, cg, 0, t, 'sig')
                    conv_chunk(l, NJ + j, bv[ci], W, cvv, 1, t, 'lf')
                    act(tg[:, 0:W], cg[:, 0:W], AF.Square, ['sig'], ['ka'])
                    ts(tg[:, 0:W], tg[:, 0:W], 0.044715, 1.0, ALU.mult, ALU.add, ['ka'], ['ka'])
                    tt('dve', tg[:, 0:W], tg[:, 0:W], cg[:, 0:W], ALU.mult, ['ka', 'sig'], ['ka'])
                    act(sg[:, 0:W], tg[:, 0:W], AF.Sigmoid, ['ka'], ['gate'], scale=1.5957691216057308)
                    tt('dve', sg[:, 0:W], sg[:, 0:W], cg[:, 0:W], ALU.mult, ['gate', 'sig'], ['gate'])
                    mi = j - 2 * [p0 for (p0, p1) in MGRP if p0 <= jp < p1][0]
                    tt('dve', m_sb[:, mi, 0:W], sg[:, 0:W], cvv[:, 0:W], ALU.mult, ['gate', 'lf'], [('m', mi)])
                for (p0, p1) in MGRP:
                    if jp == p1 - 1:
                        down_half(2 * (p1 - p0), l, W)
            if t == nt - 1:
                S.dma('pool', o_pc[l], upre[l][:, :, :], reads=[('upre', l, uc) for uc in range(88)], semkey='out')
            if W > TT:
                S.dma('pool', o_sc[l], sco[l][:, :, :, :], reads=[('sco', l)], semkey='out')

        def layer(t, l, W):
            if KSTOP < 1:
                return
            rmsnorm(V_G1 + l * 16, hT, 'hT', W)
            if KSTOP < 2:
                return
            for h in range(4):
                lin_head('a', t, l, h, W)
                if KSTOP < 3:
                    return
            if KSTOP < 4:
                return
            swa(t, l, W)
            if KSTOP < 5:
                return
            for h in range(4):
                lin_head('r', t, l, h, W)
            if KSTOP < 6:
                return
            mixkeys = [('mix', i) for i in range(16)]
            for ocp in range(8):
                w, wk = next_w()
                for ci in range(2):
                    oc = ocp * 2 + ci
                    by = pbank('proj')
                    proj_fm(w, wk, ci, 16, 0, mix, (lambda k: ('mix', k)), by, W)
                    tt('dve', x_sb[:, oc, 0:W], x_sb[:, oc, 0:W], pb[by][:, 0:W], ALU.add, ['x', ('ps', by)], ['x'])
            if KSTOP < 7:
                return
            ffn(t, l, W)

        plan_weights()
        for dst, src, key in [(vec, vecs, 'vec'), (sinkt, sinkb, 'sinkt'), (sink4, sinks4, 'sink4'), (cmk, cmask, 'masks'), (dmk, dmask, 'masks'),
                              (rwc, rowc, 'rwc'), (bia, biasd, 'bia'), (bis, biass, 'bis'), (sel, seld, 'sel'), (idb, identb, 'idb'),
                              (idf, identf, 'idf'), (ones, onesd, 'ones')]:
            S.dma('pool', dst[:], src, writes=[key], semkey='const')
        for l in range(2):
            S.dma('pool', scp[l][:, :, :, :], scT[l], writes=['scp'], semkey='const')
        S.settle('const', ['vec', 'sinkt', 'sink4', 'masks', 'rwc', 'bia', 'bis', 'sel', 'idb', 'idf', 'ones', 'scp'])
        tt('dve', vec[:, V_LB + 4:V_LB + 8], vec[:, V_RAW + 4:V_RAW + 8], vec[:, V_RAW:V_RAW + 4], ALU.subtract, ['vec'], ['vec'])
        act(vec[:, V_LB + 4:V_LB + 8], vec[:, V_LB + 4:V_LB + 8], AF.Sigmoid, ['vec'], ['vec'])
        ts(vec[:, V_OML:V_OML + 8], vec[:, V_LB:V_LB + 8], -1.0, 1.0, ALU.mult, ALU.add, ['vec'], ['vec'])
        ts(vec[:, V_LBM1:V_LBM1 + 8], vec[:, V_LB:V_LB + 8], -1.0, 0.0, ALU.add, ALU.add, ['vec'], ['vec'])
        for l in range(2):
            if l == 0:
                S.op('pool', lambda e: e.memset(vs[:, :], 0.0), writes=['vs'])
            S.op('pool', lambda e, l=l: e.memset(upre[l][:, :, :], 0.0), writes=[('upre', l, uc) for uc in range(88)])
            S.op('pool', lambda e, l=l: e.memset(kwin[l][:, :, :], 0.0), writes=[('kwin', l)])
            S.op('pool', lambda e, l=l: e.memset(vwin[l][:, :, :], 0.0), writes=[('vwin', l)])
            for k in 'ar':
                for h in range(4):
                    S.op('pool', lambda e, k=k, l=l, h=h: e.memset(Sf[(k, l, h)][:, :], 0.0), writes=[('Sf', k, l, h)])

        for t in range(nt):
            W = TT + NS if t == 0 else TT
            S.dma('sp', x_sb[:, :, 0:TT], xT[:, :, t * TT:(t + 1) * TT].rearrange("c p n -> p c n"), writes=['x'], reads=[], semkey='xload')
            if t == 0:
                S.dma('sp', x_sb[:, :, TT:W], xT[:, :, T:T + NS].rearrange("c p n -> p c n"), writes=['x'], semkey='xload')
            for l in range(2):
                layer(t, l, W)
            rmsnorm(V_GF, x_sb, 'x', W)
            S.dma('pool', yT[:, :, t * TT:(t + 1) * TT].rearrange("c p n -> p c n"), x_sb[:, :, 0:TT], reads=['x'], semkey='yout')
            if t == 0:
                S.dma('pool', yT[:, :, T:T + NS].rearrange("c p n -> p c n"), x_sb[:, :, TT:W], reads=['x'], semkey='yout')
        for l in range(2):
            for h in range(4):
                S.dma('pool', o_hs[l, h], Sf[('a', l, h)][:, :], reads=[('Sf', 'a', l, h)], semkey='out')
                S.dma('pool', o_rs[l, h], Sf[('r', l, h)][:, :], reads=[('Sf', 'r', l, h)], semkey='out')
        assert KSTOP < 99 or wstate['used'] == len(wq), (wstate, len(wq))
        S.finish('sp')
        print("ops", S.nops, {k: v for k, v in S.cnt.items()})
    return nc


_CACHE = {}


def _consts():
    import ml_dtypes
    bf = ml_dtypes.bfloat16
    c = {}
    s = np.arange(128)[:, None]
    tcol = np.arange(256)[None, :]
    tl = tcol % 128
    c['cmask'] = (((s // 64) == (tl // 64)) & (s <= tl)).astype(np.float32).astype(bf)
    dm = np.zeros((128, 4, 256), np.float32)
    rowc = np.zeros((128, 9, 256), np.float32)
    rowc[:, 0, :] = 1.0
    rowc[:, 0, ::64] = 0.0
    for h in range(4):
        g = np.float64(GAMMAS[h])
        dm[:, h, :] = np.where(tl >= s, g ** np.maximum(tl - s, 0) * SCALE, 0.0)
        rowc[:, 1 + h, :] = g ** (tl + 1.0)
        rowc[:, 5 + h, :] = g ** (127.0 - tl) * SCALE
    c['dmask'] = dm.astype(bf)
    c['rowc'] = rowc
    qi = np.arange(128)[:, None]
    sj = np.arange(256)[None, :]
    dist = 128 + qi - sj
    valid = (dist >= 0) & (dist < 128)
    bd = np.zeros((128, 16, 256), np.float32)
    for h in range(8):
        bd[:, h, :] = np.where(valid, -SLOPES[h] * dist, -1e30)
        bd[:, 8 + h, :] = np.where(valid & (sj >= 128), -SLOPES[h] * dist, -1e30)
    c['biasd'] = bd.astype(bf)
    bs = np.zeros((4, 2, 128), np.float32)
    for g in range(4):
        for kv in range(2):
            bs[g, kv, :] = -SLOPES[kv * 4 + g] * (127.0 - np.arange(128))
    c['biass'] = bs
    sd = np.zeros((128, 4, 128), np.float32)
    for b in range(4):
        sd[b, b, :] = 1.0
    c['seld'] = sd
    c['identb'] = np.eye(128, dtype=np.float32).astype(bf)
    c['identf'] = np.eye(128, dtype=np.float32)
    on = np.zeros((128, 2, 128), np.float32)
    on[:, 0, :] = 1.0 / 2048.0
    on[:, 1, :] = 1.0 / 128.0
    c['onesd'] = on
    return c


def kernel(x_prompt, x_sample, state_hgrn, cache_swa_k, cache_swa_v, state_ret, state_conv,
           norm1_g, w_in, hgrn_lb_raw, hgrn_norm_g, swa_sinks, ret_norm_g, w_out,
           norm2_g, w_up, conv_w, conv_b, w_down, final_norm_g):
    f = lambda a: np.ascontiguousarray(np.asarray(a, dtype=np.float32))
    x_prompt, x_sample = f(x_prompt), f(x_sample)
    state_hgrn, state_ret = f(state_hgrn), f(state_ret)
    cache_swa_k, cache_swa_v, state_conv = f(cache_swa_k), f(cache_swa_v), f(state_conv)
    w_in, w_out, w_up, w_down = f(w_in), f(w_out), f(w_up), f(w_down)
    if 'nc' not in _CACHE:
        _CACHE['nc'] = build_program()
        _CACHE['c'] = _consts()
    nc = _CACHE['nc']
    cst = _CACHE['c']

    raise_if = None
    in_maps = []
    n_cores = 8
    vecs = np.zeros((128, VW), np.float32)
    n1 = f(norm1_g).reshape(2, 16, 128); n2 = f(norm2_g).reshape(2, 16, 128); nf = f(final_norm_g).reshape(16, 128)
    for l in range(2):
        vecs[:, 0 + l * 16:0 + (l + 1) * 16] = n1[l].T
        vecs[:, 32 + l * 16:32 + (l + 1) * 16] = n2[l].T
    vecs[:, 64:80] = nf.T
    ga = f(hgrn_norm_g).reshape(2, 4, 128); gc = f(ret_norm_g).reshape(2, 4, 128)
    for l in range(2):
        vecs[:, 104 + l * 4:104 + (l + 1) * 4] = ga[l].T
        vecs[:, 112 + l * 4:112 + (l + 1) * 4] = gc[l].T
    cw = f(conv_w).reshape(2, 3, 88, 128); cb = f(conv_b).reshape(2, 88, 128)
    for l in range(2):
        for tap in range(3):
            vecs[:, V_CW + (l * 3 + tap) * 88:V_CW + (l * 3 + tap + 1) * 88] = cw[l, tap].T
        vecs[:, V_CB + l * 88:V_CB + (l + 1) * 88] = cb[l].T
    vecs[:, V_EPS] = EPS
    vecs[0:64, V_MLO] = 1.0
    vecs[64:128, V_MHI] = 1.0
    lbraw = f(hgrn_lb_raw).reshape(2, 4, 128)
    vecs[:, V_RAW:V_RAW + 4] = lbraw[0].T
    vecs[:, V_RAW + 4:V_RAW + 8] = lbraw[1].T
    sk = f(swa_sinks)
    sinkb = np.broadcast_to(sk.reshape(1, 16), (128, 16)).copy()
    sinks4 = np.ascontiguousarray(sk.reshape(2, 2, 4).transpose(2, 0, 1))
    for c in range(n_cores):
        b = c % 4
        xT = np.empty((16, 128, T + NS), np.float32)
        xT[:, :, :T] = x_prompt[b].T.reshape(16, 128, T)
        xT[:, :, T:] = x_sample[c * 4:(c + 1) * 4, 0, :].T.reshape(16, 128, NS)
        m = {
            'xT': xT,
            'st_h': np.ascontiguousarray(state_hgrn[:, c * 4:(c + 1) * 4]),
            'st_r': np.ascontiguousarray(state_ret[:, c * 4:(c + 1) * 4]),
            'ck': np.ascontiguousarray(cache_swa_k[:, c * 4:(c + 1) * 4].reshape(2, 4, 128, 256)),
            'cv': np.ascontiguousarray(cache_swa_v[:, c * 4:(c + 1) * 4].reshape(2, 4, 128, 256)),
            'scT': np.ascontiguousarray(state_conv[:, c * 4:(c + 1) * 4].reshape(2, 4, 2, 88, 128).transpose(0, 4, 3, 1, 2)),
            'w_in': w_in, 'w_out': w_out, 'w_up': w_up, 'w_down': w_down,
            'vecs': vecs, 'sinkb': sinkb, 'sinks4': sinks4,
        }
        m.update(cst)
        in_maps.append(m)
    if _CACHE.get('maps_only'):
        return in_maps
    res = run_bass_kernel_spmd(nc, in_maps, core_ids=list(range(n_cores)))
    R = res.results
    y_prompt = np.stack([R[b]['yT'][:, :, :T].reshape(D, T).T for b in range(4)])
    y_sample = np.concatenate([R[c]['yT'][:, :, T:].reshape(D, NS).T for c in range(8)])[:, None, :]
    p_hgrn = np.stack([R[b]['o_hs'] for b in range(4)], axis=1)
    p_ret = np.stack([R[b]['o_rs'] for b in range(4)], axis=1)
    p_k = np.stack([R[b]['o_pk'].reshape(2, 128, 2, 128) for b in range(4)], axis=1)
    p_v = np.stack([R[b]['o_pv'].reshape(2, 128, 2, 128) for b in range(4)], axis=1)
    p_conv = np.stack([R[b]['o_pc'].transpose(0, 3, 2, 1).reshape(2, 2, 11264) for b in range(4)], axis=1)
    s_hgrn = np.concatenate([R[c]['o_shs'] for c in range(8)], axis=1)
    s_ret = np.concatenate([R[c]['o_srs'] for c in range(8)], axis=1)
    s_k = np.concatenate([R[c]['o_sk'].reshape(2, 4, 128, 2, 128) for c in range(8)], axis=1)
    s_v = np.concatenate([R[c]['o_sv'].reshape(2, 4, 128, 2, 128) for c in range(8)], axis=1)
    s_conv = np.concatenate([R[c]['o_sc'].transpose(0, 3, 4, 2, 1).reshape(2, 4, 2, 11264) for c in range(8)], axis=1)
    out = (y_prompt, y_sample, p_hgrn, p_k, p_v, p_ret, p_conv, s_hgrn, s_k, s_v, s_ret, s_conv)
    return tuple(np.ascontiguousarray(o, dtype=np.float32) for o in out)
```

```python
import os
import numpy as np
from contextlib import ExitStack
import concourse.bass as bass
import concourse.mybir as mybir
from concourse.bass_utils import run_bass_kernel_spmd

F32 = mybir.dt.float32
F32R = mybir.dt.float32r
BF16 = mybir.dt.bfloat16
AF = mybir.ActivationFunctionType
ALU = mybir.AluOpType
AX = mybir.AxisListType

P = 128
D = 2048
NKC = 16
T = 2048
TT = 256
NT = T // TT
NS = 4
DFF = 5632
NJ = DFF // P
NWT = 8 + 4 + 2 + 8 + 8 + 44 + 32
MGRP = [(0, 6), (6, 12), (12, 18), (18, 22)]
INC = 5632
EPS = 1e-6
SCALE = 128.0 ** -0.5
GAMMAS = [float(1.0 - 2.0 ** (-5.0 - h)) for h in range(4)]
SLOPES = [float(2.0 ** (-(h + 1.0))) for h in range(8)]
VW = 1024
V_G1, V_G2, V_GF = 0, 32, 64
V_LB, V_OML, V_LBM1 = 80, 88, 96
V_GA, V_GC = 104, 112
V_CW = 120
V_CB = V_CW + 2 * 3 * 88
V_EPS = V_CB + 2 * 88
V_RAW = V_EPS + 1
V_MLO = V_RAW + 8
V_MHI = V_MLO + 1
C_QA, C_FA, C_IA, C_GA, C_QB, C_KB, C_VB, C_QC, C_KC, C_VC, C_GC = 0, 512, 1024, 1536, 2048, 3072, 3328, 3584, 4096, 4608, 5120


class Sched:
    def __init__(self, nc, sems, dma_sems):
        self.nc = nc
        self.eng = {'pe': nc.tensor, 'act': nc.scalar, 'dve': nc.vector, 'pool': nc.gpsimd, 'sp': nc.sync}
        self.sem = sems
        self.cnt = {e: 0 for e in self.eng}
        self.dma_sems = list(dma_sems)
        self.dma_cnt = {}
        self.dma_tot = {}
        self.key_sem = {}
        self.waited = {e: {} for e in self.eng}
        self.lastw = {}
        self.readers = {}
        self.nops = 0

    def _need(self, e, rec, waits):
        semname, sem, val, src = rec
        if src == e and e == 'pe':
            return
        if src == 'dma':
            val = self.dma_cnt[semname[4:]] if False else self.dma_tot[semname]
        if self.waited[e].get(semname, 0) >= val:
            return
        prev = waits.get(semname)
        if prev is None or prev[1] < val:
            waits[semname] = (sem, val)

    def _deps(self, e, reads, writes):
        waits = {}
        for k in reads:
            w = self.lastw.get(k)
            if w is not None:
                self._need(e, w, waits)
        for k in writes:
            w = self.lastw.get(k)
            if w is not None and not (w[3] == e):
                self._need(e, w, waits)
            for r in self.readers.get(k, ()):
                if r[3] == e:
                    continue
                self._need(e, r, waits)
        for semname, (sem, val) in waits.items():
            self.eng[e].wait_ge(sem, val)
            self.waited[e][semname] = val

    def _record(self, rec, reads, writes):
        for k in reads:
            self.readers.setdefault(k, []).append(rec)
        for k in writes:
            self.lastw[k] = rec
            self.readers[k] = []
        self.nops += 1

    def op(self, e, fn, reads=(), writes=()):
        self._deps(e, reads, writes)
        ins = fn(self.eng[e])
        self.cnt[e] += 1
        ins.then_inc(self.sem[e], 1)
        self._record((e, self.sem[e], self.cnt[e], e), reads, writes)
        return ins

    def dma(self, e, out, in_, reads=(), writes=(), semkey=None):
        self._deps(e, reads, writes)
        if semkey is None:
            semkey = 'misc'
        if semkey not in self.key_sem:
            self.key_sem[semkey] = self.dma_sems.pop()
            self.dma_cnt[semkey] = 0
        sem = self.key_sem[semkey]
        ins = self.eng[e].dma_start(out=out, in_=in_)
        self.dma_cnt[semkey] += 16
        self.dma_tot['dma:%s' % (semkey,)] = self.dma_cnt[semkey]
        ins.then_inc(sem, 16)
        self._record(('dma:%s' % (semkey,), sem, self.dma_cnt[semkey], 'dma'), reads, writes)
        return ins

    def settle(self, semkey, keys):
        sem = self.key_sem[semkey]
        for k in keys:
            self.lastw[k] = ('dma:%s' % (semkey,), sem, self.dma_cnt[semkey], 'dma')

    def finish(self, e='sp'):
        for k, sem in self.key_sem.items():
            self.eng[e].wait_ge(sem, self.dma_cnt[k])
        for n, sem in self.sem.items():
            if self.cnt[n] > 0:
                self.eng[e].wait_ge(sem, self.cnt[n])


def build_program(nt=NT):
    nc = bass.Bass("TRN2", target_bir_lowering=False)
    nc.dge_precook = False

    def din(name, shape, dt=F32):
        return nc.dram_tensor(name, list(shape), dt, kind="ExternalInput").ap()

    def dout(name, shape, dt=F32):
        return nc.dram_tensor(name, list(shape), dt, kind="ExternalOutput").ap()

    xT = din("xT", [NKC, P, T + NS])
    st_h = din("st_h", [2, NS, 4, P, P])
    st_r = din("st_r", [2, NS, 4, P, P])
    ck = din("ck", [2, NS, P, 256])
    cv = din("cv", [2, NS, P, 256])
    scT = din("scT", [2, P, 88, NS, 2])
    w_in = din("w_in", [2, D, INC])
    w_out = din("w_out", [2, D, D])
    w_up = din("w_up", [2, D, 2 * DFF])
    w_down = din("w_down", [2, DFF, D])
    wscr = nc.dram_tensor("wscr", [2 * NWT, P, 4096], BF16).ap()
    vecs = din("vecs", [P, VW])
    sinkb = din("sinkb", [P, 16])
    sinks4 = din("sinks4", [4, 2, 2])
    cmask = din("cmask", [P, 256], BF16)
    dmask = din("dmask", [P, 4, 256], BF16)
    rowc = din("rowc", [P, 9, 256])
    biasd = din("biasd", [P, 16, 256], BF16)
    biass = din("biass", [4, 2, 128])
    seld = din("seld", [P, 4, P])
    identb = din("identb", [P, P], BF16)
    identf = din("identf", [P, P])
    onesd = din("onesd", [P, 2, P], F32R)

    yT = dout("yT", [NKC, P, T + NS])
    o_hs = dout("o_hs", [2, 4, P, P])
    o_rs = dout("o_rs", [2, 4, P, P])
    o_pk = dout("o_pk", [2, P, 256])
    o_pv = dout("o_pv", [2, P, 256])
    o_pc = dout("o_pc", [2, P, 88, 2])
    o_shs = dout("o_shs", [2, NS, 4, P, P])
    o_srs = dout("o_srs", [2, NS, 4, P, P])
    o_sk = dout("o_sk", [2, NS, P, 256])
    o_sv = dout("o_sv", [2, NS, P, 256])
    o_sc = dout("o_sc", [2, P, 88, NS, 2])


    with ExitStack() as es:
        def sbuf(name, shape, dt=F32):
            return es.enter_context(nc.sbuf_tensor(name, list(shape), dt))

        def psum(name, shape, dt=F32):
            return es.enter_context(nc.psum_tensor(name, list(shape), dt))

        WT = TT + NS
        x_sb = sbuf("x_sb", [P, NKC, WT])
        hT = sbuf("hT", [P, NKC, WT], BF16)
        mix = sbuf("mix", [P, NKC, WT], BF16)
        m_sb = sbuf("m_sb", [P, 12, WT], BF16)
        NWS = 3
        wsl = [sbuf("wsl%d" % i, [P, NKC * 256], BF16) for i in range(NWS)]
        wst = [sbuf("wst%d" % i, [P, NKC * 256]) for i in range(2)]
        vec = sbuf("vec", [P, VW])
        sinkt = sbuf("sinkt", [P, 16])
        sink4 = sbuf("sink4", [4, 2, 2])
        cmk = sbuf("cmk", [P, 256], BF16)
        dmk = sbuf("dmk", [P, 4, 256], BF16)
        rwc = sbuf("rwc", [P, 9, 256])
        bia = sbuf("bia", [P, 16, 256], BF16)
        bis = sbuf("bis", [4, 2, 128])
        sel = sbuf("sel", [P, 4, P])
        idb = sbuf("idb", [P, P], BF16)
        idf = sbuf("idf", [P, P])
        ones = sbuf("ones", [P, 2, P], F32R)
        Sf = {(k, l, h): sbuf("Sf%s%d%d" % (k, l, h), [P, P]) for k in 'ar' for l in range(2) for h in range(4)}
        kwin = [sbuf("kwin%d" % l, [P, 2, 384], BF16) for l in range(2)]
        vwin = [sbuf("vwin%d" % l, [P, 3, 256], BF16) for l in range(2)]
        upre = [sbuf("upre%d" % l, [P, 88, 2]) for l in range(2)]
        sig = sbuf("sig", [P, WT]); lf = sbuf("lf", [P, WT]); ka = sbuf("ka", [P, WT])
        bb = sbuf("bb", [P, TT]); Eb = sbuf("Eb", [P, TT]); Ei = sbuf("Ei", [P, TT]); kh = sbuf("kh", [P, TT])
        qt = sbuf("qt", [P, TT], BF16); qin = sbuf("qin", [P, TT], BF16); khb = sbuf("khb", [P, TT], BF16)
        kbar = sbuf("kbar", [P, TT], BF16); kbT = sbuf("kbT", [P, 2, 2, P], BF16); vt = sbuf("vt", [P, 2, P], BF16)
        PTs = sbuf("PTs", [P, TT], BF16); Sb = sbuf("Sb", [P, 4, P], BF16)
        o_sb = sbuf("o_sb", [P, WT], F32R); cen = sbuf("cen", [P, WT], F32R); sqo = sbuf("sqo", [P, WT], F32R)
        rstd = sbuf("rstd", [P, WT]); gate = sbuf("gate", [P, WT])
        sqx = [sbuf("sqx%d" % i, [P, WT], F32R) for i in range(2)]
        qb = sbuf("qb", [P, 8, TT], BF16)
        kv32 = sbuf("kv32", [P, 512])
        s_sc = [Eb, Ei]
        p_sc = [kh, bb]
        pn_sc = [sbuf("pn_sc%d" % i, [P, 256], BF16) for i in range(2)]
        pT_sc = [sbuf("pT_sc%d" % i, [P, 256], BF16) for i in range(2)]
        sm = [sbuf("sm%d" % i, [P, 8]) for i in range(2)]
        ug = [sbuf("ug%d" % i, [P, TT + 2]) for i in range(2)]
        cg, cvv, tg, sg = sig, lf, ka, gate
        Ss = sbuf("Ss", [P, NS, P]); tmpS = sbuf("tmpS", [P, NS, P])
        vs = sbuf("vs", [P, P]); qs = sbuf("qs", [P, NS]); ks = sbuf("ks", [P, NS]); fs = sbuf("fs", [P, NS])
        Kw = sbuf("Kw", [P, 256]); Vw = sbuf("Vw", [P, 256]); KwT = sbuf("KwT", [P, 2, P])
        kvs = sbuf("kvs", [4, 512]); qsw = sbuf("qsw", [P, NS, 8])
        ssm = sbuf("ssm", [4, 2, P]); psm = sbuf("psm", [P, 2, P]); st4 = sbuf("st4", [4, 6, 8])
        pnT = sbuf("pnT", [P, 32])
        scp = [sbuf("scp%d" % l, [P, 88, NS, 2]) for l in range(2)]
        sco = [sbuf("sco%d" % l, [P, 88, NS, 2]) for l in range(2)]
        pb = [psum("pb%d" % i, [P, 512]) for i in range(8)]
        pbb = [b.bitcast(BF16) for b in pb]

        sems = {n: es.enter_context(nc.semaphore("s_" + n)) for n in ['pe', 'act', 'dve', 'pool']}
        dsems = [es.enter_context(nc.semaphore("d%d" % i)) for i in range(16)]
        es.enter_context(nc.Block())
        S = Sched(nc, sems, dsems)

        def vcol(c):
            return vec[:, c:c + 1]

        bank_rr = [0]

        def pbank(group):
            i = bank_rr[0]
            bank_rr[0] += 1
            return (i % 4) if group == 'proj' else 4 + (i % 4)

        def mm(out, lhsT, rhs, start, stop, reads, writes):
            S.op('pe', lambda e: e.matmul(out, lhsT=lhsT, rhs=rhs, start=start, stop=stop), reads=reads, writes=writes)

        def act(out, in_, func, reads, writes, **kw):
            S.op('act', lambda e: e.activation(out=out, in_=in_, func=func, **kw), reads=reads, writes=writes)

        def tt(eng, out, in0, in1, op, reads, writes):
            S.op(eng, lambda e: e.tensor_tensor(out=out, in0=in0, in1=in1, op=op), reads=reads, writes=writes)

        def ts(out, in0, s1, s2, op0, op1, reads, writes):
            if op1 is None:
                S.op('dve', lambda e: e.tensor_scalar(out=out, in0=in0, scalar1=s1, scalar2=0.0, op0=op0, op1=ALU.add), reads=reads, writes=writes)
            else:
                S.op('dve', lambda e: e.tensor_scalar(out=out, in0=in0, scalar1=s1, scalar2=s2, op0=op0, op1=op1), reads=reads, writes=writes)

        def stt(out, in0, scalar, in1, op0, op1, reads, writes):
            S.op('dve', lambda e: e.scalar_tensor_tensor(out=out, in0=in0, scalar=scalar, in1=in1, op0=op0, op1=op1), reads=reads, writes=writes)

        wq = []
        wstate = {'issued': 0, 'used': 0}

        def plan_weights():
            for t in range(nt):
                for l in range(2):
                    base = len(wq)
                    for h in range(4):
                        wq.append((w_in[l], 0, 16, [(C_QA + h * 128, 128), (C_FA + h * 128, 128)]))
                        wq.append((w_in[l], 0, 16, [(C_GA + h * 128, 128), (C_IA + h * 128, 128)]))
                    for hp in range(4):
                        wq.append((w_in[l], 0, 16, [(C_QB + hp * 256, 256)]))
                    wq.append((w_in[l], 0, 16, [(C_KB, 256)]))
                    wq.append((w_in[l], 0, 16, [(C_VB, 256)]))
                    for h in range(4):
                        wq.append((w_in[l], 0, 16, [(C_QC + h * 128, 128), (C_KC + h * 128, 128)]))
                        wq.append((w_in[l], 0, 16, [(C_GC + h * 128, 128), (C_VC + h * 128, 128)]))
                    for oc in range(8):
                        wq.append((w_out[l], 0, 16, [(oc * 256, 256)]))
                    for (p0, p1) in MGRP:
                        for jp in range(p0, p1):
                            wq.append((w_up[l], 0, 16, [(jp * 256, 256)]))
                            wq.append((w_up[l], 0, 16, [(DFF + jp * 256, 256)]))
                        for ocp in range(8):
                            wq.append((w_down[l], 2 * p0, 2 * (p1 - p0), [(ocp * 256, 256)]))
                    for i_ in range(base, len(wq)):
                        wq[i_] = wq[i_] + (t, l * NWT + (i_ - base))
                    assert len(wq) - base == NWT, len(wq) - base

        def issue_w(i):
            mat, k0, nk, cols, tpass, sidx = wq[i]
            slot = i % NWS
            tw = sum(w for _, w in cols)
            n = nk * tw
            if tpass == 0:
                st = i % 2
                view = wst[st][:, 0:n].rearrange("p (k c) -> p k c", k=nk)
                off = 0
                for c0, w in cols:
                    src = mat[k0 * 128:(k0 + nk) * 128, c0:c0 + w].rearrange("(k p) c -> p k c", p=128)
                    S.dma('sp', view[:, :, off:off + w], src, writes=[('wst', st)], semkey=('wst', st))
                    off += w
                S.op('pool', lambda e: e.tensor_copy(out=wsl[slot][:, 0:n], in_=wst[st][:, 0:n]), reads=[('wst', st)], writes=[('w', slot)])
                if wstate.get('pending') is not None:
                    pslot, psidx, pn = wstate['pending']
                    S.dma('sp', wscr[psidx][:, 0:pn], wsl[pslot][:, 0:pn], reads=[('w', pslot)], writes=[('wscr', psidx)], semkey='wsc')
                wstate['pending'] = (slot, sidx, n)
            else:
                if wstate.get('pending') is not None:
                    pslot, psidx, pn = wstate['pending']
                    S.dma('sp', wscr[psidx][:, 0:pn], wsl[pslot][:, 0:pn], reads=[('w', pslot)], writes=[('wscr', psidx)], semkey='wsc')
                    wstate['pending'] = None
                S.dma('sp', wsl[slot][:, 0:n], wscr[sidx][:, 0:n], reads=[('wscr', sidx)], writes=[('w', slot)], semkey=('w', slot))

        def next_w():
            i = wstate['used']
            wstate['used'] += 1
            while wstate['issued'] < min(len(wq), i + NWS):
                issue_w(wstate['issued'])
                wstate['issued'] += 1
            mat, k0, nk, cols, tpass, sidx = wq[i]
            slot = i % NWS
            tw = sum(w for _, w in cols)
            view = wsl[slot][:, 0:nk * tw].rearrange("p (k c) -> p k c", k=nk)
            return view, ('w', slot)

        def proj_fm(wv, wkey, ci, nk, k0, actT, akey, bank, W, first=True, last=True):
            bk = ('ps', bank)
            for k in range(nk):
                st = first and k == 0
                sp_ = last and k == nk - 1 and W == TT
                ak = akey(k0 + k) if callable(akey) else akey
                mm(pb[bank][:, 0:TT], wv[:, k, ci * 128:(ci + 1) * 128], actT[:, k0 + k, 0:TT], st, sp_, [wkey, ak], [bk])
                if W > TT:
                    mm(pb[bank][:, TT:W], wv[:, k, ci * 128:(ci + 1) * 128], actT[:, k0 + k, TT:W], False, last and k == nk - 1, [wkey, ak], [bk])

        def proj_tm(wv, wkey, ci, bank, W):
            bk = ('ps', bank)
            for j in range(2):
                for k in range(NKC):
                    mm(pb[bank][:, j * 128:(j + 1) * 128], hT[:, k, j * 128:(j + 1) * 128], wv[:, k, ci * 128:(ci + 1) * 128], k == 0, k == NKC - 1, [wkey, 'hT'], [bk])
            if W > TT:
                for k in range(NKC):
                    mm(pb[bank][0:4, 256:384], hT[:, k, TT:W], wv[:, k, ci * 128:(ci + 1) * 128], k == 0, k == NKC - 1, [wkey, 'hT'], [bk])

        def rmsnorm(gcol0, dst, dkey, W):
            bank = pbank('mx')
            bk = ('ps', bank)
            for c in range(NKC):
                sq = sqx[c % 2]
                act(sq[:, 0:W], x_sb[:, c, 0:W], AF.Square, ['x'], [('sqx', c % 2)])
                mm(pb[bank][:, 0:W], ones[:, 0, :], sq[:, 0:W], c == 0, c == NKC - 1, [('sqx', c % 2), 'ones'], [bk])
            act(rstd[:, 0:W], pb[bank][:, 0:W], AF.Ln, [bk, 'vec'], ['rstd'], bias=vcol(V_EPS), scale=1.0)
            act(rstd[:, 0:W], rstd[:, 0:W], AF.Exp, ['rstd'], ['rstd'], scale=-0.5)
            for c in range(NKC):
                stt(dst[:, c, 0:W], x_sb[:, c, 0:W], vcol(gcol0 + c), rstd[:, 0:W], ALU.mult, ALU.mult, ['x', 'rstd', 'vec'], [dkey(c) if callable(dkey) else dkey])

        import os
        KSTOP = int(os.environ.get('KSTOP', '99'))
        KSUB = int(os.environ.get('KSUB', '99'))

        def lin_head(kind, t, l, h, W):
            first_tile = (t == 0)
            w1, k1 = next_w()
            bq = pbank('proj'); bf = pbank('proj')
            proj_fm(w1, k1, 0, 16, 0, hT, 'hT', bq, W)
            proj_fm(w1, k1, 1, 16, 0, hT, 'hT', bf, W)
            w2, k2 = next_w()
            bg = pbank('proj'); bi = pbank('proj')
            proj_fm(w2, k2, 0, 16, 0, hT, 'hT', bg, W)
            proj_tm(w2, k2, 1, bi, W)
            kq, kf, kg, ki = ('ps', bq), ('ps', bf), ('ps', bg), ('ps', bi)
            lh = l * 4 + h
            act(vt[:, :, :], pb[bi][:, 0:256].rearrange("p (j d) -> p j d", j=2), AF.Copy, [ki], ['vt'])
            if W > TT:
                act(vs[0:4, :], pb[bi][0:4, 256:384], AF.Copy, [ki], ['vs'])
                act(qs[:, :], pb[bq][:, TT:W], AF.Copy, [kq], ['qs'])
            if KSUB < 1:
                return
            act(gate[:, 0:W], pb[bg][:, 0:W], AF.Silu, [kg], ['gate'])
            if KSUB < 2:
                return
            if kind == 'a':
                act(sig[:, 0:W], pb[bf][:, 0:W], AF.Sigmoid, [kf], ['sig'])
                act(lf[:, 0:W], sig[:, 0:W], AF.Ln, ['sig', 'vec'], ['lf'], scale=vcol(V_OML + lh), bias=vcol(V_LB + lh))
                ts(ka[:, 0:W], sig[:, 0:W], -1.0, vcol(V_LBM1 + lh), ALU.add, ALU.mult, ['sig', 'vec'], ['ka'])
                S.op('dve', lambda e: e.tensor_tensor_scan(out=bb[:, :], data0=rwc[:, 0, :], data1=lf[:, 0:TT], initial=0.0, op0=ALU.mult, op1=ALU.add), reads=['lf', 'rwc'], writes=['bb'])
                act(Eb[:, :], bb[:, :], AF.Exp, ['bb'], ['Eb'])
                act(Ei[:, :], bb[:, :], AF.Exp, ['bb'], ['Ei'], scale=-1.0)
                tt('dve', qt[:, :], pb[bq][:, 0:TT], Eb[:, :], ALU.mult, [kq, 'Eb'], ['qt'])
                tt('dve', kh[:, :], ka[:, 0:TT], Ei[:, :], ALU.mult, ['ka', 'Ei'], ['kh'])
                act(khb[:, :], kh[:, :], AF.Copy, ['kh'], ['khb'])
                E3 = Eb[:, :].rearrange("p (c t) -> p c t", t=64)
                tt('dve', kbar[:, :].rearrange("p (c t) -> p c t", t=64), kh[:, :].rearrange("p (c t) -> p c t", t=64),
                   E3[:, :, 63:64].to_broadcast([P, 4, 64]), ALU.mult, ['kh', 'Eb'], ['kbar'])
                qinter = qt
                maskT = cmk[:, :]
                nch = 4
            else:
                act(qt[:, :], pb[bq][:, 0:TT], AF.Copy, [kq], ['qt'])
                tt('dve', qin[:, :], pb[bq][:, 0:TT], rwc[:, 1 + h, :], ALU.mult, [kq, 'rwc'], ['qin'])
                act(khb[:, :], pb[bf][:, 0:TT], AF.Copy, [kf], ['khb'])
                tt('dve', kbar[:, :], pb[bf][:, 0:TT], rwc[:, 5 + h, :], ALU.mult, [kf, 'rwc'], ['kbar'])
                qinter = qin
                maskT = dmk[:, h, :]
                nch = 2
            if KSUB < 3:
                return
            btr = pbank('mx')
            for j in range(2):
                S.op('pe', lambda e, j=j: e.transpose(out=pbb[btr][:, j * 128:(j + 1) * 128], in_=kbar[:, j * 128:(j + 1) * 128], identity=idb[:, :]), reads=['kbar', 'idb'], writes=[('ps', btr)])
            if kind == 'a':
                act(kbT[:, 0, :, :], pbb[btr][:, 0:256].rearrange("p (j d) -> p j d", j=2), AF.Identity, [('ps', btr), 'vec'], ['kbT'], scale=vcol(V_MLO))
                act(kbT[:, 1, :, :], pbb[btr][:, 0:256].rearrange("p (j d) -> p j d", j=2), AF.Identity, [('ps', btr), 'vec'], ['kbT'], scale=vcol(V_MHI))
            else:
                act(kbT[:, 0, :, :], pbb[btr][:, 0:256].rearrange("p (j d) -> p j d", j=2), AF.Copy, [('ps', btr)], ['kbT'])
            if KSUB < 4:
                return
            ba = pbank('mx')
            for j in range(2):
                mm(pb[ba][:, j * 128:(j + 1) * 128], khb[:, j * 128:(j + 1) * 128], qt[:, j * 128:(j + 1) * 128], True, True, ['khb', 'qt'], [('ps', ba)])
            tt('dve', PTs[:, :], pb[ba][:, 0:TT], maskT, ALU.mult, [('ps', ba), 'masks'], ['PTs'])
            if KSUB < 5:
                return
            bd = pbank('mx')
            csz = TT // nch
            for c in range(nch):
                j = (c * csz) // 128
                half = ((c * csz) % 128) // 64 if kind == 'a' else 0
                mm(pb[bd][:, c * 128:(c + 1) * 128], kbT[:, half, j, :], vt[:, j, :], True, True, ['kbT', 'vt'], [('ps', bd)])
            Sfk = ('Sf', kind, l, h)
            Sft = Sf[(kind, l, h)]
            for c in range(nch):
                act(Sb[:, c, :], Sft[:, :], AF.Copy, [Sfk], [('Sb', c)])
                if kind == 'a':
                    stt(Sft[:, :], Sft[:, :], Eb[:, c * 64 + 63:c * 64 + 64], pb[bd][:, c * 128:(c + 1) * 128], ALU.mult, ALU.add, [Sfk, 'Eb', ('ps', bd)], [Sfk])
                else:
                    stt(Sft[:, :], Sft[:, :], float(GAMMAS[h] ** 128), pb[bd][:, c * 128:(c + 1) * 128], ALU.mult, ALU.add, [Sfk, ('ps', bd)], [Sfk])
            if KSUB < 6:
                return
            bo = pbank('mx')
            ko = ('ps', bo)
            for j in range(2):
                mm(pb[bo][:, j * 128:(j + 1) * 128], vt[:, j, :], PTs[:, j * 128:(j + 1) * 128], True, False, ['vt', 'PTs'], [ko])
                cpb = nch // 2
                for cc in range(cpb):
                    c = j * cpb + cc
                    mm(pb[bo][:, c * csz:(c + 1) * csz], Sb[:, c, :], qinter[:, c * csz:(c + 1) * csz], False, cc == cpb - 1, [('Sb', c), 'qt', 'qin'], [ko])
            act(o_sb[:, 0:TT], pb[bo][:, 0:TT], AF.Copy, [ko], ['o_sb'])
            if KSUB < 7:
                return
            if W > TT:
                sample_lin(kind, l, h, bq, bf)
            if KSUB < 8:
                return
            gcol = (V_GA if kind == 'a' else V_GC) + lh
            bn = pbank('mx')
            kn = ('ps', bn)
            if kind == 'a':
                act(sqo[:, 0:W], o_sb[:, 0:W].bitcast(F32), AF.Square, ['o_sb'], ['sqo'])
                src = o_sb
            else:
                mm(pb[bn][:, 0:TT], ones[:, 1, :], o_sb[:, 0:TT], True, W == TT, ['o_sb', 'ones'], [kn])
                if W > TT:
                    mm(pb[bn][:, TT:W], ones[:, 1, :], o_sb[:, TT:W], False, True, ['o_sb', 'ones'], [kn])
                tt('dve', cen[:, 0:W], o_sb[:, 0:W].bitcast(F32), pb[bn][:, 0:W], ALU.subtract, ['o_sb', kn], ['cen'])
                act(sqo[:, 0:W], cen[:, 0:W].bitcast(F32), AF.Square, ['cen'], ['sqo'])
                src = cen
                bn = pbank('mx')
                kn = ('ps', bn)
            mm(pb[bn][:, 0:TT], ones[:, 1, :], sqo[:, 0:TT], True, W == TT, ['sqo', 'ones'], [kn])
            if W > TT:
                mm(pb[bn][:, TT:W], ones[:, 1, :], sqo[:, TT:W], False, True, ['sqo', 'ones'], [kn])
            act(rstd[:, 0:W], pb[bn][:, 0:W], AF.Ln, [kn, 'vec'], ['rstd'], bias=vcol(V_EPS), scale=1.0)
            act(rstd[:, 0:W], rstd[:, 0:W], AF.Exp, ['rstd'], ['rstd'], scale=-0.5)
            mrow = h if kind == 'a' else 12 + h
            stt(cg[:, 0:W], src[:, 0:W].bitcast(F32), vcol(gcol), rstd[:, 0:W], ALU.mult, ALU.mult, ['o_sb', 'cen', 'rstd', 'vec'], ['sig'])
            tt('dve', mix[:, mrow, 0:W], cg[:, 0:W], gate[:, 0:W], ALU.mult, ['sig', 'gate'], [('mix', mrow)])

        def sample_lin(kind, l, h, bq, bf):
            src_st = st_h if kind == 'a' else st_r
            dst_st = o_shs if kind == 'a' else o_srs
            S.dma('pool', Ss[:, :, :], src_st[l, :, h, :, :].rearrange("b k v -> k b v"), writes=['Ss'], semkey='ss')
            if kind == 'a':
                act(fs[:, :], lf[:, TT:TT + NS], AF.Exp, ['lf'], ['fs'])
                act(ks[:, :], ka[:, TT:TT + NS], AF.Copy, ['ka'], ['ks'])
            else:
                act(ks[:, :], pb[bf][:, TT:TT + NS], AF.Identity, [('ps', bf)], ['ks'], scale=SCALE)
            bv = pbank('mx')
            for b in range(NS):
                mm(pb[bv][:, b * 128:(b + 1) * 128], sel[:, b, :], vs[:, :], True, True, ['sel', 'vs'], [('ps', bv)])
            tt('dve', tmpS[:, :, :], pb[bv][:, :].rearrange("p (b v) -> p b v", b=NS), ks[:, :].unsqueeze(2).to_broadcast([P, NS, P]), ALU.mult, [('ps', bv), 'ks'], ['tmpS'])
            if kind == 'a':
                tt('dve', Ss[:, :, :], Ss[:, :, :], fs[:, :].unsqueeze(2).to_broadcast([P, NS, P]), ALU.mult, ['Ss', 'fs'], ['Ss'])
                tt('dve', Ss[:, :, :], Ss[:, :, :], tmpS[:, :, :], ALU.add, ['Ss', 'tmpS'], ['Ss'])
            else:
                stt(Ss[:, :, :].rearrange("p b v -> p (b v)"), Ss[:, :, :].rearrange("p b v -> p (b v)"), float(GAMMAS[h]), tmpS[:, :, :].rearrange("p b v -> p (b v)"), ALU.mult, ALU.add, ['Ss', 'tmpS'], ['Ss'])
            S.dma('pool', dst_st[l, :, h, :, :].rearrange("b k v -> k b v"), Ss[:, :, :], reads=['Ss'], semkey='ssout')
            bo2 = pbank('mx')
            for b in range(NS):
                mm(pb[bo2][:, b * 4:b * 4 + 4], Ss[:, b, :], qs[:, 0:4], True, True, ['Ss', 'qs'], [('ps', bo2)])
            act(o_sb[:, TT:TT + NS], pb[bo2][:, 0:16:5], AF.Copy, [('ps', bo2)], ['o_sb'])

        def swa(t, l, W):
            for hp in range(4):
                w, wk = next_w()
                for ci in range(2):
                    hh = hp * 2 + ci
                    bq = pbank('proj')
                    proj_fm(w, wk, ci, 16, 0, hT, 'hT', bq, W)
                    act(qb[:, hh, :], pb[bq][:, 0:TT], AF.Copy, [('ps', bq)], [('qb', hh)])
                    if W > TT:
                        act(qsw[:, :, hh], pb[bq][:, TT:W], AF.Copy, [('ps', bq)], ['qsw'])
            w, wk = next_w()
            for kv in range(2):
                bk_ = pbank('proj')
                proj_fm(w, wk, kv, 16, 0, hT, 'hT', bk_, TT)
                act(kwin[l][:, kv, 128:384], pb[bk_][:, 0:TT], AF.Copy, [('ps', bk_)], [('kwin', l)])
            bkt = [pbank('proj'), pbank('proj')]
            for kv in range(2):
                proj_tm(w, wk, kv, bkt[kv], W)
            w2, wk2 = next_w()
            bvt = [pbank('proj'), pbank('proj')]
            for kv in range(2):
                proj_tm(w2, wk2, kv, bvt[kv], W)
            for kv in range(2):
                act(vwin[l][:, 1:3, kv * 128:(kv + 1) * 128], pb[bvt[kv]][:, 0:256].rearrange("p (j d) -> p j d", j=2), AF.Copy, [('ps', bvt[kv])], [('vwin', l)])
                act(kv32[:, kv * 128:(kv + 1) * 128], pb[bkt[kv]][:, 128:256], AF.Copy, [('ps', bkt[kv])], ['kv32'])
                act(kv32[:, 256 + kv * 128:256 + (kv + 1) * 128], pb[bvt[kv]][:, 128:256], AF.Copy, [('ps', bvt[kv])], ['kv32'])
                if W > TT:
                    act(kvs[:, kv * 128:(kv + 1) * 128], pb[bkt[kv]][0:4, 256:384], AF.Copy, [('ps', bkt[kv])], ['kvs'])
                    act(kvs[:, 256 + kv * 128:256 + (kv + 1) * 128], pb[bvt[kv]][0:4, 256:384], AF.Copy, [('ps', bvt[kv])], ['kvs'])
            if t == nt - 1:
                S.dma('pool', o_pk[l], kv32[:, 0:256], reads=['kv32'], semkey='kvout')
                S.dma('pool', o_pv[l], kv32[:, 256:512], reads=['kv32'], semkey='kvout')
            it = 0
            for j in range(2):
                for hh in range(8):
                    kv = hh // 4
                    r = it % 2
                    it += 1
                    bs = pbank('mx')
                    mm(pb[bs][:, 0:256], qb[:, hh, j * 128:(j + 1) * 128], kwin[l][:, kv, j * 128:j * 128 + 256], True, True, [('qb', hh), ('kwin', l)], [('ps', bs)])
                    brow = hh + (8 if (t == 0 and j == 0) else 0)
                    stt(s_sc[r][:, :], pb[bs][:, 0:256], SCALE, bia[:, brow, :], ALU.mult, ALU.add, [('ps', bs), 'bia'], [['Eb', 'Ei'][r]])
                    S.op('dve', lambda e, r=r: e.tensor_reduce(out=sm[r][:, 0:1], in_=s_sc[r][:, :], axis=AX.X, op=ALU.max), reads=[['Eb', 'Ei'][r]], writes=[('sm', r)])
                    ts(sm[r][:, 1:2], sm[r][:, 0:1], sinkt[:, l * 8 + hh:l * 8 + hh + 1], -1.0, ALU.max, ALU.mult, [('sm', r), 'sinkt'], [('sm', r)])
                    act(p_sc[r][:, :], s_sc[r][:, :], AF.Exp, [['Eb', 'Ei'][r], ('sm', r)], [['kh', 'bb'][r]], bias=sm[r][:, 1:2], scale=1.0)
                    S.op('dve', lambda e, r=r: e.tensor_reduce(out=sm[r][:, 2:3], in_=p_sc[r][:, :], axis=AX.X, op=ALU.add), reads=[['kh', 'bb'][r]], writes=[('sm', r)])
                    act(sm[r][:, 3:4], sinkt[:, l * 8 + hh:l * 8 + hh + 1], AF.Exp, ['sinkt', ('sm', r)], [('sm', r)], bias=sm[r][:, 1:2], scale=1.0)
                    tt('dve', sm[r][:, 4:5], sm[r][:, 2:3], sm[r][:, 3:4], ALU.add, [('sm', r)], [('sm', r)])
                    S.op('dve', lambda e, r=r: e.reciprocal(out=sm[r][:, 5:6], in_=sm[r][:, 4:5]), reads=[('sm', r)], writes=[('sm', r)])
                    ts(pn_sc[r][:, :], p_sc[r][:, :], sm[r][:, 5:6], None, ALU.mult, None, [['kh', 'bb'][r], ('sm', r)], [('pn_sc', r)])
                    bt_ = pbank('mx')
                    for half in range(2):
                        S.op('pe', lambda e, half=half, r=r, bt_=bt_: e.transpose(out=pbb[bt_][:, half * 128:(half + 1) * 128], in_=pn_sc[r][:, half * 128:(half + 1) * 128], identity=idb[:, :]), reads=[('pn_sc', r), 'idb'], writes=[('ps', bt_)])
                    act(pT_sc[r][:, :], pbb[bt_][:, 0:256], AF.Copy, [('ps', bt_)], [('pT_sc', r)])
                    bo = pbank('mx')
                    mm(pb[bo][:, 0:128], vwin[l][:, j, kv * 128:(kv + 1) * 128], pT_sc[r][:, 0:128], True, False, [('vwin', l), ('pT_sc', r)], [('ps', bo)])
                    mm(pb[bo][:, 0:128], vwin[l][:, j + 1, kv * 128:(kv + 1) * 128], pT_sc[r][:, 128:256], False, True, [('vwin', l), ('pT_sc', r)], [('ps', bo)])
                    act(mix[:, 4 + hh, j * 128:(j + 1) * 128], pb[bo][:, 0:128], AF.Copy, [('ps', bo)], [('mix', 4 + hh)])
            if W > TT:
                sample_swa(l)
            act(kwin[l][:, :, 0:128], kwin[l][:, :, 256:384], AF.Copy, [('kwin', l)], [('kwin', l)])
            act(vwin[l][:, 0, :], vwin[l][:, 2, :], AF.Copy, [('vwin', l)], [('vwin', l)])

        def sample_swa(l):
            for b in range(NS):
                S.dma('pool', Kw[0:127, :], ck[l, b, 1:128, :], writes=['Kw'], semkey='kvw')
                S.dma('pool', Vw[0:127, :], cv[l, b, 1:128, :], writes=['Vw'], semkey='kvw')
                S.dma('pool', Kw[127:128, :], kvs[b:b + 1, 0:256], reads=['kvs'], writes=['Kw'], semkey='kvw')
                S.dma('pool', Vw[127:128, :], kvs[b:b + 1, 256:512], reads=['kvs'], writes=['Vw'], semkey='kvw')
                S.settle('kvw', ['Kw', 'Vw'])
                S.dma('pool', o_sk[l, b], Kw[:, :], reads=['Kw'], semkey='kvwout')
                S.dma('pool', o_sv[l, b], Vw[:, :], reads=['Vw'], semkey='kvwout')
                bt_ = pbank('mx')
                for kv in range(2):
                    S.op('pe', lambda e, kv=kv, bt_=bt_: e.transpose(out=pb[bt_][:, kv * 128:(kv + 1) * 128], in_=Kw[:, kv * 128:(kv + 1) * 128], identity=idf[:, :]), reads=['Kw', 'idf'], writes=[('ps', bt_)])
                act(KwT[:, :, :], pb[bt_][:, 0:256].rearrange("p (i j) -> p i j", i=2), AF.Copy, [('ps', bt_)], ['KwT'])
                bs2 = pbank('mx')
                for kv in range(2):
                    mm(pb[bs2][0:4, kv * 128:(kv + 1) * 128], qsw[:, b, kv * 4:kv * 4 + 4], KwT[:, kv, :], True, True, ['qsw', 'KwT'], [('ps', bs2)])
                stt(ssm[:, :, :].rearrange("g i j -> g (i j)"), pb[bs2][0:4, 0:256], SCALE, bis[:, 0:2, :].rearrange("g i j -> g (i j)"), ALU.mult, ALU.add, [('ps', bs2), 'bis'], ['ssm'])
                sk4 = sink4[:, l, :]
                S.op('dve', lambda e: e.tensor_reduce(out=st4[:, 0, 0:2], in_=ssm[:, :, :], axis=AX.X, op=ALU.max), reads=['ssm'], writes=['st4'])
                tt('dve', st4[:, 1, 0:2], st4[:, 0, 0:2], sk4, ALU.max, ['st4', 'sink4'], ['st4'])
                tt('dve', ssm[:, :, :], ssm[:, :, :], st4[:, 1, 0:2].unsqueeze(2).to_broadcast([4, 2, P]), ALU.subtract, ['ssm', 'st4'], ['ssm'])
                act(psm[0:4, :, :], ssm[:, :, :], AF.Exp, ['ssm'], ['psm'])
                S.op('dve', lambda e: e.tensor_reduce(out=st4[:, 2, 0:2], in_=psm[0:4, :, :], axis=AX.X, op=ALU.add), reads=['psm'], writes=['st4'])
                tt('dve', st4[:, 3, 0:2], sk4, st4[:, 1, 0:2], ALU.subtract, ['st4', 'sink4'], ['st4'])
                act(st4[:, 3, 0:2], st4[:, 3, 0:2], AF.Exp, ['st4'], ['st4'])
                tt('dve', st4[:, 4, 0:2], st4[:, 2, 0:2], st4[:, 3, 0:2], ALU.add, ['st4'], ['st4'])
                S.op('dve', lambda e: e.reciprocal(out=st4[:, 5, 0:2], in_=st4[:, 4, 0:2]), reads=['st4'], writes=['st4'])
                tt('dve', psm[0:4, :, :], psm[0:4, :, :], st4[:, 5, 0:2].unsqueeze(2).to_broadcast([4, 2, P]), ALU.mult, ['psm', 'st4'], ['psm'])
                bt2 = pbank('mx')
                for kv in range(2):
                    S.op('pe', lambda e, kv=kv, bt2=bt2: e.transpose(out=pb[bt2][:, kv * 128:(kv + 1) * 128], in_=psm[:, kv, :], identity=idf[:, :]), reads=['psm', 'idf'], writes=[('ps', bt2)])
                act(pnT[:, 0:8].rearrange("p (k g) -> p k g", k=2), pb[bt2][:, 0:256].rearrange("p (k j) -> p k j", k=2)[:, :, 0:4], AF.Copy, [('ps', bt2)], ['pnT'])
                bo = pbank('mx')
                for kv in range(2):
                    mm(pb[bo][:, kv * 4:kv * 4 + 4], Vw[:, kv * 128:(kv + 1) * 128], pnT[:, kv * 4:kv * 4 + 4], True, True, ['Vw', 'pnT'], [('ps', bo)])
                act(mix[:, 4:12, TT + b], pb[bo][:, 0:8], AF.Copy, [('ps', bo)], [('mix', 4 + i) for i in range(8)])

        def conv_chunk(l, uc, bank, W, dst, r, t, dk):
            bk = ('ps', bank)
            u = ug[r]
            uk = ('ug', r)
            act(u[:, 2:2 + TT], pb[bank][:, 0:TT], AF.Copy, [bk], [uk])
            S.op('pool', lambda e: e.tensor_copy(out=u[:, 0:2], in_=upre[l][:, uc, :]), reads=[('upre', l, uc)], writes=[uk])
            cwc = V_CW + (l * 3) * 88 + uc
            act(dst[:, 0:W], pb[bank][:, 0:W], AF.Identity, [bk, 'vec'], [dk], scale=vcol(cwc + 2 * 88), bias=vcol(V_CB + l * 88 + uc))
            stt(dst[:, 0:TT], u[:, 1:1 + TT], vcol(cwc + 88), dst[:, 0:TT], ALU.mult, ALU.add, [uk, dk, 'vec'], [dk])
            stt(dst[:, 0:TT], u[:, 0:TT], vcol(cwc), dst[:, 0:TT], ALU.mult, ALU.add, [uk, dk, 'vec'], [dk])
            S.op('pool', lambda e: e.tensor_copy(out=upre[l][:, uc, :], in_=u[:, TT:TT + 2]), reads=[uk], writes=[('upre', l, uc)])
            if W > TT:
                stt(dst[:, TT:W], scp[l][:, uc, :, 1], vcol(cwc + 88), dst[:, TT:W], ALU.mult, ALU.add, ['scp', dk, 'vec'], [dk])
                stt(dst[:, TT:W], scp[l][:, uc, :, 0], vcol(cwc), dst[:, TT:W], ALU.mult, ALU.add, ['scp', dk, 'vec'], [dk])
                S.op('pool', lambda e: e.tensor_copy(out=sco[l][:, uc, :, 0], in_=scp[l][:, uc, :, 1]), reads=['scp'], writes=[('sco', l)])
                act(sco[l][:, uc, :, 1], pb[bank][:, TT:W], AF.Copy, [bk], [('sco', l)])

        def down_half(nk, l, W):
            for ocp in range(8):
                by = [pbank('proj'), pbank('proj')]
                wa, wka = next_w()
                for ci in range(2):
                    proj_fm(wa, wka, ci, nk, 0, m_sb, (lambda k: ('m', k)), by[ci], W)
                for ci in range(2):
                    oc = ocp * 2 + ci
                    tt('dve', x_sb[:, oc, 0:W], x_sb[:, oc, 0:W], pb[by[ci]][:, 0:W], ALU.add, ['x', ('ps', by[ci])], ['x'])

        def ffn(t, l, W):
            rmsnorm(V_G2 + l * 16, hT, 'hT', W)
            for jp in range(NJ // 2):
                grp = 'proj' if jp % 2 == 0 else 'mx'
                wg, wgk = next_w()
                bg = [pbank(grp), pbank(grp)]
                for ci in range(2):
                    proj_fm(wg, wgk, ci, 16, 0, hT, 'hT', bg[ci], W)
                wv, wvk = next_w()
                bv = [pbank(grp), pbank(grp)]
                for ci in range(2):
                    proj_fm(wv, wvk, ci, 16, 0, hT, 'hT', bv[ci], W)
                for ci in range(2):
                    j = jp * 2 + ci
                    conv_chunk(l, j, bg[ci], W, cg, 0, t, 'sig')
                    conv_chunk(l, NJ + j, bv[ci], W, cvv, 1, t, 'lf')
                    act(tg[:, 0:W], cg[:, 0:W], AF.Square, ['sig'], ['ka'])
                    ts(tg[:, 0:W], tg[:, 0:W], 0.044715, 1.0, ALU.mult, ALU.add, ['ka'], ['ka'])
                    tt('dve', tg[:, 0:W], tg[:, 0:W], cg[:, 0:W], ALU.mult, ['ka', 'sig'], ['ka'])
                    act(sg[:, 0:W], tg[:, 0:W], AF.Sigmoid, ['ka'], ['gate'], scale=1.5957691216057308)
                    tt('dve', sg[:, 0:W], sg[:, 0:W], cg[:, 0:W], ALU.mult, ['gate', 'sig'], ['gate'])
                    mi = j - 2 * [p0 for (p0, p1) in MGRP if p0 <= jp < p1][0]
                    tt('dve', m_sb[:, mi, 0:W], sg[:, 0:W], cvv[:, 0:W], ALU.mult, ['gate', 'lf'], [('m', mi)])
                for (p0, p1) in MGRP:
                    if jp == p1 - 1:
                        down_half(2 * (p1 - p0), l, W)
            if t == nt - 1:
                S.dma('pool', o_pc[l], upre[l][:, :, :], reads=[('upre', l, uc) for uc in range(88)], semkey='out')
            if W > TT:
                S.dma('pool', o_sc[l], sco[l][:, :, :, :], reads=[('sco', l)], semkey='out')

        def layer(t, l, W):
            if KSTOP < 1:
                return
            rmsnorm(V_G1 + l * 16, hT, 'hT', W)
            if KSTOP < 2:
                return
            for h in range(4):
                lin_head('a', t, l, h, W)
                if KSTOP < 3:
                    return
            if KSTOP < 4:
                return
            swa(t, l, W)
            if KSTOP < 5:
                return
            for h in range(4):
                lin_head('r', t, l, h, W)
            if KSTOP < 6:
                return
            mixkeys = [('mix', i) for i in range(16)]
            for ocp in range(8):
                w, wk = next_w()
                for ci in range(2):
                    oc = ocp * 2 + ci
                    by = pbank('proj')
                    proj_fm(w, wk, ci, 16, 0, mix, (lambda k: ('mix', k)), by, W)
                    tt('dve', x_sb[:, oc, 0:W], x_sb[:, oc, 0:W], pb[by][:, 0:W], ALU.add, ['x', ('ps', by)], ['x'])
            if KSTOP < 7:
                return
            ffn(t, l, W)

        plan_weights()
        for dst, src, key in [(vec, vecs, 'vec'), (sinkt, sinkb, 'sinkt'), (sink4, sinks4, 'sink4'), (cmk, cmask, 'masks'), (dmk, dmask, 'masks'),
                              (rwc, rowc, 'rwc'), (bia, biasd, 'bia'), (bis, biass, 'bis'), (sel, seld, 'sel'), (idb, identb, 'idb'),
                              (idf, identf, 'idf'), (ones, onesd, 'ones')]:
            S.dma('pool', dst[:], src, writes=[key], semkey='const')
        for l in range(2):
            S.dma('pool', scp[l][:, :, :, :], scT[l], writes=['scp'], semkey='const')
        S.settle('const', ['vec', 'sinkt', 'sink4', 'masks', 'rwc', 'bia', 'bis', 'sel', 'idb', 'idf', 'ones', 'scp'])
        tt('dve', vec[:, V_LB + 4:V_LB + 8], vec[:, V_RAW + 4:V_RAW + 8], vec[:, V_RAW:V_RAW + 4], ALU.subtract, ['vec'], ['vec'])
        act(vec[:, V_LB + 4:V_LB + 8], vec[:, V_LB + 4:V_LB + 8], AF.Sigmoid, ['vec'], ['vec'])
        ts(vec[:, V_OML:V_OML + 8], vec[:, V_LB:V_LB + 8], -1.0, 1.0, ALU.mult, ALU.add, ['vec'], ['vec'])
        ts(vec[:, V_LBM1:V_LBM1 + 8], vec[:, V_LB:V_LB + 8], -1.0, 0.0, ALU.add, ALU.add, ['vec'], ['vec'])
        for l in range(2):
            if l == 0:
                S.op('pool', lambda e: e.memset(vs[:, :], 0.0), writes=['vs'])
            S.op('pool', lambda e, l=l: e.memset(upre[l][:, :, :], 0.0), writes=[('upre', l, uc) for uc in range(88)])
            S.op('pool', lambda e, l=l: e.memset(kwin[l][:, :, :], 0.0), writes=[('kwin', l)])
            S.op('pool', lambda e, l=l: e.memset(vwin[l][:, :, :], 0.0), writes=[('vwin', l)])
            for k in 'ar':
                for h in range(4):
                    S.op('pool', lambda e, k=k, l=l, h=h: e.memset(Sf[(k, l, h)][:, :], 0.0), writes=[('Sf', k, l, h)])

        for t in range(nt):
            W = TT + NS if t == 0 else TT
            S.dma('sp', x_sb[:, :, 0:TT], xT[:, :, t * TT:(t + 1) * TT].rearrange("c p n -> p c n"), writes=['x'], reads=[], semkey='xload')
            if t == 0:
                S.dma('sp', x_sb[:, :, TT:W], xT[:, :, T:T + NS].rearrange("c p n -> p c n"), writes=['x'], semkey='xload')
            for l in range(2):
                layer(t, l, W)
            rmsnorm(V_GF, x_sb, 'x', W)
            S.dma('pool', yT[:, :, t * TT:(t + 1) * TT].rearrange("c p n -> p c n"), x_sb[:, :, 0:TT], reads=['x'], semkey='yout')
            if t == 0:
                S.dma('pool', yT[:, :, T:T + NS].rearrange("c p n -> p c n"), x_sb[:, :, TT:W], reads=['x'], semkey='yout')
        for l in range(2):
            for h in range(4):
                S.dma('pool', o_hs[l, h], Sf[('a', l, h)][:, :], reads=[('Sf', 'a', l, h)], semkey='out')
                S.dma('pool', o_rs[l, h], Sf[('r', l, h)][:, :], reads=[('Sf', 'r', l, h)], semkey='out')
        assert KSTOP < 99 or wstate['used'] == len(wq), (wstate, len(wq))
        S.finish('sp')
        print("ops", S.nops, {k: v for k, v in S.cnt.items()})
    return nc


_CACHE = {}


def _consts():
    import ml_dtypes
    bf = ml_dtypes.bfloat16
    c = {}
    s = np.arange(128)[:, None]
    tcol = np.arange(256)[None, :]
    tl = tcol % 128
    c['cmask'] = (((s // 64) == (tl // 64)) & (s <= tl)).astype(np.float32).astype(bf)
    dm = np.zeros((128, 4, 256), np.float32)
    rowc = np.zeros((128, 9, 256), np.float32)
    rowc[:, 0, :] = 1.0
    rowc[:, 0, ::64] = 0.0
    for h in range(4):
        g = np.float64(GAMMAS[h])
        dm[:, h, :] = np.where(tl >= s, g ** np.maximum(tl - s, 0) * SCALE, 0.0)
        rowc[:, 1 + h, :] = g ** (tl + 1.0)
        rowc[:, 5 + h, :] = g ** (127.0 - tl) * SCALE
    c['dmask'] = dm.astype(bf)
    c['rowc'] = rowc
    qi = np.arange(128)[:, None]
    sj = np.arange(256)[None, :]
    dist = 128 + qi - sj
    valid = (dist >= 0) & (dist < 128)
    bd = np.zeros((128, 16, 256), np.float32)
    for h in range(8):
        bd[:, h, :] = np.where(valid, -SLOPES[h] * dist, -1e30)
        bd[:, 8 + h, :] = np.where(valid & (sj >= 128), -SLOPES[h] * dist, -1e30)
    c['biasd'] = bd.astype(bf)
    bs = np.zeros((4, 2, 128), np.float32)
    for g in range(4):
        for kv in range(2):
            bs[g, kv, :] = -SLOPES[kv * 4 + g] * (127.0 - np.arange(128))
    c['biass'] = bs
    sd = np.zeros((128, 4, 128), np.float32)
    for b in range(4):
        sd[b, b, :] = 1.0
    c['seld'] = sd
    c['identb'] = np.eye(128, dtype=np.float32).astype(bf)
    c['identf'] = np.eye(128, dtype=np.float32)
    on = np.zeros((128, 2, 128), np.float32)
    on[:, 0, :] = 1.0 / 2048.0
    on[:, 1, :] = 1.0 / 128.0
    c['onesd'] = on
    return c


def kernel(x_prompt, x_sample, state_hgrn, cache_swa_k, cache_swa_v, state_ret, state_conv,
           norm1_g, w_in, hgrn_lb_raw, hgrn_norm_g, swa_sinks, ret_norm_g, w_out,
           norm2_g, w_up, conv_w, conv_b, w_down, final_norm_g):
    f = lambda a: np.ascontiguousarray(np.asarray(a, dtype=np.float32))
    x_prompt, x_sample = f(x_prompt), f(x_sample)
    state_hgrn, state_ret = f(state_hgrn), f(state_ret)
    cache_swa_k, cache_swa_v, state_conv = f(cache_swa_k), f(cache_swa_v), f(state_conv)
    w_in, w_out, w_up, w_down = f(w_in), f(w_out), f(w_up), f(w_down)
    if 'nc' not in _CACHE:
        _CACHE['nc'] = build_program()
        _CACHE['c'] = _consts()
    nc = _CACHE['nc']
    cst = _CACHE['c']

    raise_if = None
    in_maps = []
    n_cores = 8
    vecs = np.zeros((128, VW), np.float32)
    n1 = f(norm1_g).reshape(2, 16, 128); n2 = f(norm2_g).reshape(2, 16, 128); nf = f(final_norm_g).reshape(16, 128)
    for l in range(2):
        vecs[:, 0 + l * 16:0 + (l + 1) * 16] = n1[l].T
        vecs[:, 32 + l * 16:32 + (l + 1) * 16] = n2[l].T
    vecs[:, 64:80] = nf.T
    ga = f(hgrn_norm_g).reshape(2, 4, 128); gc = f(ret_norm_g).reshape(2, 4, 128)
    for l in range(2):
        vecs[:, 104 + l * 4:104 + (l + 1) * 4] = ga[l].T
        vecs[:, 112 + l * 4:112 + (l + 1) * 4] = gc[l].T
    cw = f(conv_w).reshape(2, 3, 88, 128); cb = f(conv_b).reshape(2, 88, 128)
    for l in range(2):
        for tap in range(3):
            vecs[:, V_CW + (l * 3 + tap) * 88:V_CW + (l * 3 + tap + 1) * 88] = cw[l, tap].T
        vecs[:, V_CB + l * 88:V_CB + (l + 1) * 88] = cb[l].T
    vecs[:, V_EPS] = EPS
    vecs[0:64, V_MLO] = 1.0
    vecs[64:128, V_MHI] = 1.0
    lbraw = f(hgrn_lb_raw).reshape(2, 4, 128)
    vecs[:, V_RAW:V_RAW + 4] = lbraw[0].T
    vecs[:, V_RAW + 4:V_RAW + 8] = lbraw[1].T
    sk = f(swa_sinks)
    sinkb = np.broadcast_to(sk.reshape(1, 16), (128, 16)).copy()
    sinks4 = np.ascontiguousarray(sk.reshape(2, 2, 4).transpose(2, 0, 1))
    for c in range(n_cores):
        b = c % 4
        xT = np.empty((16, 128, T + NS), np.float32)
        xT[:, :, :T] = x_prompt[b].T.reshape(16, 128, T)
        xT[:, :, T:] = x_sample[c * 4:(c + 1) * 4, 0, :].T.reshape(16, 128, NS)
        m = {
            'xT': xT,
            'st_h': np.ascontiguousarray(state_hgrn[:, c * 4:(c + 1) * 4]),
            'st_r': np.ascontiguousarray(state_ret[:, c * 4:(c + 1) * 4]),
            'ck': np.ascontiguousarray(cache_swa_k[:, c * 4:(c + 1) * 4].reshape(2, 4, 128, 256)),
            'cv': np.ascontiguousarray(cache_swa_v[:, c * 4:(c + 1) * 4].reshape(2, 4, 128, 256)),
            'scT': np.ascontiguousarray(state_conv[:, c * 4:(c + 1) * 4].reshape(2, 4, 2, 88, 128).transpose(0, 4, 3, 1, 2)),
            'w_in': w_in, 'w_out': w_out, 'w_up': w_up, 'w_down': w_down,
            'vecs': vecs, 'sinkb': sinkb, 'sinks4': sinks4,
        }
        m.update(cst)
        in_maps.append(m)
    if _CACHE.get('maps_only'):
        return in_maps
    res = run_bass_kernel_spmd(nc, in_maps, core_ids=list(range(n_cores)))
    R = res.results
    y_prompt = np.stack([R[b]['yT'][:, :, :T].reshape(D, T).T for b in range(4)])
    y_sample = np.concatenate([R[c]['yT'][:, :, T:].reshape(D, NS).T for c in range(8)])[:, None, :]
    p_hgrn = np.stack([R[b]['o_hs'] for b in range(4)], axis=1)
    p_ret = np.stack([R[b]['o_rs'] for b in range(4)], axis=1)
    p_k = np.stack([R[b]['o_pk'].reshape(2, 128, 2, 128) for b in range(4)], axis=1)
    p_v = np.stack([R[b]['o_pv'].reshape(2, 128, 2, 128) for b in range(4)], axis=1)
    p_conv = np.stack([R[b]['o_pc'].transpose(0, 3, 2, 1).reshape(2, 2, 11264) for b in range(4)], axis=1)
    s_hgrn = np.concatenate([R[c]['o_shs'] for c in range(8)], axis=1)
    s_ret = np.concatenate([R[c]['o_srs'] for c in range(8)], axis=1)
    s_k = np.concatenate([R[c]['o_sk'].reshape(2, 4, 128, 2, 128) for c in range(8)], axis=1)
    s_v = np.concatenate([R[c]['o_sv'].reshape(2, 4, 128, 2, 128) for c in range(8)], axis=1)
    s_conv = np.concatenate([R[c]['o_sc'].transpose(0, 3, 4, 2, 1).reshape(2, 4, 2, 11264) for c in range(8)], axis=1)
    out = (y_prompt, y_sample, p_hgrn, p_k, p_v, p_ret, p_conv, s_hgrn, s_k, s_v, s_ret, s_conv)
    return tuple(np.ascontiguousarray(o, dtype=np.float32) for o in out)
```

```python
import os
import numpy as np
from contextlib import ExitStack
import concourse.bass as bass
import concourse.mybir as mybir
from concourse.bass_utils import run_bass_kernel_spmd

F32 = mybir.dt.float32
F32R = mybir.dt.float32r
BF16 = mybir.dt.bfloat16
AF = mybir.ActivationFunctionType
ALU = mybir.AluOpType
AX = mybir.AxisListType

P = 128
D = 2048
NKC = 16
T = 2048
TT = 256
NT = T // TT
NS = 4
DFF = 5632
NJ = DFF // P
NWT = 8 + 4 + 2 + 8 + 8 + 44 + 32
MGRP = [(0, 6), (6, 12), (12, 18), (18, 22)]
INC = 5632
EPS = 1e-6
SCALE = 128.0 ** -0.5
GAMMAS = [float(1.0 - 2.0 ** (-5.0 - h)) for h in range(4)]
SLOPES = [float(2.0 ** (-(h + 1.0))) for h in range(8)]
VW = 1024
V_G1, V_G2, V_GF = 0, 32, 64
V_LB, V_OML, V_LBM1 = 80, 88, 96
V_GA, V_GC = 104, 112
V_CW = 120
V_CB = V_CW + 2 * 3 * 88
V_EPS = V_CB + 2 * 88
V_RAW = V_EPS + 1
V_MLO = V_RAW + 8
V_MHI = V_MLO + 1
C_QA, C_FA, C_IA, C_GA, C_QB, C_KB, C_VB, C_QC, C_KC, C_VC, C_GC = 0, 512, 1024, 1536, 2048, 3072, 3328, 3584, 4096, 4608, 5120


class Sched:
    def __init__(self, nc, sems, dma_sems):
        self.nc = nc
        self.eng = {'pe': nc.tensor, 'act': nc.scalar, 'dve': nc.vector, 'pool': nc.gpsimd, 'sp': nc.sync}
        self.sem = sems
        self.cnt = {e: 0 for e in self.eng}
        self.dma_sems = list(dma_sems)
        self.dma_cnt = {}
        self.dma_tot = {}
        self.key_sem = {}
        self.waited = {e: {} for e in self.eng}
        self.lastw = {}
        self.readers = {}
        self.nops = 0

    def _need(self, e, rec, waits):
        semname, sem, val, src = rec
        if src == e and e == 'pe':
            return
        if src == 'dma':
            val = self.dma_cnt[semname[4:]] if False else self.dma_tot[semname]
        if self.waited[e].get(semname, 0) >= val:
            return
        prev = waits.get(semname)
        if prev is None or prev[1] < val:
            waits[semname] = (sem, val)

    def _deps(self, e, reads, writes):
        waits = {}
        for k in reads:
            w = self.lastw.get(k)
            if w is not None:
                self._need(e, w, waits)
        for k in writes:
            w = self.lastw.get(k)
            if w is not None and not (w[3] == e):
                self._need(e, w, waits)
            for r in self.readers.get(k, ()):
                if r[3] == e:
                    continue
                self._need(e, r, waits)
        for semname, (sem, val) in waits.items():
            self.eng[e].wait_ge(sem, val)
            self.waited[e][semname] = val

    def _record(self, rec, reads, writes):
        for k in reads:
            self.readers.setdefault(k, []).append(rec)
        for k in writes:
            self.lastw[k] = rec
            self.readers[k] = []
        self.nops += 1

    def op(self, e, fn, reads=(), writes=()):
        self._deps(e, reads, writes)
        ins = fn(self.eng[e])
        self.cnt[e] += 1
        ins.then_inc(self.sem[e], 1)
        self._record((e, self.sem[e], self.cnt[e], e), reads, writes)
        return ins

    def dma(self, e, out, in_, reads=(), writes=(), semkey=None, nowaw=False):
        self._deps(e, reads, () if nowaw else writes)
        if semkey is None:
            semkey = 'misc'
        if semkey not in self.key_sem:
            self.key_sem[semkey] = self.dma_sems.pop()
            self.dma_cnt[semkey] = 0
        sem = self.key_sem[semkey]
        ins = self.eng[e].dma_start(out=out, in_=in_)
        self.dma_cnt[semkey] += 16
        self.dma_tot['dma:%s' % (semkey,)] = self.dma_cnt[semkey]
        ins.then_inc(sem, 16)
        self._record(('dma:%s' % (semkey,), sem, self.dma_cnt[semkey], 'dma'), reads, writes)
        return ins

    def settle(self, semkey, keys):
        sem = self.key_sem[semkey]
        for k in keys:
            self.lastw[k] = ('dma:%s' % (semkey,), sem, self.dma_cnt[semkey], 'dma')

    def finish(self, e='sp'):
        for k, sem in self.key_sem.items():
            self.eng[e].wait_ge(sem, self.dma_cnt[k])
        for n, sem in self.sem.items():
            if self.cnt[n] > 0:
                self.eng[e].wait_ge(sem, self.cnt[n])


def build_program(nt=NT):
    nc = bass.Bass("TRN2", target_bir_lowering=False)
    nc.dge_precook = False

    def din(name, shape, dt=F32):
        return nc.dram_tensor(name, list(shape), dt, kind="ExternalInput").ap()

    def dout(name, shape, dt=F32):
        return nc.dram_tensor(name, list(shape), dt, kind="ExternalOutput").ap()

    xT = din("xT", [NKC, P, T + NS])
    st_h = din("st_h", [2, NS, 4, P, P])
    st_r = din("st_r", [2, NS, 4, P, P])
    ck = din("ck", [2, NS, P, 256])
    cv = din("cv", [2, NS, P, 256])
    scT = din("scT", [2, P, 88, NS, 2])
    w_in = din("w_in", [2, D, INC])
    w_out = din("w_out", [2, D, D])
    w_up = din("w_up", [2, D, 2 * DFF])
    w_down = din("w_down", [2, DFF, D])
    wscr = nc.dram_tensor("wscr", [2 * NWT, P, 4096], BF16).ap()
    vecs = din("vecs", [P, VW])
    sinkb = din("sinkb", [P, 16])
    sinks4 = din("sinks4", [4, 2, 2])
    cmask = din("cmask", [P, 256], BF16)
    dmask = din("dmask", [P, 4, 256], BF16)
    rowc = din("rowc", [P, 9, 256])
    biasd = din("biasd", [P, 16, 256], BF16)
    biass = din("biass", [4, 2, 128])
    seld = din("seld", [P, 4, P])
    identb = din("identb", [P, P], BF16)
    identf = din("identf", [P, P])
    onesd = din("onesd", [P, 2, P], F32R)

    yT = dout("yT", [NKC, P, T + NS])
    o_hs = dout("o_hs", [2, 4, P, P])
    o_rs = dout("o_rs", [2, 4, P, P])
    o_pk = dout("o_pk", [2, P, 256])
    o_pv = dout("o_pv", [2, P, 256])
    o_pc = dout("o_pc", [2, P, 88, 2])
    o_shs = dout("o_shs", [2, NS, 4, P, P])
    o_srs = dout("o_srs", [2, NS, 4, P, P])
    o_sk = dout("o_sk", [2, NS, P, 256])
    o_sv = dout("o_sv", [2, NS, P, 256])
    o_sc = dout("o_sc", [2, P, 88, NS, 2])


    with ExitStack() as es:
        def sbuf(name, shape, dt=F32):
            return es.enter_context(nc.sbuf_tensor(name, list(shape), dt))

        def psum(name, shape, dt=F32):
            return es.enter_context(nc.psum_tensor(name, list(shape), dt))

        WT = TT + NS
        xbufs = [sbuf("x_sb0", [P, NKC, WT]), sbuf("x_sb1", [P, NKC, WT])]
        X = {"b": xbufs[0], "k": "x0"}
        hT = sbuf("hT", [P, NKC, WT], BF16)
        mix = sbuf("mix", [P, NKC, WT], BF16)
        m_sb = sbuf("m_sb", [P, 12, WT], BF16)
        NWS = 3
        wsl = [sbuf("wsl%d" % i, [P, NKC * 256], BF16) for i in range(NWS)]
        wst = [sbuf("wst%d" % i, [P, NKC * 256]) for i in range(2)]
        vec = sbuf("vec", [P, VW])
        sinkt = sbuf("sinkt", [P, 16])
        sink4 = sbuf("sink4", [4, 2, 2])
        cmk = sbuf("cmk", [P, 256], BF16)
        dmk = sbuf("dmk", [P, 4, 256], BF16)
        rwc = sbuf("rwc", [P, 9, 256])
        bia = sbuf("bia", [P, 16, 256], BF16)
        bis = sbuf("bis", [4, 2, 128])
        sel = sbuf("sel", [P, 4, P])
        idb = sbuf("idb", [P, P], BF16)
        idf = sbuf("idf", [P, P])
        ones = sbuf("ones", [P, 2, P], F32R)
        Sf = {(k, l, h): sbuf("Sf%s%d%d" % (k, l, h), [P, P]) for k in 'ar' for l in range(2) for h in range(4)}
        kwin = [sbuf("kwin%d" % l, [P, 2, 384], BF16) for l in range(2)]
        vwin = [sbuf("vwin%d" % l, [P, 3, 256], BF16) for l in range(2)]
        upre = [sbuf("upre%d" % l, [P, 88, 2]) for l in range(2)]
        sig = sbuf("sig", [P, WT]); lf = sbuf("lf", [P, WT]); ka = sbuf("ka", [P, WT])
        bb = sbuf("bb", [P, TT]); Eb = sbuf("Eb", [P, TT]); Ei = sbuf("Ei", [P, TT]); kh = sbuf("kh", [P, TT])
        qt = sbuf("qt", [P, TT], BF16); qin = sbuf("qin", [P, TT], BF16); khb = sbuf("khb", [P, TT], BF16)
        kbar = sbuf("kbar", [P, TT], BF16); kbT = sbuf("kbT", [P, 2, 2, P], BF16); vt = sbuf("vt", [P, 2, P], BF16)
        PTs = sbuf("PTs", [P, TT], BF16); Sb = sbuf("Sb", [P, 4, P], BF16)
        o_sb = sbuf("o_sb", [P, WT], F32R); cen = sbuf("cen", [P, WT], F32R); sqo = sbuf("sqo", [P, WT], F32R)
        rstd = sbuf("rstd", [P, WT]); gate = sbuf("gate", [P, WT])
        sqx = [sbuf("sqx%d" % i, [P, WT], F32R) for i in range(2)]
        qb = sbuf("qb", [P, 8, TT], BF16)
        kv32 = sbuf("kv32", [P, 512])
        s_sc = [Eb, Ei]
        p_sc = [kh, bb]
        pn_sc = [sbuf("pn_sc%d" % i, [P, 256], BF16) for i in range(2)]
        pT_sc = [sbuf("pT_sc%d" % i, [P, 256], BF16) for i in range(2)]
        sm = [sbuf("sm%d" % i, [P, 8]) for i in range(2)]
        ug = [sbuf("ug%d" % i, [P, TT + 2]) for i in range(2)]
        cg, cvv, tg, sg = sig, lf, ka, gate
        Ss = sbuf("Ss", [P, NS, P]); tmpS = sbuf("tmpS", [P, NS, P])
        vs = sbuf("vs", [P, P]); qs = sbuf("qs", [P, NS]); ks = sbuf("ks", [P, NS]); fs = sbuf("fs", [P, NS])
        Kw = sbuf("Kw", [P, 256]); Vw = sbuf("Vw", [P, 256]); KwT = sbuf("KwT", [P, 2, P])
        kvs = sbuf("kvs", [4, 512]); qsw = sbuf("qsw", [P, NS, 8])
        ssm = sbuf("ssm", [4, 2, P]); psm = sbuf("psm", [P, 2, P]); st4 = sbuf("st4", [4, 6, 8])
        pnT = sbuf("pnT", [P, 32])
        scp = [sbuf("scp%d" % l, [P, 88, NS, 2]) for l in range(2)]
        sco = [sbuf("sco%d" % l, [P, 88, NS, 2]) for l in range(2)]
        pb = [psum("pb%d" % i, [P, 512]) for i in range(8)]
        pbb = [b.bitcast(BF16) for b in pb]

        sems = {n: es.enter_context(nc.semaphore("s_" + n)) for n in ['pe', 'act', 'dve', 'pool']}
        dsems = [es.enter_context(nc.semaphore("d%d" % i)) for i in range(26)]
        es.enter_context(nc.Block())
        S = Sched(nc, sems, dsems)

        def vcol(c):
            return vec[:, c:c + 1]

        bank_rr = [0]

        def pbank(group):
            i = bank_rr[0]
            bank_rr[0] += 1
            return (i % 4) if group == 'proj' else 4 + (i % 4)

        def mm(out, lhsT, rhs, start, stop, reads, writes):
            S.op('pe', lambda e: e.matmul(out, lhsT=lhsT, rhs=rhs, start=start, stop=stop), reads=reads, writes=writes)

        def act(out, in_, func, reads, writes, **kw):
            S.op('act', lambda e: e.activation(out=out, in_=in_, func=func, **kw), reads=reads, writes=writes)

        def tt(eng, out, in0, in1, op, reads, writes):
            S.op(eng, lambda e: e.tensor_tensor(out=out, in0=in0, in1=in1, op=op), reads=reads, writes=writes)

        def ts(out, in0, s1, s2, op0, op1, reads, writes):
            if op1 is None:
                S.op('dve', lambda e: e.tensor_scalar(out=out, in0=in0, scalar1=s1, scalar2=0.0, op0=op0, op1=ALU.add), reads=reads, writes=writes)
            else:
                S.op('dve', lambda e: e.tensor_scalar(out=out, in0=in0, scalar1=s1, scalar2=s2, op0=op0, op1=op1), reads=reads, writes=writes)

        def stt(out, in0, scalar, in1, op0, op1, reads, writes):
            S.op('dve', lambda e: e.scalar_tensor_tensor(out=out, in0=in0, scalar=scalar, in1=in1, op0=op0, op1=op1), reads=reads, writes=writes)

        wq = []
        wstate = {'issued': 0, 'used': 0}

        def plan_weights():
            for t in range(nt):
                for l in range(2):
                    base = len(wq)
                    for h in range(4):
                        wq.append((w_in[l], 0, 16, [(C_QA + h * 128, 128), (C_FA + h * 128, 128)]))
                        wq.append((w_in[l], 0, 16, [(C_GA + h * 128, 128), (C_IA + h * 128, 128)]))
                    for hp in range(4):
                        wq.append((w_in[l], 0, 16, [(C_QB + hp * 256, 256)]))
                    wq.append((w_in[l], 0, 16, [(C_KB, 256)]))
                    wq.append((w_in[l], 0, 16, [(C_VB, 256)]))
                    for h in range(4):
                        wq.append((w_in[l], 0, 16, [(C_QC + h * 128, 128), (C_KC + h * 128, 128)]))
                        wq.append((w_in[l], 0, 16, [(C_GC + h * 128, 128), (C_VC + h * 128, 128)]))
                    for oc in range(8):
                        wq.append((w_out[l], 0, 16, [(oc * 256, 256)]))
                    for (p0, p1) in MGRP:
                        for jp in range(p0, p1):
                            wq.append((w_up[l], 0, 16, [(jp * 256, 256)]))
                            wq.append((w_up[l], 0, 16, [(DFF + jp * 256, 256)]))
                        for ocp in range(8):
                            wq.append((w_down[l], 2 * p0, 2 * (p1 - p0), [(ocp * 256, 256)]))
                    for i_ in range(base, len(wq)):
                        wq[i_] = wq[i_] + (t, l * NWT + (i_ - base))
                    assert len(wq) - base == NWT, len(wq) - base

        def issue_w(i):
            mat, k0, nk, cols, tpass, sidx = wq[i]
            slot = i % NWS
            tw = sum(w for _, w in cols)
            n = nk * tw
            if tpass == 0:
                st = i % 2
                view = wst[st][:, 0:n].rearrange("p (k c) -> p k c", k=nk)
                off = 0
                for c0, w in cols:
                    src = mat[k0 * 128:(k0 + nk) * 128, c0:c0 + w].rearrange("(k p) c -> p k c", p=128)
                    S.dma('sp', view[:, :, off:off + w], src, writes=[('wst', st)], semkey=('wst', st), nowaw=(off > 0))
                    off += w
                ce = ['act', 'pool', 'act', 'dve'][i % 4]
                if ce == 'act':
                    S.op('act', lambda e: e.activation(out=wsl[slot][:, 0:n], in_=wst[st][:, 0:n], func=AF.Copy), reads=[('wst', st)], writes=[('w', slot)])
                else:
                    S.op(ce, lambda e: e.tensor_copy(out=wsl[slot][:, 0:n], in_=wst[st][:, 0:n]), reads=[('wst', st)], writes=[('w', slot)])
                if wstate.get('pending') is not None:
                    pslot, psidx, pn = wstate['pending']
                    S.dma('sp', wscr[psidx][:, 0:pn], wsl[pslot][:, 0:pn], reads=[('w', pslot)], writes=[('wscr', psidx)], semkey=('wsc', pslot))
                wstate['pending'] = (slot, sidx, n)
            else:
                if wstate.get('pending') is not None:
                    pslot, psidx, pn = wstate['pending']
                    S.dma('sp', wscr[psidx][:, 0:pn], wsl[pslot][:, 0:pn], reads=[('w', pslot)], writes=[('wscr', psidx)], semkey=('wsc', pslot))
                    wstate['pending'] = None
                S.dma('sp', wsl[slot][:, 0:n], wscr[sidx][:, 0:n], reads=[('wscr', sidx)], writes=[('w', slot)], semkey=('w', slot))

        def next_w():
            i = wstate['used']
            wstate['used'] += 1
            while wstate['issued'] < min(len(wq), i + NWS):
                issue_w(wstate['issued'])
                wstate['issued'] += 1
            mat, k0, nk, cols, tpass, sidx = wq[i]
            slot = i % NWS
            tw = sum(w for _, w in cols)
            view = wsl[slot][:, 0:nk * tw].rearrange("p (k c) -> p k c", k=nk)
            return view, ('w', slot)

        def proj_fm(wv, wkey, ci, nk, k0, actT, akey, bank, W, first=True, last=True):
            bk = ('ps', bank)
            for k in range(nk):
                st = first and k == 0
                sp_ = last and k == nk - 1 and W == TT
                ak = akey(k0 + k) if callable(akey) else akey
                mm(pb[bank][:, 0:TT], wv[:, k, ci * 128:(ci + 1) * 128], actT[:, k0 + k, 0:TT], st, sp_, [wkey, ak], [bk])
                if W > TT:
                    mm(pb[bank][:, TT:W], wv[:, k, ci * 128:(ci + 1) * 128], actT[:, k0 + k, TT:W], False, last and k == nk - 1, [wkey, ak], [bk])

        def proj_tm(wv, wkey, ci, bank, W):
            bk = ('ps', bank)
            for j in range(2):
                for k in range(NKC):
                    mm(pb[bank][:, j * 128:(j + 1) * 128], hT[:, k, j * 128:(j + 1) * 128], wv[:, k, ci * 128:(ci + 1) * 128], k == 0, k == NKC - 1, [wkey, 'hT'], [bk])
            if W > TT:
                for k in range(NKC):
                    mm(pb[bank][0:4, 256:384], hT[:, k, TT:W], wv[:, k, ci * 128:(ci + 1) * 128], k == 0, k == NKC - 1, [wkey, 'hT'], [bk])

        def rmsnorm(gcol0, dst, dkey, W):
            bank = pbank('mx')
            bk = ('ps', bank)
            for c in range(NKC):
                sq = sqx[c % 2]
                act(sq[:, 0:W], X['b'][:, c, 0:W], AF.Square, [X['k']], [('sqx', c % 2)])
                mm(pb[bank][:, 0:W], ones[:, 0, :], sq[:, 0:W], c == 0, c == NKC - 1, [('sqx', c % 2), 'ones'], [bk])
            act(rstd[:, 0:W], pb[bank][:, 0:W], AF.Ln, [bk, 'vec'], ['rstd'], bias=vcol(V_EPS), scale=1.0)
            act(rstd[:, 0:W], rstd[:, 0:W], AF.Exp, ['rstd'], ['rstd'], scale=-0.5)
            for c in range(NKC):
                stt(dst[:, c, 0:W], X['b'][:, c, 0:W], vcol(gcol0 + c), rstd[:, 0:W], ALU.mult, ALU.mult, [X['k'], 'rstd', 'vec'], [dkey(c) if callable(dkey) else dkey])

        import os
        KSTOP = int(os.environ.get('KSTOP', '99'))
        KSUB = int(os.environ.get('KSUB', '99'))

        def lin_head(kind, t, l, h, W):
            first_tile = (t == 0)
            w1, k1 = next_w()
            bq = pbank('proj'); bf = pbank('proj')
            proj_fm(w1, k1, 0, 16, 0, hT, 'hT', bq, W)
            proj_fm(w1, k1, 1, 16, 0, hT, 'hT', bf, W)
            w2, k2 = next_w()
            bg = pbank('proj'); bi = pbank('proj')
            proj_fm(w2, k2, 0, 16, 0, hT, 'hT', bg, W)
            proj_tm(w2, k2, 1, bi, W)
            kq, kf, kg, ki = ('ps', bq), ('ps', bf), ('ps', bg), ('ps', bi)
            lh = l * 4 + h
            act(vt[:, :, :], pb[bi][:, 0:256].rearrange("p (j d) -> p j d", j=2), AF.Copy, [ki], ['vt'])
            if W > TT:
                act(vs[0:4, :], pb[bi][0:4, 256:384], AF.Copy, [ki], ['vs'])
                act(qs[:, :], pb[bq][:, TT:W], AF.Copy, [kq], ['qs'])
            if KSUB < 1:
                return
            act(gate[:, 0:W], pb[bg][:, 0:W], AF.Silu, [kg], ['gate'])
            if KSUB < 2:
                return
            if kind == 'a':
                act(sig[:, 0:W], pb[bf][:, 0:W], AF.Sigmoid, [kf], ['sig'])
                act(lf[:, 0:W], sig[:, 0:W], AF.Ln, ['sig', 'vec'], ['lf'], scale=vcol(V_OML + lh), bias=vcol(V_LB + lh))
                ts(ka[:, 0:W], sig[:, 0:W], -1.0, vcol(V_LBM1 + lh), ALU.add, ALU.mult, ['sig', 'vec'], ['ka'])
                S.op('dve', lambda e: e.tensor_tensor_scan(out=bb[:, :], data0=rwc[:, 0, :], data1=lf[:, 0:TT], initial=0.0, op0=ALU.mult, op1=ALU.add), reads=['lf', 'rwc'], writes=['bb'])
                act(Eb[:, :], bb[:, :], AF.Exp, ['bb'], ['Eb'])
                act(Ei[:, :], bb[:, :], AF.Exp, ['bb'], ['Ei'], scale=-1.0)
                tt('dve', qt[:, :], pb[bq][:, 0:TT], Eb[:, :], ALU.mult, [kq, 'Eb'], ['qt'])
                tt('dve', kh[:, :], ka[:, 0:TT], Ei[:, :], ALU.mult, ['ka', 'Ei'], ['kh'])
                act(khb[:, :], kh[:, :], AF.Copy, ['kh'], ['khb'])
                E3 = Eb[:, :].rearrange("p (c t) -> p c t", t=64)
                tt('dve', kbar[:, :].rearrange("p (c t) -> p c t", t=64), kh[:, :].rearrange("p (c t) -> p c t", t=64),
                   E3[:, :, 63:64].to_broadcast([P, 4, 64]), ALU.mult, ['kh', 'Eb'], ['kbar'])
                qinter = qt
                maskT = cmk[:, :]
                nch = 4
            else:
                act(qt[:, :], pb[bq][:, 0:TT], AF.Copy, [kq], ['qt'])
                tt('dve', qin[:, :], pb[bq][:, 0:TT], rwc[:, 1 + h, :], ALU.mult, [kq, 'rwc'], ['qin'])
                act(khb[:, :], pb[bf][:, 0:TT], AF.Copy, [kf], ['khb'])
                tt('dve', kbar[:, :], pb[bf][:, 0:TT], rwc[:, 5 + h, :], ALU.mult, [kf, 'rwc'], ['kbar'])
                qinter = qin
                maskT = dmk[:, h, :]
                nch = 2
            if KSUB < 3:
                return
            btr = pbank('mx')
            for j in range(2):
                S.op('pe', lambda e, j=j: e.transpose(out=pbb[btr][:, j * 128:(j + 1) * 128], in_=kbar[:, j * 128:(j + 1) * 128], identity=idb[:, :]), reads=['kbar', 'idb'], writes=[('ps', btr)])
            if kind == 'a':
                act(kbT[:, 0, :, :], pbb[btr][:, 0:256].rearrange("p (j d) -> p j d", j=2), AF.Identity, [('ps', btr), 'vec'], ['kbT'], scale=vcol(V_MLO))
                act(kbT[:, 1, :, :], pbb[btr][:, 0:256].rearrange("p (j d) -> p j d", j=2), AF.Identity, [('ps', btr), 'vec'], ['kbT'], scale=vcol(V_MHI))
            else:
                act(kbT[:, 0, :, :], pbb[btr][:, 0:256].rearrange("p (j d) -> p j d", j=2), AF.Copy, [('ps', btr)], ['kbT'])
            if KSUB < 4:
                return
            ba = pbank('mx')
            for j in range(2):
                mm(pb[ba][:, j * 128:(j + 1) * 128], khb[:, j * 128:(j + 1) * 128], qt[:, j * 128:(j + 1) * 128], True, True, ['khb', 'qt'], [('ps', ba)])
            tt('dve', PTs[:, :], pb[ba][:, 0:TT], maskT, ALU.mult, [('ps', ba), 'masks'], ['PTs'])
            if KSUB < 5:
                return
            bd = pbank('mx')
            csz = TT // nch
            for c in range(nch):
                j = (c * csz) // 128
                half = ((c * csz) % 128) // 64 if kind == 'a' else 0
                mm(pb[bd][:, c * 128:(c + 1) * 128], kbT[:, half, j, :], vt[:, j, :], True, True, ['kbT', 'vt'], [('ps', bd)])
            Sfk = ('Sf', kind, l, h)
            Sft = Sf[(kind, l, h)]
            for c in range(nch):
                act(Sb[:, c, :], Sft[:, :], AF.Copy, [Sfk], [('Sb', c)])
                if kind == 'a':
                    stt(Sft[:, :], Sft[:, :], Eb[:, c * 64 + 63:c * 64 + 64], pb[bd][:, c * 128:(c + 1) * 128], ALU.mult, ALU.add, [Sfk, 'Eb', ('ps', bd)], [Sfk])
                else:
                    stt(Sft[:, :], Sft[:, :], float(GAMMAS[h] ** 128), pb[bd][:, c * 128:(c + 1) * 128], ALU.mult, ALU.add, [Sfk, ('ps', bd)], [Sfk])
            if KSUB < 6:
                return
            bo = pbank('mx')
            ko = ('ps', bo)
            for j in range(2):
                mm(pb[bo][:, j * 128:(j + 1) * 128], vt[:, j, :], PTs[:, j * 128:(j + 1) * 128], True, False, ['vt', 'PTs'], [ko])
                cpb = nch // 2
                for cc in range(cpb):
                    c = j * cpb + cc
                    mm(pb[bo][:, c * csz:(c + 1) * csz], Sb[:, c, :], qinter[:, c * csz:(c + 1) * csz], False, cc == cpb - 1, [('Sb', c), 'qt', 'qin'], [ko])
            act(o_sb[:, 0:TT], pb[bo][:, 0:TT], AF.Copy, [ko], ['o_sb'])
            if KSUB < 7:
                return
            if W > TT:
                sample_lin(kind, l, h, bq, bf)
            if KSUB < 8:
                return
            gcol = (V_GA if kind == 'a' else V_GC) + lh
            bn = pbank('mx')
            kn = ('ps', bn)
            if kind == 'a':
                act(sqo[:, 0:W], o_sb[:, 0:W].bitcast(F32), AF.Square, ['o_sb'], ['sqo'])
                src = o_sb
            else:
                mm(pb[bn][:, 0:TT], ones[:, 1, :], o_sb[:, 0:TT], True, W == TT, ['o_sb', 'ones'], [kn])
                if W > TT:
                    mm(pb[bn][:, TT:W], ones[:, 1, :], o_sb[:, TT:W], False, True, ['o_sb', 'ones'], [kn])
                tt('dve', cen[:, 0:W], o_sb[:, 0:W].bitcast(F32), pb[bn][:, 0:W], ALU.subtract, ['o_sb', kn], ['cen'])
                act(sqo[:, 0:W], cen[:, 0:W].bitcast(F32), AF.Square, ['cen'], ['sqo'])
                src = cen
                bn = pbank('mx')
                kn = ('ps', bn)
            mm(pb[bn][:, 0:TT], ones[:, 1, :], sqo[:, 0:TT], True, W == TT, ['sqo', 'ones'], [kn])
            if W > TT:
                mm(pb[bn][:, TT:W], ones[:, 1, :], sqo[:, TT:W], False, True, ['sqo', 'ones'], [kn])
            act(rstd[:, 0:W], pb[bn][:, 0:W], AF.Ln, [kn, 'vec'], ['rstd'], bias=vcol(V_EPS), scale=1.0)
            act(rstd[:, 0:W], rstd[:, 0:W], AF.Exp, ['rstd'], ['rstd'], scale=-0.5)
            mrow = h if kind == 'a' else 12 + h
            stt(cg[:, 0:W], src[:, 0:W].bitcast(F32), vcol(gcol), rstd[:, 0:W], ALU.mult, ALU.mult, ['o_sb', 'cen', 'rstd', 'vec'], ['sig'])
            tt('dve', mix[:, mrow, 0:W], cg[:, 0:W], gate[:, 0:W], ALU.mult, ['sig', 'gate'], [('mix', mrow)])

        def sample_lin(kind, l, h, bq, bf):
            src_st = st_h if kind == 'a' else st_r
            dst_st = o_shs if kind == 'a' else o_srs
            S.dma('pool', Ss[:, :, :], src_st[l, :, h, :, :].rearrange("b k v -> k b v"), writes=['Ss'], semkey='ss')
            if kind == 'a':
                act(fs[:, :], lf[:, TT:TT + NS], AF.Exp, ['lf'], ['fs'])
                act(ks[:, :], ka[:, TT:TT + NS], AF.Copy, ['ka'], ['ks'])
            else:
                act(ks[:, :], pb[bf][:, TT:TT + NS], AF.Identity, [('ps', bf)], ['ks'], scale=SCALE)
            bv = pbank('mx')
            for b in range(NS):
                mm(pb[bv][:, b * 128:(b + 1) * 128], sel[:, b, :], vs[:, :], True, True, ['sel', 'vs'], [('ps', bv)])
            tt('dve', tmpS[:, :, :], pb[bv][:, :].rearrange("p (b v) -> p b v", b=NS), ks[:, :].unsqueeze(2).to_broadcast([P, NS, P]), ALU.mult, [('ps', bv), 'ks'], ['tmpS'])
            if kind == 'a':
                tt('dve', Ss[:, :, :], Ss[:, :, :], fs[:, :].unsqueeze(2).to_broadcast([P, NS, P]), ALU.mult, ['Ss', 'fs'], ['Ss'])
                tt('dve', Ss[:, :, :], Ss[:, :, :], tmpS[:, :, :], ALU.add, ['Ss', 'tmpS'], ['Ss'])
            else:
                stt(Ss[:, :, :].rearrange("p b v -> p (b v)"), Ss[:, :, :].rearrange("p b v -> p (b v)"), float(GAMMAS[h]), tmpS[:, :, :].rearrange("p b v -> p (b v)"), ALU.mult, ALU.add, ['Ss', 'tmpS'], ['Ss'])
            S.dma('pool', dst_st[l, :, h, :, :].rearrange("b k v -> k b v"), Ss[:, :, :], reads=['Ss'], semkey='ssout')
            bo2 = pbank('mx')
            for b in range(NS):
                mm(pb[bo2][:, b * 4:b * 4 + 4], Ss[:, b, :], qs[:, 0:4], True, True, ['Ss', 'qs'], [('ps', bo2)])
            act(o_sb[:, TT:TT + NS], pb[bo2][:, 0:16:5], AF.Copy, [('ps', bo2)], ['o_sb'])

        def swa(t, l, W):
            for hp in range(4):
                w, wk = next_w()
                for ci in range(2):
                    hh = hp * 2 + ci
                    bq = pbank('proj')
                    proj_fm(w, wk, ci, 16, 0, hT, 'hT', bq, W)
                    act(qb[:, hh, :], pb[bq][:, 0:TT], AF.Copy, [('ps', bq)], [('qb', hh)])
                    if W > TT:
                        act(qsw[:, :, hh], pb[bq][:, TT:W], AF.Copy, [('ps', bq)], ['qsw'])
            w, wk = next_w()
            for kv in range(2):
                bk_ = pbank('proj')
                proj_fm(w, wk, kv, 16, 0, hT, 'hT', bk_, TT)
                act(kwin[l][:, kv, 128:384], pb[bk_][:, 0:TT], AF.Copy, [('ps', bk_)], [('kwin', l)])
            bkt = [pbank('proj'), pbank('proj')]
            for kv in range(2):
                proj_tm(w, wk, kv, bkt[kv], W)
            w2, wk2 = next_w()
            bvt = [pbank('proj'), pbank('proj')]
            for kv in range(2):
                proj_tm(w2, wk2, kv, bvt[kv], W)
            for kv in range(2):
                act(vwin[l][:, 1:3, kv * 128:(kv + 1) * 128], pb[bvt[kv]][:, 0:256].rearrange("p (j d) -> p j d", j=2), AF.Copy, [('ps', bvt[kv])], [('vwin', l)])
                act(kv32[:, kv * 128:(kv + 1) * 128], pb[bkt[kv]][:, 128:256], AF.Copy, [('ps', bkt[kv])], ['kv32'])
                act(kv32[:, 256 + kv * 128:256 + (kv + 1) * 128], pb[bvt[kv]][:, 128:256], AF.Copy, [('ps', bvt[kv])], ['kv32'])
                if W > TT:
                    act(kvs[:, kv * 128:(kv + 1) * 128], pb[bkt[kv]][0:4, 256:384], AF.Copy, [('ps', bkt[kv])], ['kvs'])
                    act(kvs[:, 256 + kv * 128:256 + (kv + 1) * 128], pb[bvt[kv]][0:4, 256:384], AF.Copy, [('ps', bvt[kv])], ['kvs'])
            if t == nt - 1:
                S.dma('pool', o_pk[l], kv32[:, 0:256], reads=['kv32'], semkey='kvout')
                S.dma('pool', o_pv[l], kv32[:, 256:512], reads=['kv32'], semkey='kvout')
            it = 0
            pendingB = [None]

            def stageB(r, j, hh, kv):
                bt_ = pbank('mx')
                for half in range(2):
                    S.op('pe', lambda e, half=half: e.transpose(out=pbb[bt_][:, half * 128:(half + 1) * 128], in_=pn_sc[r][:, half * 128:(half + 1) * 128], identity=idb[:, :]), reads=[('pn_sc', r), 'idb'], writes=[('ps', bt_)])
                act(pT_sc[r][:, :], pbb[bt_][:, 0:256], AF.Copy, [('ps', bt_)], [('pT_sc', r)])
                bo = pbank('mx')
                mm(pb[bo][:, 0:128], vwin[l][:, j, kv * 128:(kv + 1) * 128], pT_sc[r][:, 0:128], True, False, [('vwin', l), ('pT_sc', r)], [('ps', bo)])
                mm(pb[bo][:, 0:128], vwin[l][:, j + 1, kv * 128:(kv + 1) * 128], pT_sc[r][:, 128:256], False, True, [('vwin', l), ('pT_sc', r)], [('ps', bo)])
                act(mix[:, 4 + hh, j * 128:(j + 1) * 128], pb[bo][:, 0:128], AF.Copy, [('ps', bo)], [('mix', 4 + hh)])

            for j in range(2):
                for hh in range(8):
                    kv = hh // 4
                    r = it % 2
                    it += 1
                    bs = pbank('mx')
                    mm(pb[bs][:, 0:256], qb[:, hh, j * 128:(j + 1) * 128], kwin[l][:, kv, j * 128:j * 128 + 256], True, True, [('qb', hh), ('kwin', l)], [('ps', bs)])
                    brow = hh + (8 if (t == 0 and j == 0) else 0)
                    stt(s_sc[r][:, :], pb[bs][:, 0:256], SCALE, bia[:, brow, :], ALU.mult, ALU.add, [('ps', bs), 'bia'], [['Eb', 'Ei'][r]])
                    S.op('dve', lambda e, r=r: e.tensor_reduce(out=sm[r][:, 0:1], in_=s_sc[r][:, :], axis=AX.X, op=ALU.max), reads=[['Eb', 'Ei'][r]], writes=[('sm', r)])
                    ts(sm[r][:, 1:2], sm[r][:, 0:1], sinkt[:, l * 8 + hh:l * 8 + hh + 1], -1.0, ALU.max, ALU.mult, [('sm', r), 'sinkt'], [('sm', r)])
                    act(p_sc[r][:, :], s_sc[r][:, :], AF.Exp, [['Eb', 'Ei'][r], ('sm', r)], [['kh', 'bb'][r]], bias=sm[r][:, 1:2], scale=1.0)
                    if pendingB[0] is not None:
                        stageB(*pendingB[0])
                        pendingB[0] = None
                    S.op('dve', lambda e, r=r: e.tensor_reduce(out=sm[r][:, 2:3], in_=p_sc[r][:, :], axis=AX.X, op=ALU.add), reads=[['kh', 'bb'][r]], writes=[('sm', r)])
                    act(sm[r][:, 3:4], sinkt[:, l * 8 + hh:l * 8 + hh + 1], AF.Exp, ['sinkt', ('sm', r)], [('sm', r)], bias=sm[r][:, 1:2], scale=1.0)
                    tt('dve', sm[r][:, 4:5], sm[r][:, 2:3], sm[r][:, 3:4], ALU.add, [('sm', r)], [('sm', r)])
                    S.op('dve', lambda e, r=r: e.reciprocal(out=sm[r][:, 5:6], in_=sm[r][:, 4:5]), reads=[('sm', r)], writes=[('sm', r)])
                    ts(pn_sc[r][:, :], p_sc[r][:, :], sm[r][:, 5:6], None, ALU.mult, None, [['kh', 'bb'][r], ('sm', r)], [('pn_sc', r)])
                    pendingB[0] = (r, j, hh, kv)
            stageB(*pendingB[0])
            if W > TT:
                sample_swa(l)
            act(kwin[l][:, :, 0:128], kwin[l][:, :, 256:384], AF.Copy, [('kwin', l)], [('kwin', l)])
            act(vwin[l][:, 0, :], vwin[l][:, 2, :], AF.Copy, [('vwin', l)], [('vwin', l)])

        def sample_swa(l):
            for b in range(NS):
                S.dma('pool', Kw[0:127, :], ck[l, b, 1:128, :], writes=['Kw'], semkey='kvw')
                S.dma('pool', Vw[0:127, :], cv[l, b, 1:128, :], writes=['Vw'], semkey='kvw')
                S.dma('pool', Kw[127:128, :], kvs[b:b + 1, 0:256], reads=['kvs'], writes=['Kw'], semkey='kvw')
                S.dma('pool', Vw[127:128, :], kvs[b:b + 1, 256:512], reads=['kvs'], writes=['Vw'], semkey='kvw')
                S.settle('kvw', ['Kw', 'Vw'])
                S.dma('pool', o_sk[l, b], Kw[:, :], reads=['Kw'], semkey='kvwout')
                S.dma('pool', o_sv[l, b], Vw[:, :], reads=['Vw'], semkey='kvwout')
                bt_ = pbank('mx')
                for kv in range(2):
                    S.op('pe', lambda e, kv=kv, bt_=bt_: e.transpose(out=pb[bt_][:, kv * 128:(kv + 1) * 128], in_=Kw[:, kv * 128:(kv + 1) * 128], identity=idf[:, :]), reads=['Kw', 'idf'], writes=[('ps', bt_)])
                act(KwT[:, :, :], pb[bt_][:, 0:256].rearrange("p (i j) -> p i j", i=2), AF.Copy, [('ps', bt_)], ['KwT'])
                bs2 = pbank('mx')
                for kv in range(2):
                    mm(pb[bs2][0:4, kv * 128:(kv + 1) * 128], qsw[:, b, kv * 4:kv * 4 + 4], KwT[:, kv, :], True, True, ['qsw', 'KwT'], [('ps', bs2)])
                stt(ssm[:, :, :].rearrange("g i j -> g (i j)"), pb[bs2][0:4, 0:256], SCALE, bis[:, 0:2, :].rearrange("g i j -> g (i j)"), ALU.mult, ALU.add, [('ps', bs2), 'bis'], ['ssm'])
                sk4 = sink4[:, l, :]
                S.op('dve', lambda e: e.tensor_reduce(out=st4[:, 0, 0:2], in_=ssm[:, :, :], axis=AX.X, op=ALU.max), reads=['ssm'], writes=['st4'])
                tt('dve', st4[:, 1, 0:2], st4[:, 0, 0:2], sk4, ALU.max, ['st4', 'sink4'], ['st4'])
                tt('dve', ssm[:, :, :], ssm[:, :, :], st4[:, 1, 0:2].unsqueeze(2).to_broadcast([4, 2, P]), ALU.subtract, ['ssm', 'st4'], ['ssm'])
                act(psm[0:4, :, :], ssm[:, :, :], AF.Exp, ['ssm'], ['psm'])
                S.op('dve', lambda e: e.tensor_reduce(out=st4[:, 2, 0:2], in_=psm[0:4, :, :], axis=AX.X, op=ALU.add), reads=['psm'], writes=['st4'])
                tt('dve', st4[:, 3, 0:2], sk4, st4[:, 1, 0:2], ALU.subtract, ['st4', 'sink4'], ['st4'])
                act(st4[:, 3, 0:2], st4[:, 3, 0:2], AF.Exp, ['st4'], ['st4'])
                tt('dve', st4[:, 4, 0:2], st4[:, 2, 0:2], st4[:, 3, 0:2], ALU.add, ['st4'], ['st4'])
                S.op('dve', lambda e: e.reciprocal(out=st4[:, 5, 0:2], in_=st4[:, 4, 0:2]), reads=['st4'], writes=['st4'])
                tt('dve', psm[0:4, :, :], psm[0:4, :, :], st4[:, 5, 0:2].unsqueeze(2).to_broadcast([4, 2, P]), ALU.mult, ['psm', 'st4'], ['psm'])
                bt2 = pbank('mx')
                for kv in range(2):
                    S.op('pe', lambda e, kv=kv, bt2=bt2: e.transpose(out=pb[bt2][:, kv * 128:(kv + 1) * 128], in_=psm[:, kv, :], identity=idf[:, :]), reads=['psm', 'idf'], writes=[('ps', bt2)])
                act(pnT[:, 0:8].rearrange("p (k g) -> p k g", k=2), pb[bt2][:, 0:256].rearrange("p (k j) -> p k j", k=2)[:, :, 0:4], AF.Copy, [('ps', bt2)], ['pnT'])
                bo = pbank('mx')
                for kv in range(2):
                    mm(pb[bo][:, kv * 4:kv * 4 + 4], Vw[:, kv * 128:(kv + 1) * 128], pnT[:, kv * 4:kv * 4 + 4], True, True, ['Vw', 'pnT'], [('ps', bo)])
                act(mix[:, 4:12, TT + b], pb[bo][:, 0:8], AF.Copy, [('ps', bo)], [('mix', 4 + i) for i in range(8)])

        def conv_chunk(l, uc, bank, W, dst, r, t, dk):
            bk = ('ps', bank)
            u = ug[r]
            uk = ('ug', r)
            act(u[:, 2:2 + TT], pb[bank][:, 0:TT], AF.Copy, [bk], [uk])
            S.op('pool', lambda e: e.tensor_copy(out=u[:, 0:2], in_=upre[l][:, uc, :]), reads=[('upre', l, uc)], writes=[uk])
            cwc = V_CW + (l * 3) * 88 + uc
            act(dst[:, 0:W], pb[bank][:, 0:W], AF.Identity, [bk, 'vec'], [dk], scale=vcol(cwc + 2 * 88), bias=vcol(V_CB + l * 88 + uc))
            stt(dst[:, 0:TT], u[:, 1:1 + TT], vcol(cwc + 88), dst[:, 0:TT], ALU.mult, ALU.add, [uk, dk, 'vec'], [dk])
            stt(dst[:, 0:TT], u[:, 0:TT], vcol(cwc), dst[:, 0:TT], ALU.mult, ALU.add, [uk, dk, 'vec'], [dk])
            S.op('pool', lambda e: e.tensor_copy(out=upre[l][:, uc, :], in_=u[:, TT:TT + 2]), reads=[uk], writes=[('upre', l, uc)])
            if W > TT:
                stt(dst[:, TT:W], scp[l][:, uc, :, 1], vcol(cwc + 88), dst[:, TT:W], ALU.mult, ALU.add, ['scp', dk, 'vec'], [dk])
                stt(dst[:, TT:W], scp[l][:, uc, :, 0], vcol(cwc), dst[:, TT:W], ALU.mult, ALU.add, ['scp', dk, 'vec'], [dk])
                S.op('pool', lambda e: e.tensor_copy(out=sco[l][:, uc, :, 0], in_=scp[l][:, uc, :, 1]), reads=['scp'], writes=[('sco', l)])
                act(sco[l][:, uc, :, 1], pb[bank][:, TT:W], AF.Copy, [bk], [('sco', l)])

        def down_half(nk, l, W):
            for ocp in range(8):
                by = [pbank('proj'), pbank('proj')]
                wa, wka = next_w()
                for ci in range(2):
                    proj_fm(wa, wka, ci, nk, 0, m_sb, (lambda k: ('m', k)), by[ci], W)
                for ci in range(2):
                    oc = ocp * 2 + ci
                    tt('dve', X['b'][:, oc, 0:W], X['b'][:, oc, 0:W], pb[by[ci]][:, 0:W], ALU.add, [X['k'], ('ps', by[ci])], [X['k']])

        def ffn(t, l, W):
            rmsnorm(V_G2 + l * 16, hT, 'hT', W)
            for jp in range(NJ // 2):
                grp = 'proj' if jp % 2 == 0 else 'mx'
                wg, wgk = next_w()
                bg = [pbank(grp), pbank(grp)]
                for ci in range(2):
                    proj_fm(wg, wgk, ci, 16, 0, hT, 'hT', bg[ci], W)
                wv, wvk = next_w()
                bv = [pbank(grp), pbank(grp)]
                for ci in range(2):
                    proj_fm(wv, wvk, ci, 16, 0, hT, 'hT', bv[ci], W)
                for ci in range(2):
                    j = jp * 2 + ci
                    conv_chunk(l, j, bg[ci], W, cg, 0, t, 'sig')
                    conv_chunk(l, NJ + j, bv[ci], W, cvv, 1, t, 'lf')
                    act(tg[:, 0:W], cg[:, 0:W], AF.Square, ['sig'], ['ka'])
                    ts(tg[:, 0:W], tg[:, 0:W], 0.044715, 1.0, ALU.mult, ALU.add, ['ka'], ['ka'])
                    tt('dve', tg[:, 0:W], tg[:, 0:W], cg[:, 0:W], ALU.mult, ['ka', 'sig'], ['ka'])
                    act(sg[:, 0:W], tg[:, 0:W], AF.Sigmoid, ['ka'], ['gate'], scale=1.5957691216057308)
                    tt('dve', sg[:, 0:W], sg[:, 0:W], cg[:, 0:W], ALU.mult, ['gate', 'sig'], ['gate'])
                    mi = j - 2 * [p0 for (p0, p1) in MGRP if p0 <= jp < p1][0]
                    tt('dve', m_sb[:, mi, 0:W], sg[:, 0:W], cvv[:, 0:W], ALU.mult, ['gate', 'lf'], [('m', mi)])
                for (p0, p1) in MGRP:
                    if jp == p1 - 1:
                        down_half(2 * (p1 - p0), l, W)
            if t == nt - 1:
                S.dma('pool', o_pc[l], upre[l][:, :, :], reads=[('upre', l, uc) for uc in range(88)], semkey='out')
            if W > TT:
                S.dma('pool', o_sc[l], sco[l][:, :, :, :], reads=[('sco', l)], semkey='out')

        def layer(t, l, W):
            if KSTOP < 1:
                return
            rmsnorm(V_G1 + l * 16, hT, 'hT', W)
            if KSTOP < 2:
                return
            for h in range(4):
                lin_head('a', t, l, h, W)
                if KSTOP < 3:
                    return
            if KSTOP < 4:
                return
            swa(t, l, W)
            if KSTOP < 5:
                return
            for h in range(4):
                lin_head('r', t, l, h, W)
            if KSTOP < 6:
                return
            mixkeys = [('mix', i) for i in range(16)]
            for ocp in range(8):
                w, wk = next_w()
                for ci in range(2):
                    oc = ocp * 2 + ci
                    by = pbank('proj')
                    proj_fm(w, wk, ci, 16, 0, mix, (lambda k: ('mix', k)), by, W)
                    tt('dve', X['b'][:, oc, 0:W], X['b'][:, oc, 0:W], pb[by][:, 0:W], ALU.add, [X['k'], ('ps', by)], [X['k']])
            if KSTOP < 7:
                return
            ffn(t, l, W)

        plan_weights()
        for dst, src, key in [(vec, vecs, 'vec'), (sinkt, sinkb, 'sinkt'), (sink4, sinks4, 'sink4'), (cmk, cmask, 'masks'), (dmk, dmask, 'masks'),
                              (rwc, rowc, 'rwc'), (bia, biasd, 'bia'), (bis, biass, 'bis'), (sel, seld, 'sel'), (idb, identb, 'idb'),
                              (idf, identf, 'idf'), (ones, onesd, 'ones')]:
            S.dma('pool', dst[:], src, writes=[key], semkey='const')
        for l in range(2):
            S.dma('pool', scp[l][:, :, :, :], scT[l], writes=['scp'], semkey='const')
        S.settle('const', ['vec', 'sinkt', 'sink4', 'masks', 'rwc', 'bia', 'bis', 'sel', 'idb', 'idf', 'ones', 'scp'])
        tt('dve', vec[:, V_LB + 4:V_LB + 8], vec[:, V_RAW + 4:V_RAW + 8], vec[:, V_RAW:V_RAW + 4], ALU.subtract, ['vec'], ['vec'])
        act(vec[:, V_LB + 4:V_LB + 8], vec[:, V_LB + 4:V_LB + 8], AF.Sigmoid, ['vec'], ['vec'])
        ts(vec[:, V_OML:V_OML + 8], vec[:, V_LB:V_LB + 8], -1.0, 1.0, ALU.mult, ALU.add, ['vec'], ['vec'])
        ts(vec[:, V_LBM1:V_LBM1 + 8], vec[:, V_LB:V_LB + 8], -1.0, 0.0, ALU.add, ALU.add, ['vec'], ['vec'])
        for l in range(2):
            if l == 0:
                S.op('pool', lambda e: e.memset(vs[:, :], 0.0), writes=['vs'])
            S.op('pool', lambda e, l=l: e.memset(upre[l][:, :, :], 0.0), writes=[('upre', l, uc) for uc in range(88)])
            S.op('pool', lambda e, l=l: e.memset(kwin[l][:, :, :], 0.0), writes=[('kwin', l)])
            S.op('pool', lambda e, l=l: e.memset(vwin[l][:, :, :], 0.0), writes=[('vwin', l)])
            for k in 'ar':
                for h in range(4):
                    S.op('pool', lambda e, k=k, l=l, h=h: e.memset(Sf[(k, l, h)][:, :], 0.0), writes=[('Sf', k, l, h)])

        def load_x(t):
            xb = xbufs[t % 2]
            xk = "x%d" % (t % 2)
            S.dma('sp', xb[:, :, 0:TT], xT[:, :, t * TT:(t + 1) * TT].rearrange("c p n -> p c n"), writes=[xk], semkey=('xload', t % 2))
            if t == 0:
                S.dma('sp', xb[:, :, TT:TT + NS], xT[:, :, T:T + NS].rearrange("c p n -> p c n"), writes=[xk], semkey=('xload', t % 2))

        load_x(0)
        for t in range(nt):
            W = TT + NS if t == 0 else TT
            X['b'] = xbufs[t % 2]
            X['k'] = "x%d" % (t % 2)
            if t + 1 < nt:
                load_x(t + 1)
            for l in range(2):
                layer(t, l, W)
            rmsnorm(V_GF, X['b'], X['k'], W)
            S.dma('pool', yT[:, :, t * TT:(t + 1) * TT].rearrange("c p n -> p c n"), X['b'][:, :, 0:TT], reads=[X['k']], semkey=('yout', t % 2))
            if t == 0:
                S.dma('pool', yT[:, :, T:T + NS].rearrange("c p n -> p c n"), X['b'][:, :, TT:W], reads=[X['k']], semkey=('yout', t % 2))
        for l in range(2):
            for h in range(4):
                S.dma('pool', o_hs[l, h], Sf[('a', l, h)][:, :], reads=[('Sf', 'a', l, h)], semkey='out')
                S.dma('pool', o_rs[l, h], Sf[('r', l, h)][:, :], reads=[('Sf', 'r', l, h)], semkey='out')
        assert KSTOP < 99 or wstate['used'] == len(wq), (wstate, len(wq))
        S.finish('sp')
        print("ops", S.nops, {k: v for k, v in S.cnt.items()})
    return nc


_CACHE = {}


def _consts():
    import ml_dtypes
    bf = ml_dtypes.bfloat16
    c = {}
    s = np.arange(128)[:, None]
    tcol = np.arange(256)[None, :]
    tl = tcol % 128
    c['cmask'] = (((s // 64) == (tl // 64)) & (s <= tl)).astype(np.float32).astype(bf)
    dm = np.zeros((128, 4, 256), np.float32)
    rowc = np.zeros((128, 9, 256), np.float32)
    rowc[:, 0, :] = 1.0
    rowc[:, 0, ::64] = 0.0
    for h in range(4):
        g = np.float64(GAMMAS[h])
        dm[:, h, :] = np.where(tl >= s, g ** np.maximum(tl - s, 0) * SCALE, 0.0)
        rowc[:, 1 + h, :] = g ** (tl + 1.0)
        rowc[:, 5 + h, :] = g ** (127.0 - tl) * SCALE
    c['dmask'] = dm.astype(bf)
    c['rowc'] = rowc
    qi = np.arange(128)[:, None]
    sj = np.arange(256)[None, :]
    dist = 128 + qi - sj
    valid = (dist >= 0) & (dist < 128)
    bd = np.zeros((128, 16, 256), np.float32)
    for h in range(8):
        bd[:, h, :] = np.where(valid, -SLOPES[h] * dist, -1e30)
        bd[:, 8 + h, :] = np.where(valid & (sj >= 128), -SLOPES[h] * dist, -1e30)
    c['biasd'] = bd.astype(bf)
    bs = np.zeros((4, 2, 128), np.float32)
    for g in range(4):
        for kv in range(2):
            bs[g, kv, :] = -SLOPES[kv * 4 + g] * (127.0 - np.arange(128))
    c['biass'] = bs
    sd = np.zeros((128, 4, 128), np.float32)
    for b in range(4):
        sd[b, b, :] = 1.0
    c['seld'] = sd
    c['identb'] = np.eye(128, dtype=np.float32).astype(bf)
    c['identf'] = np.eye(128, dtype=np.float32)
    on = np.zeros((128, 2, 128), np.float32)
    on[:, 0, :] = 1.0 / 2048.0
    on[:, 1, :] = 1.0 / 128.0
    c['onesd'] = on
    return c


def kernel(x_prompt, x_sample, state_hgrn, cache_swa_k, cache_swa_v, state_ret, state_conv,
           norm1_g, w_in, hgrn_lb_raw, hgrn_norm_g, swa_sinks, ret_norm_g, w_out,
           norm2_g, w_up, conv_w, conv_b, w_down, final_norm_g):
    f = lambda a: np.ascontiguousarray(np.asarray(a, dtype=np.float32))
    x_prompt, x_sample = f(x_prompt), f(x_sample)
    state_hgrn, state_ret = f(state_hgrn), f(state_ret)
    cache_swa_k, cache_swa_v, state_conv = f(cache_swa_k), f(cache_swa_v), f(state_conv)
    w_in, w_out, w_up, w_down = f(w_in), f(w_out), f(w_up), f(w_down)
    if 'nc' not in _CACHE:
        _CACHE['nc'] = build_program()
        _CACHE['c'] = _consts()
    nc = _CACHE['nc']
    cst = _CACHE['c']

    raise_if = None
    in_maps = []
    n_cores = 8
    vecs = np.zeros((128, VW), np.float32)
    n1 = f(norm1_g).reshape(2, 16, 128); n2 = f(norm2_g).reshape(2, 16, 128); nf = f(final_norm_g).reshape(16, 128)
    for l in range(2):
        vecs[:, 0 + l * 16:0 + (l + 1) * 16] = n1[l].T
        vecs[:, 32 + l * 16:32 + (l + 1) * 16] = n2[l].T
    vecs[:, 64:80] = nf.T
    ga = f(hgrn_norm_g).reshape(2, 4, 128); gc = f(ret_norm_g).reshape(2, 4, 128)
    for l in range(2):
        vecs[:, 104 + l * 4:104 + (l + 1) * 4] = ga[l].T
        vecs[:, 112 + l * 4:112 + (l + 1) * 4] = gc[l].T
    cw = f(conv_w).reshape(2, 3, 88, 128); cb = f(conv_b).reshape(2, 88, 128)
    for l in range(2):
        for tap in range(3):
            vecs[:, V_CW + (l * 3 + tap) * 88:V_CW + (l * 3 + tap + 1) * 88] = cw[l, tap].T
        vecs[:, V_CB + l * 88:V_CB + (l + 1) * 88] = cb[l].T
    vecs[:, V_EPS] = EPS
    vecs[0:64, V_MLO] = 1.0
    vecs[64:128, V_MHI] = 1.0
    lbraw = f(hgrn_lb_raw).reshape(2, 4, 128)
    vecs[:, V_RAW:V_RAW + 4] = lbraw[0].T
    vecs[:, V_RAW + 4:V_RAW + 8] = lbraw[1].T
    sk = f(swa_sinks)
    sinkb = np.broadcast_to(sk.reshape(1, 16), (128, 16)).copy()
    sinks4 = np.ascontiguousarray(sk.reshape(2, 2, 4).transpose(2, 0, 1))
    for c in range(n_cores):
        b = c % 4
        xT = np.empty((16, 128, T + NS), np.float32)
        xT[:, :, :T] = x_prompt[b].T.reshape(16, 128, T)
        xT[:, :, T:] = x_sample[c * 4:(c + 1) * 4, 0, :].T.reshape(16, 128, NS)
        m = {
            'xT': xT,
            'st_h': np.ascontiguousarray(state_hgrn[:, c * 4:(c + 1) * 4]),
            'st_r': np.ascontiguousarray(state_ret[:, c * 4:(c + 1) * 4]),
            'ck': np.ascontiguousarray(cache_swa_k[:, c * 4:(c + 1) * 4].reshape(2, 4, 128, 256)),
            'cv': np.ascontiguousarray(cache_swa_v[:, c * 4:(c + 1) * 4].reshape(2, 4, 128, 256)),
            'scT': np.ascontiguousarray(state_conv[:, c * 4:(c + 1) * 4].reshape(2, 4, 2, 88, 128).transpose(0, 4, 3, 1, 2)),
            'w_in': w_in, 'w_out': w_out, 'w_up': w_up, 'w_down': w_down,
            'vecs': vecs, 'sinkb': sinkb, 'sinks4': sinks4,
        }
        m.update(cst)
        in_maps.append(m)
    if _CACHE.get('maps_only'):
        return in_maps
    res = run_bass_kernel_spmd(nc, in_maps, core_ids=list(range(n_cores)))
    R = res.results
    y_prompt = np.stack([R[b]['yT'][:, :, :T].reshape(D, T).T for b in range(4)])
    y_sample = np.concatenate([R[c]['yT'][:, :, T:].reshape(D, NS).T for c in range(8)])[:, None, :]
    p_hgrn = np.stack([R[b]['o_hs'] for b in range(4)], axis=1)
    p_ret = np.stack([R[b]['o_rs'] for b in range(4)], axis=1)
    p_k = np.stack([R[b]['o_pk'].reshape(2, 128, 2, 128) for b in range(4)], axis=1)
    p_v = np.stack([R[b]['o_pv'].reshape(2, 128, 2, 128) for b in range(4)], axis=1)
    p_conv = np.stack([R[b]['o_pc'].transpose(0, 3, 2, 1).reshape(2, 2, 11264) for b in range(4)], axis=1)
    s_hgrn = np.concatenate([R[c]['o_shs'] for c in range(8)], axis=1)
    s_ret = np.concatenate([R[c]['o_srs'] for c in range(8)], axis=1)
    s_k = np.concatenate([R[c]['o_sk'].reshape(2, 4, 128, 2, 128) for c in range(8)], axis=1)
    s_v = np.concatenate([R[c]['o_sv'].reshape(2, 4, 128, 2, 128) for c in range(8)], axis=1)
    s_conv = np.concatenate([R[c]['o_sc'].transpose(0, 3, 4, 2, 1).reshape(2, 4, 2, 11264) for c in range(8)], axis=1)
    out = (y_prompt, y_sample, p_hgrn, p_k, p_v, p_ret, p_conv, s_hgrn, s_k, s_v, s_ret, s_conv)
    return tuple(np.ascontiguousarray(o, dtype=np.float32) for o in out)
```

```python
import os
import numpy as np
from contextlib import ExitStack
import concourse.bass as bass
import concourse.mybir as mybir
from concourse.bass_utils import run_bass_kernel_spmd

F32 = mybir.dt.float32
F32R = mybir.dt.float32r
BF16 = mybir.dt.bfloat16
AF = mybir.ActivationFunctionType
ALU = mybir.AluOpType
AX = mybir.AxisListType

P = 128
D = 2048
NKC = 16
T = 2048
TT = 256
NT = T // TT
NS = 4
DFF = 5632
NJ = DFF // P
NWT = 8 + 4 + 2 + 8 + 8 + 44 + 32
MGRP = [(0, 6), (6, 12), (12, 18), (18, 22)]
INC = 5632
EPS = 1e-6
SCALE = 128.0 ** -0.5
GAMMAS = [float(1.0 - 2.0 ** (-5.0 - h)) for h in range(4)]
SLOPES = [float(2.0 ** (-(h + 1.0))) for h in range(8)]
VW = 1024
V_G1, V_G2, V_GF = 0, 32, 64
V_LB, V_OML, V_LBM1 = 80, 88, 96
V_GA, V_GC = 104, 112
V_CW = 120
V_CB = V_CW + 2 * 3 * 88
V_EPS = V_CB + 2 * 88
V_RAW = V_EPS + 1
V_MLO = V_RAW + 8
V_MHI = V_MLO + 1
C_QA, C_FA, C_IA, C_GA, C_QB, C_KB, C_VB, C_QC, C_KC, C_VC, C_GC = 0, 512, 1024, 1536, 2048, 3072, 3328, 3584, 4096, 4608, 5120


class Sched:
    def __init__(self, nc, sems, dma_sems):
        self.nc = nc
        self.eng = {'pe': nc.tensor, 'act': nc.scalar, 'dve': nc.vector, 'pool': nc.gpsimd, 'sp': nc.sync}
        self.sem = sems
        self.cnt = {e: 0 for e in self.eng}
        self.dma_sems = list(dma_sems)
        self.dma_cnt = {}
        self.dma_tot = {}
        self.key_sem = {}
        self.waited = {e: {} for e in self.eng}
        self.lastw = {}
        self.readers = {}
        self.nops = 0

    def _need(self, e, rec, waits):
        semname, sem, val, src = rec
        if src == e and e == 'pe':
            return
        if src == 'dma':
            val = self.dma_cnt[semname[4:]] if False else self.dma_tot[semname]
        if self.waited[e].get(semname, 0) >= val:
            return
        prev = waits.get(semname)
        if prev is None or prev[1] < val:
            waits[semname] = (sem, val)

    def _deps(self, e, reads, writes):
        waits = {}
        for k in reads:
            w = self.lastw.get(k)
            if w is not None:
                self._need(e, w, waits)
        for k in writes:
            w = self.lastw.get(k)
            if w is not None and not (w[3] == e):
                self._need(e, w, waits)
            for r in self.readers.get(k, ()):
                if r[3] == e:
                    continue
                self._need(e, r, waits)
        for semname, (sem, val) in waits.items():
            self.eng[e].wait_ge(sem, val)
            self.waited[e][semname] = val

    def _record(self, rec, reads, writes):
        for k in reads:
            self.readers.setdefault(k, []).append(rec)
        for k in writes:
            self.lastw[k] = rec
            self.readers[k] = []
        self.nops += 1

    def op(self, e, fn, reads=(), writes=()):
        self._deps(e, reads, writes)
        ins = fn(self.eng[e])
        self.cnt[e] += 1
        ins.then_inc(self.sem[e], 1)
        self._record((e, self.sem[e], self.cnt[e], e), reads, writes)
        return ins

    def dma(self, e, out, in_, reads=(), writes=(), semkey=None, nowaw=False):
        self._deps(e, reads, () if nowaw else writes)
        if semkey is None:
            semkey = 'misc'
        if semkey not in self.key_sem:
            self.key_sem[semkey] = self.dma_sems.pop()
            self.dma_cnt[semkey] = 0
        sem = self.key_sem[semkey]
        ins = self.eng[e].dma_start(out=out, in_=in_)
        self.dma_cnt[semkey] += 16
        self.dma_tot['dma:%s' % (semkey,)] = self.dma_cnt[semkey]
        ins.then_inc(sem, 16)
        self._record(('dma:%s' % (semkey,), sem, self.dma_cnt[semkey], 'dma'), reads, writes)
        return ins

    def settle(self, semkey, keys):
        sem = self.key_sem[semkey]
        for k in keys:
            self.lastw[k] = ('dma:%s' % (semkey,), sem, self.dma_cnt[semkey], 'dma')

    def finish(self, e='sp'):
        for k, sem in self.key_sem.items():
            self.eng[e].wait_ge(sem, self.dma_cnt[k])
        for n, sem in self.sem.items():
            if self.cnt[n] > 0:
                self.eng[e].wait_ge(sem, self.cnt[n])


def build_program(nt=NT):
    nc = bass.Bass("TRN2", target_bir_lowering=False)
    nc.dge_precook = False

    def din(name, shape, dt=F32):
        return nc.dram_tensor(name, list(shape), dt, kind="ExternalInput").ap()

    def dout(name, shape, dt=F32):
        return nc.dram_tensor(name, list(shape), dt, kind="ExternalOutput").ap()

    xT = din("xT", [NKC, P, T + NS])
    st_h = din("st_h", [2, NS, 4, P, P])
    st_r = din("st_r", [2, NS, 4, P, P])
    ck = din("ck", [2, NS, P, 256])
    cv = din("cv", [2, NS, P, 256])
    scT = din("scT", [2, P, 88, NS, 2])
    w_in = din("w_in", [2, D, INC])
    w_out = din("w_out", [2, D, D])
    w_up = din("w_up", [2, D, 2 * DFF])
    w_down = din("w_down", [2, DFF, D])
    wscr = nc.dram_tensor("wscr", [2 * NWT, P, 4096], BF16).ap()
    vecs = din("vecs", [P, VW])
    sinkb = din("sinkb", [P, 16])
    sinks4 = din("sinks4", [4, 2, 2])
    cmask = din("cmask", [P, 256], BF16)
    dmask = din("dmask", [P, 4, 256], BF16)
    rowc = din("rowc", [P, 9, 256])
    biasd = din("biasd", [P, 16, 256], BF16)
    biass = din("biass", [4, 2, 128])
    seld = din("seld", [P, 4, P])
    identb = din("identb", [P, P], BF16)
    identf = din("identf", [P, P])
    onesd = din("onesd", [P, 2, P], F32R)

    yT = dout("yT", [NKC, P, T + NS])
    o_hs = dout("o_hs", [2, 4, P, P])
    o_rs = dout("o_rs", [2, 4, P, P])
    o_pk = dout("o_pk", [2, P, 256])
    o_pv = dout("o_pv", [2, P, 256])
    o_pc = dout("o_pc", [2, P, 88, 2])
    o_shs = dout("o_shs", [2, NS, 4, P, P])
    o_srs = dout("o_srs", [2, NS, 4, P, P])
    o_sk = dout("o_sk", [2, NS, P, 256])
    o_sv = dout("o_sv", [2, NS, P, 256])
    o_sc = dout("o_sc", [2, P, 88, NS, 2])


    with ExitStack() as es:
        def sbuf(name, shape, dt=F32):
            return es.enter_context(nc.sbuf_tensor(name, list(shape), dt))

        def psum(name, shape, dt=F32):
            return es.enter_context(nc.psum_tensor(name, list(shape), dt))

        WT = TT + NS
        xbufs = [sbuf("x_sb0", [P, NKC, WT]), sbuf("x_sb1", [P, NKC, WT])]
        X = {"b": xbufs[0], "k": "x0"}
        hT = sbuf("hT", [P, NKC, WT], BF16)
        mix = sbuf("mix", [P, NKC, WT], BF16)
        m_sb = sbuf("m_sb", [P, 12, WT], BF16)
        NWS = 3
        wsl = [sbuf("wsl%d" % i, [P, NKC * 256], BF16) for i in range(NWS)]
        wst = [sbuf("wst%d" % i, [P, NKC * 256]) for i in range(2)]
        vec = sbuf("vec", [P, VW])
        sinkt = sbuf("sinkt", [P, 16])
        sink4 = sbuf("sink4", [4, 2, 2])
        cmk = sbuf("cmk", [P, 256], BF16)
        dmk = sbuf("dmk", [P, 4, 256], BF16)
        rwc = sbuf("rwc", [P, 9, 256])
        bia = sbuf("bia", [P, 16, 256], BF16)
        bis = sbuf("bis", [4, 2, 128])
        sel = sbuf("sel", [P, 4, P])
        idb = sbuf("idb", [P, P], BF16)
        idf = sbuf("idf", [P, P])
        ones = sbuf("ones", [P, 2, P], F32R)
        Sf = {(k, l, h): sbuf("Sf%s%d%d" % (k, l, h), [P, P]) for k in 'ar' for l in range(2) for h in range(4)}
        kwin = [sbuf("kwin%d" % l, [P, 2, 384], BF16) for l in range(2)]
        vwin = [sbuf("vwin%d" % l, [P, 3, 256], BF16) for l in range(2)]
        upre = [sbuf("upre%d" % l, [P, 88, 2]) for l in range(2)]
        sig = sbuf("sig", [P, WT]); lf = sbuf("lf", [P, WT]); ka = sbuf("ka", [P, WT])
        bb = sbuf("bb", [P, TT]); Eb = sbuf("Eb", [P, TT]); Ei = sbuf("Ei", [P, TT]); kh = sbuf("kh", [P, TT])
        qt = sbuf("qt", [P, TT], BF16); qin = sbuf("qin", [P, TT], BF16); khb = sbuf("khb", [P, TT], BF16)
        kbar = sbuf("kbar", [P, TT], BF16); kbT = sbuf("kbT", [P, 2, 2, P], BF16); vt = sbuf("vt", [P, 2, P], BF16)
        PTs = sbuf("PTs", [P, TT], BF16); Sb = sbuf("Sb", [P, 4, P], BF16)
        o_sb = sbuf("o_sb", [P, WT], F32R); cen = sbuf("cen", [P, WT], F32R); sqo = sbuf("sqo", [P, WT], F32R)
        rstd = sbuf("rstd", [P, WT]); gate = sbuf("gate", [P, WT])
        sqx = [sbuf("sqx%d" % i, [P, WT], F32R) for i in range(2)]
        qb = sbuf("qb", [P, 8, TT], BF16)
        kv32 = sbuf("kv32", [P, 512])
        s_sc = [Eb, Ei]
        p_sc = [kh, bb]
        pn_sc = [sbuf("pn_sc%d" % i, [P, 256], BF16) for i in range(2)]
        pT_sc = [sbuf("pT_sc%d" % i, [P, 256], BF16) for i in range(2)]
        sm = [sbuf("sm%d" % i, [P, 8]) for i in range(2)]
        ug = [sbuf("ug%d" % i, [P, TT + 2]) for i in range(2)]
        cg, cvv, tg, sg = sig, lf, ka, gate
        Ss = sbuf("Ss", [P, NS, P]); tmpS = sbuf("tmpS", [P, NS, P])
        vs = sbuf("vs", [P, P]); qs = sbuf("qs", [P, NS]); ks = sbuf("ks", [P, NS]); fs = sbuf("fs", [P, NS])
        Kw = sbuf("Kw", [P, 256]); Vw = sbuf("Vw", [P, 256]); KwT = sbuf("KwT", [P, 2, P])
        kvs = sbuf("kvs", [4, 512]); qsw = sbuf("qsw", [P, NS, 8])
        ssm = sbuf("ssm", [4, 2, P]); psm = sbuf("psm", [P, 2, P]); st4 = sbuf("st4", [4, 6, 8])
        pnT = sbuf("pnT", [P, 32])
        scp = [sbuf("scp%d" % l, [P, 88, NS, 2]) for l in range(2)]
        sco = [sbuf("sco%d" % l, [P, 88, NS, 2]) for l in range(2)]
        pb = [psum("pb%d" % i, [P, 512]) for i in range(8)]
        pbb = [b.bitcast(BF16) for b in pb]

        sems = {n: es.enter_context(nc.semaphore("s_" + n)) for n in ['pe', 'act', 'dve', 'pool']}
        dsems = [es.enter_context(nc.semaphore("d%d" % i)) for i in range(26)]
        es.enter_context(nc.Block())
        S = Sched(nc, sems, dsems)

        def vcol(c):
            return vec[:, c:c + 1]

        bank_rr = [0]

        def pbank(group):
            i = bank_rr[0]
            bank_rr[0] += 1
            return (i % 4) if group == 'proj' else 4 + (i % 4)

        def mm(out, lhsT, rhs, start, stop, reads, writes):
            S.op('pe', lambda e: e.matmul(out, lhsT=lhsT, rhs=rhs, start=start, stop=stop), reads=reads, writes=writes)

        def act(out, in_, func, reads, writes, **kw):
            S.op('act', lambda e: e.activation(out=out, in_=in_, func=func, **kw), reads=reads, writes=writes)

        def tt(eng, out, in0, in1, op, reads, writes):
            S.op(eng, lambda e: e.tensor_tensor(out=out, in0=in0, in1=in1, op=op), reads=reads, writes=writes)

        def ts(out, in0, s1, s2, op0, op1, reads, writes):
            if op1 is None:
                S.op('dve', lambda e: e.tensor_scalar(out=out, in0=in0, scalar1=s1, scalar2=0.0, op0=op0, op1=ALU.add), reads=reads, writes=writes)
            else:
                S.op('dve', lambda e: e.tensor_scalar(out=out, in0=in0, scalar1=s1, scalar2=s2, op0=op0, op1=op1), reads=reads, writes=writes)

        def stt(out, in0, scalar, in1, op0, op1, reads, writes):
            S.op('dve', lambda e: e.scalar_tensor_tensor(out=out, in0=in0, scalar=scalar, in1=in1, op0=op0, op1=op1), reads=reads, writes=writes)

        wq = []
        wstate = {'issued': 0, 'used': 0}

        def plan_weights():
            for t in range(nt):
                for l in range(2):
                    base = len(wq)
                    for h in range(4):
                        wq.append((w_in[l], 0, 16, [(C_QA + h * 128, 128), (C_FA + h * 128, 128)]))
                        wq.append((w_in[l], 0, 16, [(C_GA + h * 128, 128), (C_IA + h * 128, 128)]))
                    for hp in range(4):
                        wq.append((w_in[l], 0, 16, [(C_QB + hp * 256, 256)]))
                    wq.append((w_in[l], 0, 16, [(C_KB, 256)]))
                    wq.append((w_in[l], 0, 16, [(C_VB, 256)]))
                    for h in range(4):
                        wq.append((w_in[l], 0, 16, [(C_QC + h * 128, 128), (C_KC + h * 128, 128)]))
                        wq.append((w_in[l], 0, 16, [(C_GC + h * 128, 128), (C_VC + h * 128, 128)]))
                    for oc in range(8):
                        wq.append((w_out[l], 0, 16, [(oc * 256, 256)]))
                    for (p0, p1) in MGRP:
                        for jp in range(p0, p1):
                            wq.append((w_up[l], 0, 16, [(jp * 256, 256)]))
                            wq.append((w_up[l], 0, 16, [(DFF + jp * 256, 256)]))
                        for ocp in range(8):
                            wq.append((w_down[l], 2 * p0, 2 * (p1 - p0), [(ocp * 256, 256)]))
                    for i_ in range(base, len(wq)):
                        wq[i_] = wq[i_] + (t, l * NWT + (i_ - base))
                    assert len(wq) - base == NWT, len(wq) - base

        def issue_w(i):
            mat, k0, nk, cols, tpass, sidx = wq[i]
            slot = i % NWS
            tw = sum(w for _, w in cols)
            n = nk * tw
            if tpass == 0:
                st = i % 2
                view = wst[st][:, 0:n].rearrange("p (k c) -> p k c", k=nk)
                off = 0
                for c0, w in cols:
                    src = mat[k0 * 128:(k0 + nk) * 128, c0:c0 + w].rearrange("(k p) c -> p k c", p=128)
                    S.dma('sp', view[:, :, off:off + w], src, writes=[('wst', st)], semkey=('wst', st), nowaw=(off > 0))
                    off += w
                ce = ['act', 'pool', 'act', 'dve'][i % 4]
                if ce == 'act':
                    S.op('act', lambda e: e.activation(out=wsl[slot][:, 0:n], in_=wst[st][:, 0:n], func=AF.Copy), reads=[('wst', st)], writes=[('w', slot)])
                else:
                    S.op(ce, lambda e: e.tensor_copy(out=wsl[slot][:, 0:n], in_=wst[st][:, 0:n]), reads=[('wst', st)], writes=[('w', slot)])
                if wstate.get('pending') is not None:
                    pslot, psidx, pn = wstate['pending']
                    S.dma('sp', wscr[psidx][:, 0:pn], wsl[pslot][:, 0:pn], reads=[('w', pslot)], writes=[('wscr', psidx)], semkey=('wsc', pslot))
                wstate['pending'] = (slot, sidx, n)
            else:
                if wstate.get('pending') is not None:
                    pslot, psidx, pn = wstate['pending']
                    S.dma('sp', wscr[psidx][:, 0:pn], wsl[pslot][:, 0:pn], reads=[('w', pslot)], writes=[('wscr', psidx)], semkey=('wsc', pslot))
                    wstate['pending'] = None
                S.dma('sp', wsl[slot][:, 0:n], wscr[sidx][:, 0:n], reads=[('wscr', sidx)], writes=[('w', slot)], semkey=('w', slot))

        def next_w():
            i = wstate['used']
            wstate['used'] += 1
            while wstate['issued'] < min(len(wq), i + NWS):
                issue_w(wstate['issued'])
                wstate['issued'] += 1
            mat, k0, nk, cols, tpass, sidx = wq[i]
            slot = i % NWS
            tw = sum(w for _, w in cols)
            view = wsl[slot][:, 0:nk * tw].rearrange("p (k c) -> p k c", k=nk)
            return view, ('w', slot)

        def proj_fm(wv, wkey, ci, nk, k0, actT, akey, bank, W, first=True, last=True):
            bk = ('ps', bank)
            for k in range(nk):
                st = first and k == 0
                sp_ = last and k == nk - 1 and W == TT
                ak = akey(k0 + k) if callable(akey) else akey
                mm(pb[bank][:, 0:TT], wv[:, k, ci * 128:(ci + 1) * 128], actT[:, k0 + k, 0:TT], st, sp_, [wkey, ak], [bk])
                if W > TT:
                    mm(pb[bank][:, TT:W], wv[:, k, ci * 128:(ci + 1) * 128], actT[:, k0 + k, TT:W], False, last and k == nk - 1, [wkey, ak], [bk])

        def proj_tm(wv, wkey, ci, bank, W):
            bk = ('ps', bank)
            for j in range(2):
                for k in range(NKC):
                    mm(pb[bank][:, j * 128:(j + 1) * 128], hT[:, k, j * 128:(j + 1) * 128], wv[:, k, ci * 128:(ci + 1) * 128], k == 0, k == NKC - 1, [wkey, 'hT'], [bk])
            if W > TT:
                for k in range(NKC):
                    mm(pb[bank][0:4, 256:384], hT[:, k, TT:W], wv[:, k, ci * 128:(ci + 1) * 128], k == 0, k == NKC - 1, [wkey, 'hT'], [bk])

        def rmsnorm(gcol0, dst, dkey, W):
            bank = pbank('mx')
            bk = ('ps', bank)
            for c in range(NKC):
                sq = sqx[c % 2]
                act(sq[:, 0:W], X['b'][:, c, 0:W], AF.Square, [X['k']], [('sqx', c % 2)])
                mm(pb[bank][:, 0:W], ones[:, 0, :], sq[:, 0:W], c == 0, c == NKC - 1, [('sqx', c % 2), 'ones'], [bk])
            act(rstd[:, 0:W], pb[bank][:, 0:W], AF.Ln, [bk, 'vec'], ['rstd'], bias=vcol(V_EPS), scale=1.0)
            act(rstd[:, 0:W], rstd[:, 0:W], AF.Exp, ['rstd'], ['rstd'], scale=-0.5)
            for c in range(NKC):
                stt(dst[:, c, 0:W], X['b'][:, c, 0:W], vcol(gcol0 + c), rstd[:, 0:W], ALU.mult, ALU.mult, [X['k'], 'rstd', 'vec'], [dkey(c) if callable(dkey) else dkey])

        import os
        KSTOP = int(os.environ.get('KSTOP', '99'))
        KSUB = int(os.environ.get('KSUB', '99'))

        def lin_head(kind, t, l, h, W):
            first_tile = (t == 0)
            w1, k1 = next_w()
            bq = pbank('proj'); bf = pbank('proj')
            proj_fm(w1, k1, 1, 16, 0, hT, 'hT', bf, W)
            proj_fm(w1, k1, 0, 16, 0, hT, 'hT', bq, W)
            w2, k2 = next_w()
            bg = pbank('proj'); bi = pbank('proj')
            proj_fm(w2, k2, 0, 16, 0, hT, 'hT', bg, W)
            proj_tm(w2, k2, 1, bi, W)
            kq, kf, kg, ki = ('ps', bq), ('ps', bf), ('ps', bg), ('ps', bi)
            lh = l * 4 + h
            act(vt[:, :, :], pb[bi][:, 0:256].rearrange("p (j d) -> p j d", j=2), AF.Copy, [ki], ['vt'])
            if W > TT:
                act(vs[0:4, :], pb[bi][0:4, 256:384], AF.Copy, [ki], ['vs'])
                act(qs[:, :], pb[bq][:, TT:W], AF.Copy, [kq], ['qs'])
            if KSUB < 1:
                return
            act(gate[:, 0:W], pb[bg][:, 0:W], AF.Silu, [kg], ['gate'])
            if KSUB < 2:
                return
            if kind == 'a':
                act(sig[:, 0:W], pb[bf][:, 0:W], AF.Sigmoid, [kf], ['sig'])
                act(lf[:, 0:W], sig[:, 0:W], AF.Ln, ['sig', 'vec'], ['lf'], scale=vcol(V_OML + lh), bias=vcol(V_LB + lh))
                ts(ka[:, 0:W], sig[:, 0:W], -1.0, vcol(V_LBM1 + lh), ALU.add, ALU.mult, ['sig', 'vec'], ['ka'])
                S.op('dve', lambda e: e.tensor_tensor_scan(out=bb[:, :], data0=rwc[:, 0, :], data1=lf[:, 0:TT], initial=0.0, op0=ALU.mult, op1=ALU.add), reads=['lf', 'rwc'], writes=['bb'])
                act(Eb[:, :], bb[:, :], AF.Exp, ['bb'], ['Eb'])
                act(Ei[:, :], bb[:, :], AF.Exp, ['bb'], ['Ei'], scale=-1.0)
                tt('dve', qt[:, :], pb[bq][:, 0:TT], Eb[:, :], ALU.mult, [kq, 'Eb'], ['qt'])
                tt('dve', kh[:, :], ka[:, 0:TT], Ei[:, :], ALU.mult, ['ka', 'Ei'], ['kh'])
                act(khb[:, :], kh[:, :], AF.Copy, ['kh'], ['khb'])
                E3 = Eb[:, :].rearrange("p (c t) -> p c t", t=64)
                tt('dve', kbar[:, :].rearrange("p (c t) -> p c t", t=64), kh[:, :].rearrange("p (c t) -> p c t", t=64),
                   E3[:, :, 63:64].to_broadcast([P, 4, 64]), ALU.mult, ['kh', 'Eb'], ['kbar'])
                qinter = qt
                maskT = cmk[:, :]
                nch = 4
            else:
                act(qt[:, :], pb[bq][:, 0:TT], AF.Copy, [kq], ['qt'])
                tt('dve', qin[:, :], pb[bq][:, 0:TT], rwc[:, 1 + h, :], ALU.mult, [kq, 'rwc'], ['qin'])
                act(khb[:, :], pb[bf][:, 0:TT], AF.Copy, [kf], ['khb'])
                tt('dve', kbar[:, :], pb[bf][:, 0:TT], rwc[:, 5 + h, :], ALU.mult, [kf, 'rwc'], ['kbar'])
                qinter = qin
                maskT = dmk[:, h, :]
                nch = 2
            if KSUB < 3:
                return
            btr = pbank('mx')
            for j in range(2):
                S.op('pe', lambda e, j=j: e.transpose(out=pbb[btr][:, j * 128:(j + 1) * 128], in_=kbar[:, j * 128:(j + 1) * 128], identity=idb[:, :]), reads=['kbar', 'idb'], writes=[('ps', btr)])
            if kind == 'a':
                act(kbT[:, 0, :, :], pbb[btr][:, 0:256].rearrange("p (j d) -> p j d", j=2), AF.Identity, [('ps', btr), 'vec'], ['kbT'], scale=vcol(V_MLO))
                act(kbT[:, 1, :, :], pbb[btr][:, 0:256].rearrange("p (j d) -> p j d", j=2), AF.Identity, [('ps', btr), 'vec'], ['kbT'], scale=vcol(V_MHI))
            else:
                act(kbT[:, 0, :, :], pbb[btr][:, 0:256].rearrange("p (j d) -> p j d", j=2), AF.Copy, [('ps', btr)], ['kbT'])
            if KSUB < 4:
                return
            ba = pbank('mx')
            for j in range(2):
                mm(pb[ba][:, j * 128:(j + 1) * 128], khb[:, j * 128:(j + 1) * 128], qt[:, j * 128:(j + 1) * 128], True, True, ['khb', 'qt'], [('ps', ba)])
            tt('dve', PTs[:, :], pb[ba][:, 0:TT], maskT, ALU.mult, [('ps', ba), 'masks'], ['PTs'])
            if KSUB < 5:
                return
            bd = pbank('mx')
            csz = TT // nch
            for c in range(nch):
                j = (c * csz) // 128
                half = ((c * csz) % 128) // 64 if kind == 'a' else 0
                mm(pb[bd][:, c * 128:(c + 1) * 128], kbT[:, half, j, :], vt[:, j, :], True, True, ['kbT', 'vt'], [('ps', bd)])
            Sfk = ('Sf', kind, l, h)
            Sft = Sf[(kind, l, h)]
            for c in range(nch):
                act(Sb[:, c, :], Sft[:, :], AF.Copy, [Sfk], [('Sb', c)])
                if kind == 'a':
                    stt(Sft[:, :], Sft[:, :], Eb[:, c * 64 + 63:c * 64 + 64], pb[bd][:, c * 128:(c + 1) * 128], ALU.mult, ALU.add, [Sfk, 'Eb', ('ps', bd)], [Sfk])
                else:
                    stt(Sft[:, :], Sft[:, :], float(GAMMAS[h] ** 128), pb[bd][:, c * 128:(c + 1) * 128], ALU.mult, ALU.add, [Sfk, ('ps', bd)], [Sfk])
            if KSUB < 6:
                return
            bo = pbank('mx')
            ko = ('ps', bo)
            for j in range(2):
                mm(pb[bo][:, j * 128:(j + 1) * 128], vt[:, j, :], PTs[:, j * 128:(j + 1) * 128], True, False, ['vt', 'PTs'], [ko])
                cpb = nch // 2
                for cc in range(cpb):
                    c = j * cpb + cc
                    mm(pb[bo][:, c * csz:(c + 1) * csz], Sb[:, c, :], qinter[:, c * csz:(c + 1) * csz], False, cc == cpb - 1, [('Sb', c), 'qt', 'qin'], [ko])
            act(o_sb[:, 0:TT], pb[bo][:, 0:TT], AF.Copy, [ko], ['o_sb'])
            if KSUB < 7:
                return
            if W > TT:
                sample_lin(kind, l, h, bq, bf)
            if KSUB < 8:
                return
            gcol = (V_GA if kind == 'a' else V_GC) + lh
            bn = pbank('mx')
            kn = ('ps', bn)
            if kind == 'a':
                act(sqo[:, 0:W], o_sb[:, 0:W].bitcast(F32), AF.Square, ['o_sb'], ['sqo'])
                src = o_sb
            else:
                mm(pb[bn][:, 0:TT], ones[:, 1, :], o_sb[:, 0:TT], True, W == TT, ['o_sb', 'ones'], [kn])
                if W > TT:
                    mm(pb[bn][:, TT:W], ones[:, 1, :], o_sb[:, TT:W], False, True, ['o_sb', 'ones'], [kn])
                tt('dve', cen[:, 0:W], o_sb[:, 0:W].bitcast(F32), pb[bn][:, 0:W], ALU.subtract, ['o_sb', kn], ['cen'])
                act(sqo[:, 0:W], cen[:, 0:W].bitcast(F32), AF.Square, ['cen'], ['sqo'])
                src = cen
                bn = pbank('mx')
                kn = ('ps', bn)
            mm(pb[bn][:, 0:TT], ones[:, 1, :], sqo[:, 0:TT], True, W == TT, ['sqo', 'ones'], [kn])
            if W > TT:
                mm(pb[bn][:, TT:W], ones[:, 1, :], sqo[:, TT:W], False, True, ['sqo', 'ones'], [kn])
            act(rstd[:, 0:W], pb[bn][:, 0:W], AF.Ln, [kn, 'vec'], ['rstd'], bias=vcol(V_EPS), scale=1.0)
            act(rstd[:, 0:W], rstd[:, 0:W], AF.Exp, ['rstd'], ['rstd'], scale=-0.5)
            mrow = h if kind == 'a' else 12 + h
            stt(cg[:, 0:W], src[:, 0:W].bitcast(F32), vcol(gcol), rstd[:, 0:W], ALU.mult, ALU.mult, ['o_sb', 'cen', 'rstd', 'vec'], ['sig'])
            tt('dve', mix[:, mrow, 0:W], cg[:, 0:W], gate[:, 0:W], ALU.mult, ['sig', 'gate'], [('mix', mrow)])

        def sample_lin(kind, l, h, bq, bf):
            src_st = st_h if kind == 'a' else st_r
            dst_st = o_shs if kind == 'a' else o_srs
            S.dma('pool', Ss[:, :, :], src_st[l, :, h, :, :].rearrange("b k v -> k b v"), writes=['Ss'], semkey='ss')
            if kind == 'a':
                act(fs[:, :], lf[:, TT:TT + NS], AF.Exp, ['lf'], ['fs'])
                act(ks[:, :], ka[:, TT:TT + NS], AF.Copy, ['ka'], ['ks'])
            else:
                act(ks[:, :], pb[bf][:, TT:TT + NS], AF.Identity, [('ps', bf)], ['ks'], scale=SCALE)
            bv = pbank('mx')
            for b in range(NS):
                mm(pb[bv][:, b * 128:(b + 1) * 128], sel[:, b, :], vs[:, :], True, True, ['sel', 'vs'], [('ps', bv)])
            tt('dve', tmpS[:, :, :], pb[bv][:, :].rearrange("p (b v) -> p b v", b=NS), ks[:, :].unsqueeze(2).to_broadcast([P, NS, P]), ALU.mult, [('ps', bv), 'ks'], ['tmpS'])
            if kind == 'a':
                tt('dve', Ss[:, :, :], Ss[:, :, :], fs[:, :].unsqueeze(2).to_broadcast([P, NS, P]), ALU.mult, ['Ss', 'fs'], ['Ss'])
                tt('dve', Ss[:, :, :], Ss[:, :, :], tmpS[:, :, :], ALU.add, ['Ss', 'tmpS'], ['Ss'])
            else:
                stt(Ss[:, :, :].rearrange("p b v -> p (b v)"), Ss[:, :, :].rearrange("p b v -> p (b v)"), float(GAMMAS[h]), tmpS[:, :, :].rearrange("p b v -> p (b v)"), ALU.mult, ALU.add, ['Ss', 'tmpS'], ['Ss'])
            S.dma('pool', dst_st[l, :, h, :, :].rearrange("b k v -> k b v"), Ss[:, :, :], reads=['Ss'], semkey='ssout')
            bo2 = pbank('mx')
            for b in range(NS):
                mm(pb[bo2][:, b * 4:b * 4 + 4], Ss[:, b, :], qs[:, 0:4], True, True, ['Ss', 'qs'], [('ps', bo2)])
            act(o_sb[:, TT:TT + NS], pb[bo2][:, 0:16:5], AF.Copy, [('ps', bo2)], ['o_sb'])

        def swa(t, l, W):
            for hp in range(4):
                w, wk = next_w()
                for ci in range(2):
                    hh = hp * 2 + ci
                    bq = pbank('proj')
                    proj_fm(w, wk, ci, 16, 0, hT, 'hT', bq, W)
                    act(qb[:, hh, :], pb[bq][:, 0:TT], AF.Copy, [('ps', bq)], [('qb', hh)])
                    if W > TT:
                        act(qsw[:, :, hh], pb[bq][:, TT:W], AF.Copy, [('ps', bq)], ['qsw'])
            w, wk = next_w()
            for kv in range(2):
                bk_ = pbank('proj')
                proj_fm(w, wk, kv, 16, 0, hT, 'hT', bk_, TT)
                act(kwin[l][:, kv, 128:384], pb[bk_][:, 0:TT], AF.Copy, [('ps', bk_)], [('kwin', l)])
            bkt = [pbank('proj'), pbank('proj')]
            for kv in range(2):
                proj_tm(w, wk, kv, bkt[kv], W)
            w2, wk2 = next_w()
            bvt = [pbank('proj'), pbank('proj')]
            for kv in range(2):
                proj_tm(w2, wk2, kv, bvt[kv], W)
            for kv in range(2):
                act(vwin[l][:, 1:3, kv * 128:(kv + 1) * 128], pb[bvt[kv]][:, 0:256].rearrange("p (j d) -> p j d", j=2), AF.Copy, [('ps', bvt[kv])], [('vwin', l)])
                act(kv32[:, kv * 128:(kv + 1) * 128], pb[bkt[kv]][:, 128:256], AF.Copy, [('ps', bkt[kv])], ['kv32'])
                act(kv32[:, 256 + kv * 128:256 + (kv + 1) * 128], pb[bvt[kv]][:, 128:256], AF.Copy, [('ps', bvt[kv])], ['kv32'])
                if W > TT:
                    act(kvs[:, kv * 128:(kv + 1) * 128], pb[bkt[kv]][0:4, 256:384], AF.Copy, [('ps', bkt[kv])], ['kvs'])
                    act(kvs[:, 256 + kv * 128:256 + (kv + 1) * 128], pb[bvt[kv]][0:4, 256:384], AF.Copy, [('ps', bvt[kv])], ['kvs'])
            if t == nt - 1:
                S.dma('pool', o_pk[l], kv32[:, 0:256], reads=['kv32'], semkey='kvout')
                S.dma('pool', o_pv[l], kv32[:, 256:512], reads=['kv32'], semkey='kvout')
            it = 0
            pendingB = [None]

            def stageB(r, j, hh, kv):
                bt_ = pbank('mx')
                for half in range(2):
                    S.op('pe', lambda e, half=half: e.transpose(out=pbb[bt_][:, half * 128:(half + 1) * 128], in_=pn_sc[r][:, half * 128:(half + 1) * 128], identity=idb[:, :]), reads=[('pn_sc', r), 'idb'], writes=[('ps', bt_)])
                act(pT_sc[r][:, :], pbb[bt_][:, 0:256], AF.Copy, [('ps', bt_)], [('pT_sc', r)])
                bo = pbank('mx')
                mm(pb[bo][:, 0:128], vwin[l][:, j, kv * 128:(kv + 1) * 128], pT_sc[r][:, 0:128], True, False, [('vwin', l), ('pT_sc', r)], [('ps', bo)])
                mm(pb[bo][:, 0:128], vwin[l][:, j + 1, kv * 128:(kv + 1) * 128], pT_sc[r][:, 128:256], False, True, [('vwin', l), ('pT_sc', r)], [('ps', bo)])
                act(mix[:, 4 + hh, j * 128:(j + 1) * 128], pb[bo][:, 0:128], AF.Copy, [('ps', bo)], [('mix', 4 + hh)])

            for j in range(2):
                for hh in range(8):
                    kv = hh // 4
                    r = it % 2
                    it += 1
                    bs = pbank('mx')
                    mm(pb[bs][:, 0:256], qb[:, hh, j * 128:(j + 1) * 128], kwin[l][:, kv, j * 128:j * 128 + 256], True, True, [('qb', hh), ('kwin', l)], [('ps', bs)])
                    brow = hh + (8 if (t == 0 and j == 0) else 0)
                    stt(s_sc[r][:, :], pb[bs][:, 0:256], SCALE, bia[:, brow, :], ALU.mult, ALU.add, [('ps', bs), 'bia'], [['Eb', 'Ei'][r]])
                    S.op('dve', lambda e, r=r: e.tensor_reduce(out=sm[r][:, 0:1], in_=s_sc[r][:, :], axis=AX.X, op=ALU.max), reads=[['Eb', 'Ei'][r]], writes=[('sm', r)])
                    ts(sm[r][:, 1:2], sm[r][:, 0:1], sinkt[:, l * 8 + hh:l * 8 + hh + 1], -1.0, ALU.max, ALU.mult, [('sm', r), 'sinkt'], [('sm', r)])
                    act(p_sc[r][:, :], s_sc[r][:, :], AF.Exp, [['Eb', 'Ei'][r], ('sm', r)], [['kh', 'bb'][r]], bias=sm[r][:, 1:2], scale=1.0)
                    if pendingB[0] is not None:
                        stageB(*pendingB[0])
                        pendingB[0] = None
                    S.op('dve', lambda e, r=r: e.tensor_reduce(out=sm[r][:, 2:3], in_=p_sc[r][:, :], axis=AX.X, op=ALU.add), reads=[['kh', 'bb'][r]], writes=[('sm', r)])
                    act(sm[r][:, 3:4], sinkt[:, l * 8 + hh:l * 8 + hh + 1], AF.Exp, ['sinkt', ('sm', r)], [('sm', r)], bias=sm[r][:, 1:2], scale=1.0)
                    tt('dve', sm[r][:, 4:5], sm[r][:, 2:3], sm[r][:, 3:4], ALU.add, [('sm', r)], [('sm', r)])
                    S.op('dve', lambda e, r=r: e.reciprocal(out=sm[r][:, 5:6], in_=sm[r][:, 4:5]), reads=[('sm', r)], writes=[('sm', r)])
                    ts(pn_sc[r][:, :], p_sc[r][:, :], sm[r][:, 5:6], None, ALU.mult, None, [['kh', 'bb'][r], ('sm', r)], [('pn_sc', r)])
                    pendingB[0] = (r, j, hh, kv)
            stageB(*pendingB[0])
            if W > TT:
                sample_swa(l)
            act(kwin[l][:, :, 0:128], kwin[l][:, :, 256:384], AF.Copy, [('kwin', l)], [('kwin', l)])
            act(vwin[l][:, 0, :], vwin[l][:, 2, :], AF.Copy, [('vwin', l)], [('vwin', l)])

        def sample_swa(l):
            for b in range(NS):
                S.dma('pool', Kw[0:127, :], ck[l, b, 1:128, :], writes=['Kw'], semkey='kvw')
                S.dma('pool', Vw[0:127, :], cv[l, b, 1:128, :], writes=['Vw'], semkey='kvw')
                S.dma('pool', Kw[127:128, :], kvs[b:b + 1, 0:256], reads=['kvs'], writes=['Kw'], semkey='kvw')
                S.dma('pool', Vw[127:128, :], kvs[b:b + 1, 256:512], reads=['kvs'], writes=['Vw'], semkey='kvw')
                S.settle('kvw', ['Kw', 'Vw'])
                S.dma('pool', o_sk[l, b], Kw[:, :], reads=['Kw'], semkey='kvwout')
                S.dma('pool', o_sv[l, b], Vw[:, :], reads=['Vw'], semkey='kvwout')
                bt_ = pbank('mx')
                for kv in range(2):
                    S.op('pe', lambda e, kv=kv, bt_=bt_: e.transpose(out=pb[bt_][:, kv * 128:(kv + 1) * 128], in_=Kw[:, kv * 128:(kv + 1) * 128], identity=idf[:, :]), reads=['Kw', 'idf'], writes=[('ps', bt_)])
                act(KwT[:, :, :], pb[bt_][:, 0:256].rearrange("p (i j) -> p i j", i=2), AF.Copy, [('ps', bt_)], ['KwT'])
                bs2 = pbank('mx')
                for kv in range(2):
                    mm(pb[bs2][0:4, kv * 128:(kv + 1) * 128], qsw[:, b, kv * 4:kv * 4 + 4], KwT[:, kv, :], True, True, ['qsw', 'KwT'], [('ps', bs2)])
                stt(ssm[:, :, :].rearrange("g i j -> g (i j)"), pb[bs2][0:4, 0:256], SCALE, bis[:, 0:2, :].rearrange("g i j -> g (i j)"), ALU.mult, ALU.add, [('ps', bs2), 'bis'], ['ssm'])
                sk4 = sink4[:, l, :]
                S.op('dve', lambda e: e.tensor_reduce(out=st4[:, 0, 0:2], in_=ssm[:, :, :], axis=AX.X, op=ALU.max), reads=['ssm'], writes=['st4'])
                tt('dve', st4[:, 1, 0:2], st4[:, 0, 0:2], sk4, ALU.max, ['st4', 'sink4'], ['st4'])
                tt('dve', ssm[:, :, :], ssm[:, :, :], st4[:, 1, 0:2].unsqueeze(2).to_broadcast([4, 2, P]), ALU.subtract, ['ssm', 'st4'], ['ssm'])
                act(psm[0:4, :, :], ssm[:, :, :], AF.Exp, ['ssm'], ['psm'])
                S.op('dve', lambda e: e.tensor_reduce(out=st4[:, 2, 0:2], in_=psm[0:4, :, :], axis=AX.X, op=ALU.add), reads=['psm'], writes=['st4'])
                tt('dve', st4[:, 3, 0:2], sk4, st4[:, 1, 0:2], ALU.subtract, ['st4', 'sink4'], ['st4'])
                act(st4[:, 3, 0:2], st4[:, 3, 0:2], AF.Exp, ['st4'], ['st4'])
                tt('dve', st4[:, 4, 0:2], st4[:, 2, 0:2], st4[:, 3, 0:2], ALU.add, ['st4'], ['st4'])
                S.op('dve', lambda e: e.reciprocal(out=st4[:, 5, 0:2], in_=st4[:, 4, 0:2]), reads=['st4'], writes=['st4'])
                tt('dve', psm[0:4, :, :], psm[0:4, :, :], st4[:, 5, 0:2].unsqueeze(2).to_broadcast([4, 2, P]), ALU.mult, ['psm', 'st4'], ['psm'])
                bt2 = pbank('mx')
                for kv in range(2):
                    S.op('pe', lambda e, kv=kv, bt2=bt2: e.transpose(out=pb[bt2][:, kv * 128:(kv + 1) * 128], in_=psm[:, kv, :], identity=idf[:, :]), reads=['psm', 'idf'], writes=[('ps', bt2)])
                act(pnT[:, 0:8].rearrange("p (k g) -> p k g", k=2), pb[bt2][:, 0:256].rearrange("p (k j) -> p k j", k=2)[:, :, 0:4], AF.Copy, [('ps', bt2)], ['pnT'])
                bo = pbank('mx')
                for kv in range(2):
                    mm(pb[bo][:, kv * 4:kv * 4 + 4], Vw[:, kv * 128:(kv + 1) * 128], pnT[:, kv * 4:kv * 4 + 4], True, True, ['Vw', 'pnT'], [('ps', bo)])
                act(mix[:, 4:12, TT + b], pb[bo][:, 0:8], AF.Copy, [('ps', bo)], [('mix', 4 + i) for i in range(8)])

        def conv_chunk(l, uc, bank, W, dst, r, t, dk):
            bk = ('ps', bank)
            u = ug[r]
            uk = ('ug', r)
            act(u[:, 2:2 + TT], pb[bank][:, 0:TT], AF.Copy, [bk], [uk])
            S.op('pool', lambda e: e.tensor_copy(out=u[:, 0:2], in_=upre[l][:, uc, :]), reads=[('upre', l, uc)], writes=[uk])
            cwc = V_CW + (l * 3) * 88 + uc
            act(dst[:, 0:W], pb[bank][:, 0:W], AF.Identity, [bk, 'vec'], [dk], scale=vcol(cwc + 2 * 88), bias=vcol(V_CB + l * 88 + uc))
            stt(dst[:, 0:TT], u[:, 1:1 + TT], vcol(cwc + 88), dst[:, 0:TT], ALU.mult, ALU.add, [uk, dk, 'vec'], [dk])
            stt(dst[:, 0:TT], u[:, 0:TT], vcol(cwc), dst[:, 0:TT], ALU.mult, ALU.add, [uk, dk, 'vec'], [dk])
            S.op('pool', lambda e: e.tensor_copy(out=upre[l][:, uc, :], in_=u[:, TT:TT + 2]), reads=[uk], writes=[('upre', l, uc)])
            if W > TT:
                stt(dst[:, TT:W], scp[l][:, uc, :, 1], vcol(cwc + 88), dst[:, TT:W], ALU.mult, ALU.add, ['scp', dk, 'vec'], [dk])
                stt(dst[:, TT:W], scp[l][:, uc, :, 0], vcol(cwc), dst[:, TT:W], ALU.mult, ALU.add, ['scp', dk, 'vec'], [dk])
                S.op('pool', lambda e: e.tensor_copy(out=sco[l][:, uc, :, 0], in_=scp[l][:, uc, :, 1]), reads=['scp'], writes=[('sco', l)])
                act(sco[l][:, uc, :, 1], pb[bank][:, TT:W], AF.Copy, [bk], [('sco', l)])

        def down_half(nk, l, W):
            for ocp in range(8):
                by = [pbank('proj'), pbank('proj')]
                wa, wka = next_w()
                for ci in range(2):
                    proj_fm(wa, wka, ci, nk, 0, m_sb, (lambda k: ('m', k)), by[ci], W)
                for ci in range(2):
                    oc = ocp * 2 + ci
                    tt('dve', X['b'][:, oc, 0:W], X['b'][:, oc, 0:W], pb[by[ci]][:, 0:W], ALU.add, [X['k'], ('ps', by[ci])], [X['k']])

        def ffn(t, l, W):
            rmsnorm(V_G2 + l * 16, hT, 'hT', W)
            for jp in range(NJ // 2):
                grp = 'proj' if jp % 2 == 0 else 'mx'
                wg, wgk = next_w()
                bg = [pbank(grp), pbank(grp)]
                for ci in range(2):
                    proj_fm(wg, wgk, ci, 16, 0, hT, 'hT', bg[ci], W)
                wv, wvk = next_w()
                bv = [pbank(grp), pbank(grp)]
                for ci in range(2):
                    proj_fm(wv, wvk, ci, 16, 0, hT, 'hT', bv[ci], W)
                for ci in range(2):
                    j = jp * 2 + ci
                    conv_chunk(l, j, bg[ci], W, cg, 0, t, 'sig')
                    conv_chunk(l, NJ + j, bv[ci], W, cvv, 1, t, 'lf')
                    act(tg[:, 0:W], cg[:, 0:W], AF.Square, ['sig'], ['ka'], scale=0.21145921592630217)
                    stt(tg[:, 0:W], tg[:, 0:W], 1.0, cg[:, 0:W], ALU.add, ALU.mult, ['ka', 'sig'], ['ka'])
                    act(sg[:, 0:W], tg[:, 0:W], AF.Sigmoid, ['ka'], ['gate'], scale=1.5957691216057308)
                    tt('dve', sg[:, 0:W], sg[:, 0:W], cg[:, 0:W], ALU.mult, ['gate', 'sig'], ['gate'])
                    mi = j - 2 * [p0 for (p0, p1) in MGRP if p0 <= jp < p1][0]
                    tt('dve', m_sb[:, mi, 0:W], sg[:, 0:W], cvv[:, 0:W], ALU.mult, ['gate', 'lf'], [('m', mi)])
                for (p0, p1) in MGRP:
                    if jp == p1 - 1:
                        down_half(2 * (p1 - p0), l, W)
            if t == nt - 1:
                S.dma('pool', o_pc[l], upre[l][:, :, :], reads=[('upre', l, uc) for uc in range(88)], semkey='out')
            if W > TT:
                S.dma('pool', o_sc[l], sco[l][:, :, :, :], reads=[('sco', l)], semkey='out')

        def layer(t, l, W):
            if KSTOP < 1:
                return
            rmsnorm(V_G1 + l * 16, hT, 'hT', W)
            if KSTOP < 2:
                return
            for h in range(4):
                lin_head('a', t, l, h, W)
                if KSTOP < 3:
                    return
            if KSTOP < 4:
                return
            swa(t, l, W)
            if KSTOP < 5:
                return
            for h in range(4):
                lin_head('r', t, l, h, W)
            if KSTOP < 6:
                return
            mixkeys = [('mix', i) for i in range(16)]
            for ocp in range(8):
                w, wk = next_w()
                for ci in range(2):
                    oc = ocp * 2 + ci
                    by = pbank('proj')
                    proj_fm(w, wk, ci, 16, 0, mix, (lambda k: ('mix', k)), by, W)
                    tt('dve', X['b'][:, oc, 0:W], X['b'][:, oc, 0:W], pb[by][:, 0:W], ALU.add, [X['k'], ('ps', by)], [X['k']])
            if KSTOP < 7:
                return
            ffn(t, l, W)

        plan_weights()
        for dst, src, key in [(vec, vecs, 'vec'), (sinkt, sinkb, 'sinkt'), (sink4, sinks4, 'sink4'), (cmk, cmask, 'masks'), (dmk, dmask, 'masks'),
                              (rwc, rowc, 'rwc'), (bia, biasd, 'bia'), (bis, biass, 'bis'), (sel, seld, 'sel'), (idb, identb, 'idb'),
                              (idf, identf, 'idf'), (ones, onesd, 'ones')]:
            S.dma('pool', dst[:], src, writes=[key], semkey='const')
        for l in range(2):
            S.dma('pool', scp[l][:, :, :, :], scT[l], writes=['scp'], semkey='const')
        S.settle('const', ['vec', 'sinkt', 'sink4', 'masks', 'rwc', 'bia', 'bis', 'sel', 'idb', 'idf', 'ones', 'scp'])
        tt('dve', vec[:, V_LB + 4:V_LB + 8], vec[:, V_RAW + 4:V_RAW + 8], vec[:, V_RAW:V_RAW + 4], ALU.subtract, ['vec'], ['vec'])
        act(vec[:, V_LB + 4:V_LB + 8], vec[:, V_LB + 4:V_LB + 8], AF.Sigmoid, ['vec'], ['vec'])
        ts(vec[:, V_OML:V_OML + 8], vec[:, V_LB:V_LB + 8], -1.0, 1.0, ALU.mult, ALU.add, ['vec'], ['vec'])
        ts(vec[:, V_LBM1:V_LBM1 + 8], vec[:, V_LB:V_LB + 8], -1.0, 0.0, ALU.add, ALU.add, ['vec'], ['vec'])
        for l in range(2):
            if l == 0:
                S.op('pool', lambda e: e.memset(vs[:, :], 0.0), writes=['vs'])
            S.op('pool', lambda e, l=l: e.memset(upre[l][:, :, :], 0.0), writes=[('upre', l, uc) for uc in range(88)])
            S.op('pool', lambda e, l=l: e.memset(kwin[l][:, :, :], 0.0), writes=[('kwin', l)])
            S.op('pool', lambda e, l=l: e.memset(vwin[l][:, :, :], 0.0), writes=[('vwin', l)])
            for k in 'ar':
                for h in range(4):
                    S.op('pool', lambda e, k=k, l=l, h=h: e.memset(Sf[(k, l, h)][:, :], 0.0), writes=[('Sf', k, l, h)])

        def load_x(t):
            xb = xbufs[t % 2]
            xk = "x%d" % (t % 2)
            S.dma('sp', xb[:, :, 0:TT], xT[:, :, t * TT:(t + 1) * TT].rearrange("c p n -> p c n"), writes=[xk], semkey=('xload', t % 2))
            if t == 0:
                S.dma('sp', xb[:, :, TT:TT + NS], xT[:, :, T:T + NS].rearrange("c p n -> p c n"), writes=[xk], semkey=('xload', t % 2))

        load_x(0)
        for t in range(nt):
            W = TT + NS if t == 0 else TT
            X['b'] = xbufs[t % 2]
            X['k'] = "x%d" % (t % 2)
            if t + 1 < nt:
                load_x(t + 1)
            for l in range(2):
                layer(t, l, W)
            rmsnorm(V_GF, X['b'], X['k'], W)
            S.dma('pool', yT[:, :, t * TT:(t + 1) * TT].rearrange("c p n -> p c n"), X['b'][:, :, 0:TT], reads=[X['k']], semkey=('yout', t % 2))
            if t == 0:
                S.dma('pool', yT[:, :, T:T + NS].rearrange("c p n -> p c n"), X['b'][:, :, TT:W], reads=[X['k']], semkey=('yout', t % 2))
        for l in range(2):
            for h in range(4):
                S.dma('pool', o_hs[l, h], Sf[('a', l, h)][:, :], reads=[('Sf', 'a', l, h)], semkey='out')
                S.dma('pool', o_rs[l, h], Sf[('r', l, h)][:, :], reads=[('Sf', 'r', l, h)], semkey='out')
        assert KSTOP < 99 or wstate['used'] == len(wq), (wstate, len(wq))
        S.finish('sp')
        print("ops", S.nops, {k: v for k, v in S.cnt.items()})
    return nc


_CACHE = {}


def _consts():
    import ml_dtypes
    bf = ml_dtypes.bfloat16
    c = {}
    s = np.arange(128)[:, None]
    tcol = np.arange(256)[None, :]
    tl = tcol % 128
    c['cmask'] = (((s // 64) == (tl // 64)) & (s <= tl)).astype(np.float32).astype(bf)
    dm = np.zeros((128, 4, 256), np.float32)
    rowc = np.zeros((128, 9, 256), np.float32)
    rowc[:, 0, :] = 1.0
    rowc[:, 0, ::64] = 0.0
    for h in range(4):
        g = np.float64(GAMMAS[h])
        dm[:, h, :] = np.where(tl >= s, g ** np.maximum(tl - s, 0) * SCALE, 0.0)
        rowc[:, 1 + h, :] = g ** (tl + 1.0)
        rowc[:, 5 + h, :] = g ** (127.0 - tl) * SCALE
    c['dmask'] = dm.astype(bf)
    c['rowc'] = rowc
    qi = np.arange(128)[:, None]
    sj = np.arange(256)[None, :]
    dist = 128 + qi - sj
    valid = (dist >= 0) & (dist < 128)
    bd = np.zeros((128, 16, 256), np.float32)
    for h in range(8):
        bd[:, h, :] = np.where(valid, -SLOPES[h] * dist, -1e30)
        bd[:, 8 + h, :] = np.where(valid & (sj >= 128), -SLOPES[h] * dist, -1e30)
    c['biasd'] = bd.astype(bf)
    bs = np.zeros((4, 2, 128), np.float32)
    for g in range(4):
        for kv in range(2):
            bs[g, kv, :] = -SLOPES[kv * 4 + g] * (127.0 - np.arange(128))
    c['biass'] = bs
    sd = np.zeros((128, 4, 128), np.float32)
    for b in range(4):
        sd[b, b, :] = 1.0
    c['seld'] = sd
    c['identb'] = np.eye(128, dtype=np.float32).astype(bf)
    c['identf'] = np.eye(128, dtype=np.float32)
    on = np.zeros((128, 2, 128), np.float32)
    on[:, 0, :] = 1.0 / 2048.0
    on[:, 1, :] = 1.0 / 128.0
    c['onesd'] = on
    return c


def kernel(x_prompt, x_sample, state_hgrn, cache_swa_k, cache_swa_v, state_ret, state_conv,
           norm1_g, w_in, hgrn_lb_raw, hgrn_norm_g, swa_sinks, ret_norm_g, w_out,
           norm2_g, w_up, conv_w, conv_b, w_down, final_norm_g):
    f = lambda a: np.ascontiguousarray(np.asarray(a, dtype=np.float32))
    x_prompt, x_sample = f(x_prompt), f(x_sample)
    state_hgrn, state_ret = f(state_hgrn), f(state_ret)
    cache_swa_k, cache_swa_v, state_conv = f(cache_swa_k), f(cache_swa_v), f(state_conv)
    w_in, w_out, w_up, w_down = f(w_in), f(w_out), f(w_up), f(w_down)
    if 'nc' not in _CACHE:
        _CACHE['nc'] = build_program()
        _CACHE['c'] = _consts()
    nc = _CACHE['nc']
    cst = _CACHE['c']

    raise_if = None
    in_maps = []
    n_cores = 8
    vecs = np.zeros((128, VW), np.float32)
    n1 = f(norm1_g).reshape(2, 16, 128); n2 = f(norm2_g).reshape(2, 16, 128); nf = f(final_norm_g).reshape(16, 128)
    for l in range(2):
        vecs[:, 0 + l * 16:0 + (l + 1) * 16] = n1[l].T
        vecs[:, 32 + l * 16:32 + (l + 1) * 16] = n2[l].T
    vecs[:, 64:80] = nf.T
    ga = f(hgrn_norm_g).reshape(2, 4, 128); gc = f(ret_norm_g).reshape(2, 4, 128)
    for l in range(2):
        vecs[:, 104 + l * 4:104 + (l + 1) * 4] = ga[l].T
        vecs[:, 112 + l * 4:112 + (l + 1) * 4] = gc[l].T
    cw = f(conv_w).reshape(2, 3, 88, 128); cb = f(conv_b).reshape(2, 88, 128)
    for l in range(2):
        for tap in range(3):
            vecs[:, V_CW + (l * 3 + tap) * 88:V_CW + (l * 3 + tap + 1) * 88] = cw[l, tap].T
        vecs[:, V_CB + l * 88:V_CB + (l + 1) * 88] = cb[l].T
    vecs[:, V_EPS] = EPS
    vecs[0:64, V_MLO] = 1.0
    vecs[64:128, V_MHI] = 1.0
    lbraw = f(hgrn_lb_raw).reshape(2, 4, 128)
    vecs[:, V_RAW:V_RAW + 4] = lbraw[0].T
    vecs[:, V_RAW + 4:V_RAW + 8] = lbraw[1].T
    sk = f(swa_sinks)
    sinkb = np.broadcast_to(sk.reshape(1, 16), (128, 16)).copy()
    sinks4 = np.ascontiguousarray(sk.reshape(2, 2, 4).transpose(2, 0, 1))
    for c in range(n_cores):
        b = c % 4
        xT = np.empty((16, 128, T + NS), np.float32)
        xT[:, :, :T] = x_prompt[b].T.reshape(16, 128, T)
        xT[:, :, T:] = x_sample[c * 4:(c + 1) * 4, 0, :].T.reshape(16, 128, NS)
        m = {
            'xT': xT,
            'st_h': np.ascontiguousarray(state_hgrn[:, c * 4:(c + 1) * 4]),
            'st_r': np.ascontiguousarray(state_ret[:, c * 4:(c + 1) * 4]),
            'ck': np.ascontiguousarray(cache_swa_k[:, c * 4:(c + 1) * 4].reshape(2, 4, 128, 256)),
            'cv': np.ascontiguousarray(cache_swa_v[:, c * 4:(c + 1) * 4].reshape(2, 4, 128, 256)),
            'scT': np.ascontiguousarray(state_conv[:, c * 4:(c + 1) * 4].reshape(2, 4, 2, 88, 128).transpose(0, 4, 3, 1, 2)),
            'w_in': w_in, 'w_out': w_out, 'w_up': w_up, 'w_down': w_down,
            'vecs': vecs, 'sinkb': sinkb, 'sinks4': sinks4,
        }
        m.update(cst)
        in_maps.append(m)
    if _CACHE.get('maps_only'):
        return in_maps
    res = run_bass_kernel_spmd(nc, in_maps, core_ids=list(range(n_cores)))
    R = res.results
    y_prompt = np.stack([R[b]['yT'][:, :, :T].reshape(D, T).T for b in range(4)])
    y_sample = np.concatenate([R[c]['yT'][:, :, T:].reshape(D, NS).T for c in range(8)])[:, None, :]
    p_hgrn = np.stack([R[b]['o_hs'] for b in range(4)], axis=1)
    p_ret = np.stack([R[b]['o_rs'] for b in range(4)], axis=1)
    p_k = np.stack([R[b]['o_pk'].reshape(2, 128, 2, 128) for b in range(4)], axis=1)
    p_v = np.stack([R[b]['o_pv'].reshape(2, 128, 2, 128) for b in range(4)], axis=1)
    p_conv = np.stack([R[b]['o_pc'].transpose(0, 3, 2, 1).reshape(2, 2, 11264) for b in range(4)], axis=1)
    s_hgrn = np.concatenate([R[c]['o_shs'] for c in range(8)], axis=1)
    s_ret = np.concatenate([R[c]['o_srs'] for c in range(8)], axis=1)
    s_k = np.concatenate([R[c]['o_sk'].reshape(2, 4, 128, 2, 128) for c in range(8)], axis=1)
    s_v = np.concatenate([R[c]['o_sv'].reshape(2, 4, 128, 2, 128) for c in range(8)], axis=1)
    s_conv = np.concatenate([R[c]['o_sc'].transpose(0, 3, 4, 2, 1).reshape(2, 4, 2, 11264) for c in range(8)], axis=1)
    out = (y_prompt, y_sample, p_hgrn, p_k, p_v, p_ret, p_conv, s_hgrn, s_k, s_v, s_ret, s_conv)
    return tuple(np.ascontiguousarray(o, dtype=np.float32) for o in out)
```
